# Optimizing a Trainium2 kernel written in Bass

```python
import math
import jax, jax.numpy as jnp
from jax import lax
import numpy as np

D_MODEL = 1024
BATCH = 1
SEQ = 16384
DEPTH = 2

N_MIX_HEADS = 16
HEAD_DIM = D_MODEL // N_MIX_HEADS
N_HEADS_DIL = 6
DILATED_PAIRS = ((128, 1), (512, 4), (2048, 16))
N_HEADS_DIFF = 4
DIFF_QK_DIM = HEAD_DIM // 2
N_HEADS_SB = 6
D_MIX = (N_HEADS_DIL + N_HEADS_DIFF + N_HEADS_SB) * HEAD_DIM
IN_SIZES = (N_HEADS_DIL * HEAD_DIM,) * 3 + (N_HEADS_DIFF * 2 * DIFF_QK_DIM,) * 2 + (N_HEADS_DIFF * HEAD_DIM,) + (N_HEADS_SB * HEAD_DIM,) * 3
D_IN = sum(IN_SIZES)
D_FF = 2816
Q_BLOCK = 128
NORM_EPS = 1e-6
N_NORMS = 6

kernel_name = "hybrid_dilated_diff_stickbreaking_macaron"


def rmsnorm(x, g):
    xf = x.astype(jnp.float32)
    y = xf * lax.rsqrt(jnp.mean(xf * xf, axis=-1, keepdims=True) + NORM_EPS)
    return (y * g.astype(jnp.float32)).astype(x.dtype)


def swiglu(h, w_gate, w_up, w_down):
    return (jax.nn.silu(h @ w_gate) * (h @ w_up)) @ w_down


def alibi_slopes(n):
    return 2.0 ** (-8.0 * jnp.arange(1, n + 1, dtype=jnp.float32) / n)


def to_heads(t, n_heads):
    b, s, _ = t.shape
    return t.reshape(b, s, n_heads, -1).transpose(0, 2, 1, 3)


def from_heads(t):
    b, h, s, dh = t.shape
    return t.transpose(0, 2, 1, 3).reshape(b, s, h * dh)


def query_blocks(t):
    b, h, s, dh = t.shape
    return t.reshape(b, h, s // Q_BLOCK, Q_BLOCK, dh).transpose(2, 0, 1, 3, 4)


def merge_blocks(t):
    nq, b, h, q, dh = t.shape
    return t.transpose(1, 2, 0, 3, 4).reshape(b, h, nq * q, dh)


def dilated_branch(q, k, v, slopes, window, dilation):
    b, h, s, dh = q.shape
    n_back = window // dilation
    chunk = dilation * n_back
    sp = -(-s // chunk) * chunk
    seg = sp // dilation
    nb = seg // n_back

    def split(t):
        t = jnp.pad(t, ((0, 0), (0, 0), (0, sp - s), (0, 0)))
        t = t.reshape(b, h, seg, dilation, dh).transpose(0, 1, 3, 2, 4)
        return t.reshape(b, h, dilation, nb, n_back, dh)

    def with_prev(t):
        prev = jnp.concatenate([jnp.zeros_like(t[:, :, :, :1]), t[:, :, :, :-1]], axis=3)
        return jnp.concatenate([prev, t], axis=4)

    qb = split(q)
    kk = with_prev(split(k))
    vv = with_prev(split(v))
    sc = jnp.einsum('bhrnqd,bhrnkd->bhrnqk', qb, kk, preferred_element_type=jnp.float32)
    qi = jnp.arange(n_back)[:, None]
    ki = jnp.arange(2 * n_back)[None, :]
    dist = n_back + qi - ki
    band = (dist >= 0) & (dist <= n_back)
    first = (jnp.arange(nb) == 0)[:, None, None] & (ki < n_back)[None]
    valid = band[None] & ~first
    sc = sc - slopes[None, :, None, None, None, None] * (dilation * dist).astype(jnp.float32)
    sc = jnp.where(valid, sc, -jnp.inf)
    m = jnp.max(sc, axis=-1, keepdims=True)
    p = jnp.exp(sc - m)
    den = jnp.sum(p, axis=-1)
    o = jnp.einsum('bhrnqk,bhrnkd->bhrnqd', p, vv, preferred_element_type=jnp.float32) / den[..., None]
    lse = m[..., 0] + jnp.log(den)
    o = o.reshape(b, h, dilation, seg, dh).transpose(0, 1, 3, 2, 4).reshape(b, h, sp, dh)[:, :, :s]
    lse = lse.reshape(b, h, dilation, seg).transpose(0, 1, 3, 2).reshape(b, h, sp)[:, :, :s]
    return o, lse


def dilated_attention(q, k, v):
    slopes = alibi_slopes(q.shape[1])
    outs, lses = zip(*[dilated_branch(q, k, v, slopes, w, d) for (w, d) in DILATED_PAIRS])
    wts = jax.nn.softmax(jnp.stack(lses), axis=0)
    return jnp.sum(wts[..., None] * jnp.stack(outs), axis=0)


def differential_attention(q1, q2, k1, k2, v, lam):
    s = q1.shape[2]
    slopes = alibi_slopes(q1.shape[1])
    kpos = jnp.arange(s)
    starts = jnp.arange(s // Q_BLOCK) * Q_BLOCK

    def block(args):
        start, a, c = args
        rel = (start + jnp.arange(Q_BLOCK))[:, None] - kpos[None, :]
        bias = jnp.where(rel >= 0, -slopes[:, None, None] * rel.astype(jnp.float32), -jnp.inf)
        a1 = jax.nn.softmax(jnp.einsum('bhqd,bhkd->bhqk', a, k1, preferred_element_type=jnp.float32) + bias, axis=-1)
        a2 = jax.nn.softmax(jnp.einsum('bhqd,bhkd->bhqk', c, k2, preferred_element_type=jnp.float32) + bias, axis=-1)
        return jnp.einsum('bhqk,bhkd->bhqd', a1 - lam * a2, v, preferred_element_type=jnp.float32)

    return merge_blocks(lax.map(block, (starts, query_blocks(q1), query_blocks(q2))))


def stick_breaking_attention(q, k, v):
    s = q.shape[2]
    kpos = jnp.arange(s)
    starts = jnp.arange(s // Q_BLOCK) * Q_BLOCK

    def block(args):
        start, a = args
        valid = kpos[None, :] < (start + jnp.arange(Q_BLOCK))[:, None]
        z = jnp.einsum('bhqd,bhkd->bhqk', a, k, preferred_element_type=jnp.float32)
        log_fail = jnp.where(valid, jax.nn.log_sigmoid(-z), 0.0)
        after = lax.cumsum(log_fail, axis=3, reverse=True) - log_fail
        wts = jnp.where(valid, jnp.exp(jax.nn.log_sigmoid(z) + after), 0.0)
        return jnp.einsum('bhqk,bhkd->bhqd', wts, v, preferred_element_type=jnp.float32)

    return merge_blocks(lax.map(block, (starts, query_blocks(q))))


def token_mixer(h, w_in, w_out, lam_params, subln_gain, layer):
    proj = h @ w_in
    idx = [int(i) for i in np.cumsum(IN_SIZES)[:-1]]
    qa, ka, va, qb, kb, vb, qc, kc, vc = jnp.split(proj, idx, axis=-1)
    b, s, _ = h.shape
    o_a = dilated_attention(to_heads(qa, N_HEADS_DIL) * (HEAD_DIM ** -0.5), to_heads(ka, N_HEADS_DIL), to_heads(va, N_HEADS_DIL))
    qb = qb.reshape(b, s, N_HEADS_DIFF, 2, DIFF_QK_DIM).transpose(0, 2, 3, 1, 4) * (DIFF_QK_DIM ** -0.5)
    kb = kb.reshape(b, s, N_HEADS_DIFF, 2, DIFF_QK_DIM).transpose(0, 2, 3, 1, 4)
    lam_init = 0.8 - 0.6 * math.exp(-0.3 * layer)
    lam = jnp.exp(jnp.sum(lam_params[0] * lam_params[1])) - jnp.exp(jnp.sum(lam_params[2] * lam_params[3])) + lam_init
    o_b = differential_attention(qb[:, :, 0], qb[:, :, 1], kb[:, :, 0], kb[:, :, 1], to_heads(vb, N_HEADS_DIFF), lam.astype(jnp.float32))
    o_b = rmsnorm(o_b, subln_gain) * (1.0 - lam_init)
    o_c = stick_breaking_attention(to_heads(qc, N_HEADS_SB) * (HEAD_DIM ** -0.5), to_heads(kc, N_HEADS_SB), to_heads(vc, N_HEADS_SB))
    mixed = jnp.concatenate([from_heads(o_a), from_heads(o_b), from_heads(o_c)], axis=-1).astype(h.dtype)
    return mixed @ w_out


def setup_inputs(seed: int = 0) -> dict:
    key = jax.random.key(seed)
    ks = jax.random.split(key, 10)
    f32 = jnp.float32
    x = jax.random.normal(ks[0], (BATCH, SEQ, D_MODEL), f32)
    norm_gains = 1.0 + 0.02 * jax.random.normal(ks[1], (DEPTH, N_NORMS, D_MODEL), f32)
    w_ffn_gate = jax.random.normal(ks[2], (DEPTH, 2, D_MODEL, D_FF), f32) * D_MODEL ** -0.5
    w_ffn_up = jax.random.normal(ks[3], (DEPTH, 2, D_MODEL, D_FF), f32) * D_MODEL ** -0.5
    w_ffn_down = jax.random.normal(ks[4], (DEPTH, 2, D_FF, D_MODEL), f32) * D_FF ** -0.5
    w_in = jax.random.normal(ks[5], (DEPTH, D_MODEL, D_IN), f32) * D_MODEL ** -0.5
    w_out = jax.random.normal(ks[6], (DEPTH, D_MIX, D_MODEL), f32) * D_MIX ** -0.5
    diff_lambda = 0.1 * jax.random.normal(ks[7], (DEPTH, 4, DIFF_QK_DIM), f32)
    diff_subln_gain = 1.0 + 0.02 * jax.random.normal(ks[8], (DEPTH, HEAD_DIM), f32)
    return {"x": x, "norm_gains": norm_gains, "w_ffn_gate": w_ffn_gate, "w_ffn_up": w_ffn_up,
            "w_ffn_down": w_ffn_down, "w_in": w_in, "w_out": w_out,
            "diff_lambda": diff_lambda, "diff_subln_gain": diff_subln_gain}


def reference(x, norm_gains, w_ffn_gate, w_ffn_up, w_ffn_down, w_in, w_out, diff_lambda, diff_subln_gain):
    for layer in range(DEPTH):
        g = norm_gains[layer]
        h = rmsnorm(x, g[0])
        x = x + 0.5 * rmsnorm(swiglu(h, w_ffn_gate[layer, 0], w_ffn_up[layer, 0], w_ffn_down[layer, 0]), g[1])
        h = rmsnorm(x, g[2])
        x = x + rmsnorm(token_mixer(h, w_in[layer], w_out[layer], diff_lambda[layer], diff_subln_gain[layer], layer), g[3])
        h = rmsnorm(x, g[4])
        x = x + 0.5 * rmsnorm(swiglu(h, w_ffn_gate[layer, 1], w_ffn_up[layer, 1], w_ffn_down[layer, 1]), g[5])
    return x
```

```python
import math
import contextlib
import numpy as np
import ml_dtypes
import concourse.bass as bass
import concourse.mybir as mybir
from concourse.bass_utils import run_bass_kernel_spmd

F32 = mybir.dt.float32
BF16 = mybir.dt.bfloat16
AF = mybir.ActivationFunctionType
ALU = mybir.AluOpType
NPBF = ml_dtypes.bfloat16

NCORES = 8
D = 1024
DFF = 2816
NF = DFF // 128
S = 16384
TOK = S // NCORES
NB = TOK // 128
NBLK = S // 128
DEPTH = 2
EPS = 1e-6
NEG = -30000.0
THRESH = 120.0
HEADS = ([("dil", h) for h in range(6)] + [("diff", h) for h in range(4)] + [("sb", h) for h in range(6)])
QCOL = {"dil": 0, "diff": 1152, "sb": 1920}
KCOL = {"dil": 384, "diff": 1408, "sb": 2304}
VCOL = {"dil": 768, "diff": 1664, "sb": 2688}
KROWS = {"dil": 96, "diff": 112, "sb": 64}


def alibi(n):
    return [2.0 ** (-8.0 * (i + 1) / n) for i in range(n)]


SLOPE = {("dil", h): alibi(6)[h] for h in range(6)}
SLOPE.update({("diff", h): alibi(4)[h] for h in range(4)})
SLOPE_LIST = [("dil", h) for h in range(6)] + [("diff", h) for h in range(4)]


class Buf:
    __slots__ = ("name", "writer", "readers", "sem", "semcnt")

    def __init__(self, name):
        self.name = name
        self.writer = None
        self.readers = {}
        self.sem = None
        self.semcnt = 0


ENGS = ("pe", "act", "dve", "pool", "sp")


class Prog:
    def __init__(self, nc, es):
        self.nc = nc
        self.es = es
        self.q = {e: [] for e in ENGS}
        self.esem = {e: es.enter_context(nc.semaphore("S_" + e)) for e in ("pe", "act", "dve", "pool")}
        self.ecnt = {e: 0 for e in self.esem}
        self.waited = {e: {} for e in ENGS}
        self.pending = {e: [] for e in ENGS}
        self.finals = []
        self.nsem = 4
        self.dma_tks = {}

    def _wait(self, e, tk):
        if tk is None:
            return
        sem, val, src = tk
        if sem is None:
            assert src == e, "cross-engine wait on unsignaled op"
            return
        if src == "pe" and e == "pe":
            return
        key = id(sem)
        if self.waited[e].get(key, 0) >= val:
            return
        self.waited[e][key] = val
        self.q[e].append(("w", sem, val))

    def _deps(self, e, reads, writes):
        for b in reads:
            self._wait(e, b.writer)
        for b in writes:
            self._wait(e, b.writer)
            for r in b.readers.values():
                self._wait(e, r)

    def _commit(self, tk, reads, writes):
        for b in reads:
            if b not in writes:
                b.readers[id(tk[0])] = tk
        for b in writes:
            b.writer = tk
            b.readers = {}

    def op(self, e, fn, reads=(), writes=(), sig=True):
        self._deps(e, reads, writes)
        if sig:
            self.ecnt[e] += 1
            tk = (self.esem[e], self.ecnt[e], e)
            self.q[e].append(("o", fn, self.esem[e], 1))
            for (rs, ws) in self.pending[e]:
                self._commit(tk, rs, ws)
            self.pending[e] = []
            self._commit(tk, reads, writes)
        else:
            assert e == "pe"
            self.q[e].append(("o", fn, None, 0))
            self.pending[e].append((tuple(reads), tuple(writes)))
            for b in writes:
                b.writer = (None, 0, e)
                b.readers = {}

    def dma(self, e, out_ap, in_ap, reads=(), writes=(), sembuf=None, final=False):
        self._deps(e, reads, writes)
        sb = sembuf
        if sb.sem is None:
            sb.sem = self.es.enter_context(self.nc.semaphore("D%d_%s" % (self.nsem, sb.name)))
            self.nsem += 1
        sb.semcnt += 16
        tk = (sb.sem, sb.semcnt, "dma")
        self.q[e].append(("d", out_ap, in_ap, sb.sem))
        self.dma_tks[id(sb.sem)] = tk
        self._commit(tk, reads, writes)
        if final:
            self.finals.append(tk)
        return tk

    def barrier(self):
        for b_e in ENGS:
            assert not self.pending[b_e], "dangling unsignaled ops on " + b_e
        tks = [(self.esem[e], self.ecnt[e], e) for e in self.esem if self.ecnt[e] > 0]
        tks += list(self.dma_tks.values())
        for e in ENGS:
            for tk in tks:
                self._wait(e, tk)

    def run(self):
        nc = self.nc
        for b_e in ENGS:
            assert not self.pending[b_e], "dangling unsignaled ops on " + b_e
        for tk in self.finals:
            self._wait("sp", tk)

        def replay(eng, items):
            for it in items:
                if it[0] == "w":
                    eng.wait_ge(it[1], it[2])
                elif it[0] == "o":
                    ins = it[1](eng)
                    if it[2] is not None:
                        ins.then_inc(it[2], it[3])
                else:
                    eng.dma_start(out=it[1], in_=it[2]).then_inc(it[3], 16)

        with nc.Block() as block:
            @block.tensor
            def _(eng):
                replay(eng, self.q["pe"])

            @block.scalar
            def _(eng):
                replay(eng, self.q["act"])

            @block.vector
            def _(eng):
                replay(eng, self.q["dve"])

            @block.gpsimd
            def _(eng):
                replay(eng, self.q["pool"])

            @block.sync
            def _(eng):
                replay(eng, self.q["sp"])


def pipeline(n, stages):
    maxlag = max(l for l, _ in stages)
    for t in range(n + maxlag):
        for lag, fn in stages:
            u = t - lag
            if 0 <= u < n:
                fn(u)


def _bf(x):
    return np.asarray(x, dtype=np.float32).astype(NPBF)


def core_constants(c):
    p = np.arange(128)[:, None].astype(np.int64)
    col = np.arange(128)[None, :].astype(np.int64)
    out = {}
    msb = np.zeros((128, 8, 128), np.float32)
    for a in range(8):
        r = 7 - a
        if r > c:
            msb[:, a, :] = NEG
        elif r == c:
            msb[:, a, :] = np.where(col > p, 0.0, NEG)
    out["c_msb"] = _bf(msb.reshape(128, 1024))
    mc = np.zeros((128, 8, 128), np.float32)
    for r in range(8):
        if r > c:
            mc[:, r, :] = NEG
        elif r == c:
            mc[:, r, :] = np.where(p >= col, 0.0, NEG)
    out["c_mc"] = _bf(mc.reshape(128, 1024))
    td = np.full((128, 24, 128), NEG, np.float32)
    for idx in range(24):
        Dd = idx - 7 + c
        if 0 <= Dd <= 16:
            delta = 128 * Dd - col + p
            mult = ((delta >= 0) & (delta <= 128)).astype(np.int64)
            mult = mult + ((delta >= 0) & (delta <= 512) & (delta % 4 == 0))
            mult = mult + ((delta >= 0) & (delta <= 2048) & (delta % 16 == 0))
            td[:, idx, :] = np.where(mult > 0, np.log(np.maximum(mult, 1)), NEG)
    out["c_td"] = _bf(td.reshape(128, 24 * 128))
    ds = np.arange(159)[None, :]
    out["c_base"] = (128.0 * (ds - 31 + c) + p).astype(np.float32)
    return out


def shared_constants():
    out = {}
    out["c_identb"] = _bf(np.eye(128))
    out["c_identf"] = np.eye(128, dtype=np.float32)
    e_dil = np.zeros((8, 128), np.float32)
    for r in range(4):
        e_dil[r, 64 + r] = 1.0
    e_diff = np.zeros((8, 128), np.float32)
    for r in range(4):
        e_diff[r, 32 + r] = 1.0
        e_diff[4 + r, 96 + r] = 1.0
    out["c_edil"] = _bf(e_dil)
    out["c_ediff"] = _bf(e_diff)
    colq = np.arange(512)
    jq = (colq % 128).astype(np.float32)
    ii = (colq // 128).astype(np.float32)
    qa = np.stack([jq, jq, -1024.0 * ii, -1024.0 * ii] * 2, 0)
    out["c_qaug"] = _bf(qa)
    ks = np.zeros((8, 10, 128), np.float32)
    for si, key in enumerate(SLOPE_LIST):
        s = np.float32(SLOPE[key])
        hi = np.float32(s.astype(NPBF))
        lo = np.float32(np.float32(s - hi).astype(NPBF))
        sel = e_dil if key[0] == "dil" else e_diff
        for r, v in enumerate([hi, lo, hi, lo] * 2):
            ks[r, si, :] = sel[r] * v
    out["c_ksel"] = _bf(ks.reshape(8, 1280))
    out["c_ones8"] = _bf(np.ones((8, 512)))
    scl = np.ones((128, 2), np.float32)
    scl[0:64, 0] = 64.0 ** -0.5
    scl[0:32, 1] = 32.0 ** -0.5
    scl[64:96, 1] = 32.0 ** -0.5
    out["c_scl"] = scl
    return out


CONST_SPECS = {
    "c_msb": ([128, 1024], BF16), "c_mc": ([128, 1024], BF16), "c_td": ([128, 3072], BF16),
    "c_base": ([128, 159], F32), "c_identb": ([128, 128], BF16), "c_identf": ([128, 128], F32),
    "c_edil": ([8, 128], BF16), "c_ediff": ([8, 128], BF16), "c_qaug": ([8, 512], BF16),
    "c_ksel": ([8, 1280], BF16), "c_ones8": ([8, 512], BF16), "c_scl": ([128, 2], F32),
}


def build_program(stages, debug_mix=False, head_sel=None, attn_only=False):
    nc = bass.Bass("TRN2", target_bir_lowering=False)
    es = contextlib.ExitStack()
    P = Prog(nc, es)
    has_b = [l for (k, l) in stages if k == "B"]
    has_a = [l for (k, l) in stages if k == "A"]
    last_is_a = stages[-1][0] == "A"

    def din(name, shape, dt=F32):
        return nc.dram_tensor(name, shape, dt, kind="ExternalInput").ap()

    def dout(name, shape, dt=F32):
        return nc.dram_tensor(name, shape, dt, kind="ExternalOutput").ap()

    x_in = din("x_in", [TOK, D])
    x_out = dout("x_out", [TOK, D])
    consts = {k: din(k, shp, dt) for k, (shp, dt) in CONST_SPECS.items()}
    WD = {}
    for l in has_a:
        WD[("gate", l, 0)] = din(f"gate{l}_0", [D, DFF])
        WD[("up", l, 0)] = din(f"up{l}_0", [D, DFF])
        WD[("down", l, 0)] = din(f"down{l}_0", [DFF, D])
        WD[("win", l)] = din(f"win{l}", [D, 3072])
    for l in has_b:
        WD[("gate", l, 1)] = din(f"gate{l}_1", [D, DFF])
        WD[("up", l, 1)] = din(f"up{l}_1", [D, DFF])
        WD[("down", l, 1)] = din(f"down{l}_1", [DFF, D])
        WD[("wout", l)] = din(f"wout{l}", [D, D])
        WD[("dlam", l)] = din(f"dlam{l}", [1, 128])
        WD[("dgain", l)] = din(f"dgain{l}", [1, 64])
    gains = {l: din(f"gains{l}", [6, D]) for l in sorted(set(has_a + has_b))}
    if has_b:
        kt_all = din("kt_all", [16, 128, S], BF16)
        v_all = din("v_all", [16, 128, NBLK * 65], BF16)
        qt_in = din("qt_in", [16, 128, TOK], BF16)
    if last_is_a:
        qt_out = dout("qt_out", [16, 128, TOK], BF16)
        kt_out = dout("kt_out", [16, 128, TOK], BF16)
        v_out = dout("v_out", [16, 128, NB * 65], BF16)

    def sb(name, shape, dt):
        return es.enter_context(nc.sbuf_tensor(name, shape, dt))

    def ps(name, shape, dt):
        return es.enter_context(nc.psum_tensor(name, shape, dt))

    X = sb("X", [128, NB * D], F32)
    xb = [Buf(f"x{m}") for m in range(NB)]
    ARB = sb("ARB", [128, 59392], BF16)
    ARF = sb("ARF", [128, 4112], F32)
    identb = sb("identb", [128, 128], BF16)
    identf = sb("identf", [128, 128], F32)
    edil = sb("edil", [8, 128], BF16)
    ediff = sb("ediff", [8, 128], BF16)
    qaug = sb("qaug", [8, 512], BF16)
    ksel = sb("ksel", [8, 1280], BF16)
    ones8 = sb("ones8", [8, 512], BF16)
    scl = sb("scl", [128, 2], F32)
    base = sb("base", [128, 159], F32)
    epsT = sb("epsT", [128, 1], F32)
    small = sb("small", [128, 64], F32)
    cbuf = Buf("consts")
    PF = [ps(f"pf{i}", [128, 512], F32) for i in range(6)]
    PFb = [Buf(f"pf{i}") for i in range(6)]
    PB = [ps(f"pb{i}", [128, 1024], BF16) for i in range(2)]
    PBb = [Buf(f"pb{i}") for i in range(2)]

    def vb(off, n):
        return ARB[:, off:off + n]

    def vf(off, n):
        return ARF[:, off:off + n]

    xsem = Buf("xload")
    xv = x_in.rearrange("(m p) d -> p m d", p=128)
    for m in range(NB):
        P.dma("sp", X[:, m * D:(m + 1) * D], xv[:, m, :], writes=[xb[m]], sembuf=xsem)
    tkx = (xsem.sem, xsem.semcnt, "dma")
    for m in range(NB):
        xb[m].writer = tkx
    for (tile, key) in ((identb, "c_identb"), (identf, "c_identf"), (edil, "c_edil"), (ediff, "c_ediff"),
                        (qaug, "c_qaug"), (ksel, "c_ksel"), (ones8, "c_ones8"), (scl, "c_scl"), (base, "c_base")):
        P.dma("sp", tile[:], consts[key], writes=[], sembuf=cbuf)
    cbuf.writer = (cbuf.sem, cbuf.semcnt, "dma")
    P.op("pool", lambda e: e.memset(epsT[:], EPS), writes=[cbuf])

    GREP = [vf(0, 1024), vf(1024, 1024)]
    GREPb = [Buf("grep0"), Buf("grep1")]
    TT = vf(2048, 1024)
    TTb = Buf("tt")

    def load_gain(slot, layer, idx):
        P.dma("sp", GREP[slot], bcast_rows(gains[layer], idx, D), writes=[GREPb[slot]], sembuf=GREPb[slot])

    def bcast_rows(ap2d, row, n):
        r = ap2d[row:row + 1, 0:n]
        return bass.AP(tensor=r.tensor, offset=r.offset, ap=[[0, 128], [1, n]])

    sc_ctr = [0]

    def sc_col():
        i = sc_ctr[0] % 48
        sc_ctr[0] += 1
        return small[:, i:i + 1], SCb[i]

    SCb = [Buf(f"sc{i}") for i in range(64)]

    def prenorm_block(m, gslot, hn_ap, hnb):
        xm = X[:, m * D:(m + 1) * D]
        ssq, ssqb = sc_col()
        std, stdb = sc_col()
        rstd, rstdb = sc_col()
        P.op("act", lambda e: e.activation(out=hn_ap, in_=xm, func=AF.Square, accum_out=ssq),
             reads=[xb[m]], writes=[ssqb, hnb])
        P.op("act", lambda e: e.activation(out=std, in_=ssq, func=AF.Sqrt, bias=epsT[:, 0:1], scale=1.0 / D),
             reads=[ssqb, cbuf], writes=[stdb])
        P.op("dve", lambda e: e.reciprocal(out=rstd, in_=std), reads=[stdb], writes=[rstdb])
        P.op("dve", lambda e: e.scalar_tensor_tensor(out=hn_ap, in0=xm, scalar=rstd, in1=GREP[gslot],
                                                     op0=ALU.mult, op1=ALU.mult),
             reads=[xb[m], rstdb, GREPb[gslot]], writes=[hnb])

    def transpose_block(hn_ap, hnb, hT3, hTb, tokcol, pbi):
        pb = PB[pbi]
        for k in range(8):
            P.op("pe", lambda e, k=k: e.transpose(out=pb[:, k * 128:(k + 1) * 128], in_=hn_ap[:, k * 128:(k + 1) * 128],
                                                  identity=identb[:]),
                 reads=[hnb, cbuf], writes=[PBb[pbi]], sig=(k == 7))
        P.op("act", lambda e: e.activation(out=hT3[:, :, tokcol:tokcol + 128],
                                           in_=pb[:, 0:1024].rearrange("p (k t) -> p k t", k=8), func=AF.Copy),
             reads=[PBb[pbi]], writes=[hTb])

    def postnorm_residual(m, ybanks, gslot, coef):
        xm = X[:, m * D:(m + 1) * D]
        s0, s0b = sc_col()
        s1, s1b = sc_col()
        ssq, ssqb = sc_col()
        std, stdb = sc_col()
        rstd, rstdb = sc_col()
        for (sc, scb, bi) in ((s0, s0b, ybanks[0]), (s1, s1b, ybanks[1])):
            P.op("act", lambda e, sc=sc, bi=bi: e.activation(out=JUNK2[0], in_=PF[bi][:], func=AF.Square, accum_out=sc),
                 reads=[PFb[bi]], writes=[scb, JUNK2b[0]])
        P.op("dve", lambda e: e.tensor_tensor(out=ssq, in0=s0, in1=s1, op=ALU.add), reads=[s0b, s1b], writes=[ssqb])
        P.op("act", lambda e: e.activation(out=std, in_=ssq, func=AF.Sqrt, bias=epsT[:, 0:1], scale=1.0 / D),
             reads=[ssqb, cbuf], writes=[stdb])
        P.op("dve", lambda e: e.reciprocal(out=rstd, in_=std), reads=[stdb], writes=[rstdb])
        for hf in range(2):
            bi = ybanks[hf]
            P.op("dve", lambda e, bi=bi, hf=hf: e.scalar_tensor_tensor(
                out=TT[:, hf * 512:(hf + 1) * 512], in0=PF[bi][:], scalar=float(coef),
                in1=GREP[gslot][:, hf * 512:(hf + 1) * 512], op0=ALU.mult, op1=ALU.mult),
                reads=[PFb[bi], GREPb[gslot]], writes=[TTb])
        P.op("dve", lambda e: e.scalar_tensor_tensor(out=xm, in0=TT[:], scalar=rstd, in1=xm, op0=ALU.mult, op1=ALU.add),
             reads=[TTb, rstdb, xb[m]], writes=[xb[m]])

    def ffn(layer, which, g_pre, g_post):
        wg, wu, wdn = WD[("gate", layer, which)], WD[("up", layer, which)], WD[("down", layer, which)]
        ACT3 = vb(0, 22528).rearrange("p (f t) -> p f t", f=NF)
        WDN = vb(22528, 22528).rearrange("p (h f c) -> p h f c", h=2, f=NF)
        HT3 = vb(45056, 8192).rearrange("p (k t) -> p k t", k=8)
        WGU = vb(53248, 4096).rearrange("p (s w k n) -> p s w k n", s=2, w=2, k=8)
        HN = [vb(57344, 1024), vb(58368, 1024)]
        actb = [Buf(f"act{f}") for f in range(NF)]
        wdb = [[Buf(f"wd{h}_{i}") for i in range(2)] for h in range(2)]
        htb = Buf("ht")
        wgub = [[Buf(f"wgu{s}_{w}") for w in range(2)] for s in range(2)]
        hnb = [Buf("hn0"), Buf("hn1")]
        wgv = wg.rearrange("(k p) n -> p k n", p=128)
        wuv = wu.rearrange("(k p) n -> p k n", p=128)
        wdv = wdn.rearrange("(f p) c -> p f c", p=128)
        load_gain(0, layer, g_pre)
        load_gain(1, layer, g_post)

        def load_gu(f):
            s = f % 2
            P.dma("pool", WGU[:, s, 0], wgv[:, :, f * 128:(f + 1) * 128], writes=[wgub[s][0]], sembuf=wgub[s][0])
            P.dma("pool", WGU[:, s, 1], wuv[:, :, f * 128:(f + 1) * 128], writes=[wgub[s][1]], sembuf=wgub[s][1])

        def load_wd(h):
            for i in range(2):
                P.dma("pool", WDN[:, h, i * 11:(i + 1) * 11, :], wdv[:, i * 11:(i + 1) * 11, h * 512:(h + 1) * 512],
                      writes=[wdb[h][i]], sembuf=wdb[h][i])

        for tg in range(2):
            for j in range(8):
                m = tg * 8 + j
                prenorm_block(m, 0, HN[j % 2], hnb[j % 2])
                transpose_block(HN[j % 2], hnb[j % 2], HT3, htb, j * 128, j % 2)
            load_gu(0)
            load_gu(1)
            load_wd(0)
            load_wd(1)
            for f in range(NF):
                s = f % 2
                for hf in range(2):
                    gb, ub = (0, 1) if hf == 0 else (2, 3)
                    for (w, bank) in ((0, gb), (1, ub)):
                        for k in range(8):
                            P.op("pe", lambda e, w=w, bank=bank, k=k, s=s, hf=hf: e.matmul(
                                PF[bank][:], lhsT=WGU[:, s, w, k, :], rhs=HT3[:, k, hf * 512:(hf + 1) * 512],
                                start=(k == 0), stop=(k == 7)),
                                reads=[wgub[s][w], htb], writes=[PFb[bank]], sig=(k == 7))
                    SG = JUNK2[hf]
                    P.op("act", lambda e, gb=gb, SG=SG: e.activation(out=SG, in_=PF[gb][:], func=AF.Silu),
                         reads=[PFb[gb]], writes=[JUNK2b[hf]])
                    P.op("dve", lambda e, ub=ub, SG=SG, f=f, hf=hf: e.tensor_tensor(
                        out=ACT3[:, f, hf * 512:(hf + 1) * 512], in0=SG, in1=PF[ub][:], op=ALU.mult),
                        reads=[JUNK2b[hf], PFb[ub]], writes=[actb[f]])
                if f + 2 < NF:
                    load_gu(f + 2)
            for j in range(8):
                m = tg * 8 + j
                for h in range(2):
                    bank = 4 + h
                    for f in range(NF):
                        P.op("pe", lambda e, f=f, h=h, bank=bank, j=j: e.matmul(
                            PF[bank][:], lhsT=ACT3[:, f, j * 128:(j + 1) * 128], rhs=WDN[:, h, f, :],
                            start=(f == 0), stop=(f == NF - 1)),
                            reads=[actb[f], wdb[h][f // 11]], writes=[PFb[bank]], sig=(f == NF - 1))
                postnorm_residual(m, (4, 5), 1, 0.5)

    JUNK2 = [sb("SG0", [128, 512], BF16)[:], sb("SG1", [128, 512], BF16)[:]]
    JUNK2b = [Buf("sg0"), Buf("sg1")]

    def proj(layer):
        win = WD[("win", layer)]
        winv = win.rearrange("(k p) n -> p k n", p=128)
        HT3 = vb(0, 16384).rearrange("p (k t) -> p k t", k=8)
        WV = vb(16384, 8192).rearrange("p (k n) -> p k n", k=8)
        WST = vb(24576, 4096).rearrange("p (s k n) -> p s k n", s=4, k=8)
        HN = [vb(28672, 1024), vb(29696, 1024)]
        QST = [vb(30720, 512), vb(31232, 512)]
        VST = [vb(31744, 520).rearrange("p (h e) -> p h e", h=8), vb(32264, 520).rearrange("p (h e) -> p h e", h=8)]
        htb = Buf("pht")
        wvb = [Buf(f"wv{i}") for i in range(3)]
        wstb = [Buf(f"wst{i}") for i in range(4)]
        hnb = [Buf("phn0"), Buf("phn1")]
        qstb = [Buf("qst0"), Buf("qst1")]
        vstb = [Buf("vst0"), Buf("vst1")]
        load_gain(0, layer, 2)
        for s4 in range(4):
            P.op("pool", lambda e, s4=s4: e.memset(WST[:, s4], 0.0), writes=[wstb[s4]])
        for i in range(2):
            P.op("pool", lambda e, i=i: e.memset(VST[i], 1.0), writes=[vstb[i]])
        for i, (c0, n, d0) in enumerate(((768, 384, 0), (1664, 256, 384), (2688, 384, 640))):
            P.dma("pool", WV[:, :, d0:d0 + n], winv[:, :, c0:c0 + n], writes=[wvb[i]], sembuf=wvb[i])
        for m in range(NB):
            prenorm_block(m, 0, HN[m % 2], hnb[m % 2])
            transpose_block(HN[m % 2], hnb[m % 2], HT3, htb, m * 128, m % 2)
        cnt = [0, 0]
        qi = 0
        ring = 0
        for hh, (typ, h) in enumerate(HEADS):
            for which in ("q", "k"):
                colb = (QCOL if which == "q" else KCOL)[typ] + 64 * h
                if typ == "diff":
                    s4 = 2 + cnt[1] % 2
                    cnt[1] += 1
                    P.dma("pool", WST[:, s4, :, 0:32], winv[:, :, colb:colb + 32], writes=[wstb[s4]], sembuf=wstb[s4])
                    P.dma("pool", WST[:, s4, :, 64:96], winv[:, :, colb + 32:colb + 64], writes=[wstb[s4]], sembuf=wstb[s4])
                else:
                    s4 = cnt[0] % 2
                    cnt[0] += 1
                    P.dma("pool", WST[:, s4, :, 0:64], winv[:, :, colb:colb + 64], writes=[wstb[s4]], sembuf=wstb[s4])
                for g in range(4):
                    bank = ring % 4
                    ring += 1
                    aug = typ != "sb"
                    for k in range(8):
                        P.op("pe", lambda e, k=k, s4=s4, g=g, bank=bank, aug=aug: e.matmul(
                            PF[bank][:], lhsT=WST[:, s4, k, :], rhs=HT3[:, k, g * 512:(g + 1) * 512],
                            start=(k == 0), stop=(k == 7 and not aug)),
                            reads=[wstb[s4], htb], writes=[PFb[bank]], sig=(k == 7 and not aug))
                    if aug:
                        if which == "q":
                            sel = (edil if typ == "dil" else ediff)[:, :]
                            src = qaug[:, :]
                        else:
                            si = SLOPE_LIST.index((typ, h))
                            sel = ksel[:, si * 128:(si + 1) * 128]
                            src = ones8[:, :]
                        P.op("pe", lambda e, sel=sel, src=src, bank=bank: e.matmul(
                            PF[bank][:], lhsT=sel, rhs=src, start=False, stop=True),
                            reads=[cbuf], writes=[PFb[bank]])
                    st = qi % 2
                    qi += 1
                    if which == "q":
                        sc_i = 1 if typ == "diff" else 0
                        P.op("dve", lambda e, bank=bank, st=st, sc_i=sc_i: e.tensor_scalar(
                            out=QST[st], in0=PF[bank][:], scalar1=scl[:, sc_i:sc_i + 1], scalar2=None, op0=ALU.mult),
                            reads=[PFb[bank], cbuf], writes=[qstb[st]])
                        dst = qt_out[hh, :, g * 512:(g + 1) * 512]
                    else:
                        P.op("act", lambda e, bank=bank, st=st: e.activation(out=QST[st], in_=PF[bank][:], func=AF.Copy),
                             reads=[PFb[bank]], writes=[qstb[st]])
                        dst = kt_out[hh, :, g * 512:(g + 1) * 512]
                    P.dma("sp", dst, QST[st], reads=[qstb[st]], sembuf=qstb[st], final=True)
        for hf in range(2):
            for m in range(NB):
                bank = ring % 4
                ring += 1
                for k in range(8):
                    P.op("pe", lambda e, k=k, m=m, hf=hf, bank=bank: e.matmul(
                        PF[bank][:], lhsT=HT3[:, k, m * 128:(m + 1) * 128], rhs=WV[:, k, hf * 512:(hf + 1) * 512],
                        start=(k == 0), stop=(k == 7)),
                        reads=[htb, wvb[0], wvb[1], wvb[2]], writes=[PFb[bank]], sig=(k == 7))
                st = m % 2
                P.op("act", lambda e, bank=bank, st=st: e.activation(
                    out=VST[st][:, :, 0:64], in_=PF[bank][:].rearrange("p (h e) -> p h e", h=8), func=AF.Copy),
                    reads=[PFb[bank]], writes=[vstb[st]])
                dst = v_out[hf * 8:(hf + 1) * 8, :, m * 65:(m + 1) * 65].rearrange("h p e -> p h e")
                P.dma("sp", dst, VST[st], reads=[vstb[st]], sembuf=vstb[st], final=True)

    def attention_body(layer):
        lam_init = 0.8 - 0.6 * math.exp(-0.3 * layer)
        KT = vb(0, 16384)
        VV = vb(16384, 8320).rearrange("p (s e) -> p s e", e=65)
        QT = vb(24704, 2048)
        MIX3 = vb(26752, 16384).rearrange("p (k t) -> p k t", k=8)
        MSB = vb(43136, 1024)
        MC = vb(44160, 1024)
        TD = vb(45184, 3072)
        PT = [vb(48256 + i * 512, 512) for i in range(4)]
        WT = [vb(50304 + i * 512, 512) for i in range(4)]
        MT = vb(52352, 2048).rearrange("p (m f) -> p m f", m=NB)
        ktb = [Buf(f"kt{i}") for i in range(4)]
        vvb = Buf("vv")
        qtb = Buf("qt")
        mixb = [Buf(f"mix{k}") for k in range(8)]
        mskb = Buf("masks")
        ptb = [Buf(f"pt{i}") for i in range(4)]
        wtb = [Buf(f"wt{i}") for i in range(4)]
        mtb = Buf("mt")
        SS = [vf(i * 512, 512) for i in range(4)]
        CP = [vf(2048 + i * 516, 516) for i in range(4)]
        BT = vf(0, 159)
        OAf = vf(160, 512)
        TMP = vf(672, 64)
        OTMP = vf(736, 64)
        GSUB = vf(800, 64)
        LAMT = vf(864, 128)
        ssb = [Buf(f"ss{i}") for i in range(4)]
        cpb = [Buf(f"cp{i}") for i in range(4)]
        btb = Buf("bt")
        oab = Buf("oa")
        tmpb = Buf("tmp")
        otmpb = Buf("otmp")
        gsubb = Buf("gsub")
        lamb = Buf("lam")
        lamneg, lamnegb = small[:, 60:61], Buf("lamneg")

        P.op("pool", lambda e: e.memset(vb(52352, 2048), 0.0), writes=[mtb])
        P.dma("sp", MSB, consts["c_msb"], writes=[mskb], sembuf=mskb)
        P.dma("sp", MC, consts["c_mc"], writes=[mskb], sembuf=mskb)
        P.dma("sp", TD, consts["c_td"], writes=[mskb], sembuf=mskb)
        P.dma("sp", LAMT, bcast_rows(WD[("dlam", layer)], 0, 128), writes=[lamb], sembuf=lamb)
        P.dma("sp", GSUB, bcast_rows(WD[("dgain", layer)], 0, 64), writes=[gsubb], sembuf=gsubb)
        s1, s1b = sc_col()
        s2, s2b = sc_col()
        P.op("dve", lambda e: e.scalar_tensor_tensor(out=TMP[:, 0:32], in0=LAMT[:, 0:32], scalar=1.0, in1=LAMT[:, 32:64],
                                                     op0=ALU.mult, op1=ALU.mult, accum_out=s1),
             reads=[lamb], writes=[tmpb, s1b])
        P.op("dve", lambda e: e.scalar_tensor_tensor(out=TMP[:, 0:32], in0=LAMT[:, 64:96], scalar=1.0, in1=LAMT[:, 96:128],
                                                     op0=ALU.mult, op1=ALU.mult, accum_out=s2),
             reads=[lamb, tmpb], writes=[tmpb, s2b])
        P.op("act", lambda e: e.activation(out=s1, in_=s1, func=AF.Exp), reads=[s1b], writes=[s1b])
        P.op("act", lambda e: e.activation(out=s2, in_=s2, func=AF.Exp), reads=[s2b], writes=[s2b])
        P.op("dve", lambda e: e.tensor_tensor(out=lamneg, in0=s2, in1=s1, op=ALU.subtract), reads=[s1b, s2b], writes=[lamnegb])
        P.op("dve", lambda e: e.tensor_scalar(out=lamneg, in0=lamneg, scalar1=-lam_init, scalar2=None, op0=ALU.add),
             reads=[lamnegb], writes=[lamnegb])
        P.op("dve", lambda e: e.tensor_scalar(out=GSUB, in0=GSUB, scalar1=float(1.0 - lam_init), scalar2=None, op0=ALU.mult),
             reads=[gsubb], writes=[gsubb])

        def load_head(hh, typ):
            R = KROWS[typ]
            for i in range(4):
                P.dma("sp", KT[0:R, i * 4096:(i + 1) * 4096], kt_all[hh, 0:R, i * 4096:(i + 1) * 4096],
                      writes=[ktb[i]], sembuf=ktb[i])
            P.dma("sp", VV.rearrange("p s e -> p (s e)"), v_all[hh], writes=[vvb], sembuf=vvb)
            P.dma("sp", QT[0:R, :], qt_in[hh, 0:R, :], writes=[qtb], sembuf=qtb)

        def kblk_buf(slot):
            return ktb[slot // 32]

        def softmax_head(hh, typ, h):
            slope = SLOPE[(typ, h)]
            half = hh % 2
            P.op("dve", lambda e: e.tensor_scalar(out=BT, in0=base[:, :], scalar1=float(-slope), scalar2=None, op0=ALU.mult),
                 reads=[cbuf], writes=[btb])
            maps = [(0, 68)] if typ == "dil" else [(0, 36), (64, 36)]
            nmap = len(maps)
            sbanks = [0, 1, 2] if typ == "dil" else [0, 1]
            for g in range(4):
                units = []
                for kb in range(0, 32 * g + 32):
                    e_ = kb - 32 * g
                    ilo = 0 if e_ < 0 else e_ // 8
                    ihi = -1
                    for i in range(ilo, 4):
                        dmin = 128 * (32 * g + 8 * i - kb) - 127
                        ok = slope * dmin <= THRESH
                        if typ == "dil":
                            dsv = 32 * g + 8 * i - kb
                            ok = ok and (-7 <= dsv <= 16)
                        if ok:
                            ihi = i
                    if typ == "dil":
                        while ilo <= ihi and not (-7 <= 32 * g + 8 * ilo - kb <= 16):
                            ilo += 1
                    if ihi >= ilo:
                        units.append((kb, ilo, ihi))
                n = len(units)
                seq = [(u, mi) for u in range(n) for mi in range(nmap)]
                ns = len(seq)
                for mi in range(nmap):
                    P.op("pe", lambda e, mi=mi: e.matmul(PF[4 + mi][0:65, :], lhsT=VV[:, 0, :], rhs=ZB[:, :], start=True, stop=False),
                         reads=[vvb, zerob], writes=[PFb[4 + mi]], sig=False)

                def qk(t, g=g, units=units, seq=seq):
                    u, mi = seq[t]
                    kb, ilo, ihi = units[u]
                    r0, nr = maps[mi]
                    bank = sbanks[t % len(sbanks)]
                    slot = 127 - kb
                    c0, c1 = ilo * 128, (ihi + 1) * 128
                    e_ = kb - 32 * g
                    need_mask = (typ == "dil") or (e_ >= 0)
                    P.op("pe", lambda e: e.matmul(PF[bank][:, c0:c1], lhsT=KT[r0:r0 + nr, slot * 128:(slot + 1) * 128],
                                                  rhs=QT[r0:r0 + nr, g * 512 + c0:g * 512 + c1], start=True, stop=not need_mask),
                         reads=[kblk_buf(slot), qtb], writes=[PFb[bank]], sig=not need_mask)
                    if need_mask:
                        if typ == "dil":
                            i0 = 32 * g + 8 * ilo - kb + 7
                            cnt = ihi - ilo + 1
                            t0 = TD[:, i0 * 128:(i0 + 1) * 128]
                            rhs = bass.AP(tensor=t0.tensor, offset=t0.offset, ap=[list(t0.ap[0]), [8 * 128, cnt], [1, 128]])
                            P.op("pe", lambda e: e.matmul(PF[bank][:, c0:c1], lhsT=identb[:], rhs=rhs, start=False, stop=True),
                                 reads=[mskb, cbuf], writes=[PFb[bank]])
                        else:
                            r = e_ % 8
                            P.op("pe", lambda e: e.matmul(PF[bank][:, c0:c0 + 128], lhsT=identb[:], rhs=MC[:, r * 128:(r + 1) * 128],
                                                          start=False, stop=True),
                                 reads=[mskb, cbuf], writes=[PFb[bank]])

                def ex(t, g=g, units=units, seq=seq):
                    u, mi = seq[t]
                    kb, ilo, ihi = units[u]
                    bank = sbanks[t % len(sbanks)]
                    c0, c1 = ilo * 128, (ihi + 1) * 128
                    ds_ = 32 * g - kb + 31
                    P.op("act", lambda e: e.activation(out=PT[t % 3][:, c0:c1], in_=PF[bank][:, c0:c1], func=AF.Exp,
                                                       bias=BT[:, ds_:ds_ + 1], scale=1.0),
                         reads=[PFb[bank], btb], writes=[ptb[t % 3]])

                def pv(t, g=g, units=units, seq=seq, ns=ns):
                    u, mi = seq[t]
                    kb, ilo, ihi = units[u]
                    slot = 127 - kb
                    c0, c1 = ilo * 128, (ihi + 1) * 128
                    ob = 4 + mi
                    last = (u == len(units) - 1)
                    P.op("pe", lambda e: e.matmul(PF[ob][0:65, c0:c1], lhsT=VV[:, slot, :], rhs=PT[t % 3][:, c0:c1],
                                                  start=False, stop=last),
                         reads=[vvb, ptb[t % 3]], writes=[PFb[ob]], sig=True)

                pipeline(ns, [(0, qk), (0, ex), (1, pv)])
                tb = [3, 2]
                for mi in range(nmap):
                    ob = 4 + mi
                    P.op("dve", lambda e, ob=ob: e.tensor_copy(out=OAf[0:65, :], in_=PF[ob][0:65, :]),
                         reads=[PFb[ob]], writes=[oab])
                    for i in range(4):
                        P.op("pe", lambda e, i=i, mi=mi: e.transpose(out=PF[tb[mi]][:, i * 65:(i + 1) * 65],
                                                                   in_=OAf[0:65, i * 128:(i + 1) * 128],
                                                                   identity=identf[0:65, 0:65]),
                             reads=[oab, cbuf], writes=[PFb[tb[mi]]], sig=(i == 3))
                for i in range(4):
                    m = 4 * g + i
                    T1 = PF[3][:, i * 65:(i + 1) * 65]
                    r1, r1b = sc_col()
                    P.op("dve", lambda e, T1=T1, r1=r1: e.reciprocal(out=r1, in_=T1[:, 64:65]), reads=[PFb[3]], writes=[r1b])
                    if typ == "dil":
                        P.op("dve", lambda e, T1=T1, r1=r1, m=m: e.tensor_scalar(
                            out=MT[:, m, half * 64:(half + 1) * 64], in0=T1[:, 0:64], scalar1=r1, scalar2=None, op0=ALU.mult),
                            reads=[PFb[3], r1b], writes=[mtb])
                    else:
                        T2 = PF[2][:, i * 65:(i + 1) * 65]
                        r2, r2b = sc_col()
                        ssq, ssqb = sc_col()
                        std, stdb = sc_col()
                        rstd, rstdb = sc_col()
                        P.op("dve", lambda e, T2=T2, r2=r2: e.reciprocal(out=r2, in_=T2[:, 64:65]), reads=[PFb[2]], writes=[r2b])
                        P.op("dve", lambda e, r2=r2: e.tensor_tensor(out=r2, in0=r2, in1=lamneg, op=ALU.mult),
                             reads=[r2b, lamnegb], writes=[r2b])
                        P.op("dve", lambda e, T1=T1, r1=r1: e.tensor_scalar(out=TMP, in0=T1[:, 0:64], scalar1=r1, scalar2=None, op0=ALU.mult),
                             reads=[PFb[3], r1b], writes=[tmpb])
                        P.op("dve", lambda e, T2=T2, r2=r2: e.scalar_tensor_tensor(out=OTMP, in0=T2[:, 0:64], scalar=r2, in1=TMP,
                                                                                  op0=ALU.mult, op1=ALU.add),
                             reads=[PFb[2], r2b, tmpb], writes=[otmpb])
                        P.op("dve", lambda e, ssq=ssq: e.scalar_tensor_tensor(out=TMP, in0=OTMP, scalar=1.0, in1=OTMP,
                                                                              op0=ALU.mult, op1=ALU.mult, accum_out=ssq),
                             reads=[otmpb, tmpb], writes=[tmpb, ssqb])
                        P.op("act", lambda e, ssq=ssq, std=std: e.activation(out=std, in_=ssq, func=AF.Sqrt, bias=epsT[:, 0:1], scale=1.0 / 64),
                             reads=[ssqb, cbuf], writes=[stdb])
                        P.op("dve", lambda e, std=std, rstd=rstd: e.reciprocal(out=rstd, in_=std), reads=[stdb], writes=[rstdb])
                        P.op("dve", lambda e, rstd=rstd, m=m: e.scalar_tensor_tensor(
                            out=MT[:, m, half * 64:(half + 1) * 64], in0=OTMP, scalar=rstd, in1=GSUB, op0=ALU.mult, op1=ALU.mult),
                            reads=[otmpb, rstdb, gsubb], writes=[mtb])

        def sb_head(hh, h):
            half = hh % 2
            sa = [0, 3, 4, 7, 8, 11, 12, 15]
            sbm = [1, 2, 5, 6, 9, 10, 13, 14]
            ua = [(m, u, 0) for m in sa for u in range(2 * (m + 1))]
            ub = [(m, u, 1) for m in sbm for u in range(2 * (m + 1))]
            assert len(ua) == len(ub)
            units = [x for pair_ in zip(ua, ub) for x in pair_]
            n = len(units)
            NR = 4

            def qk(t):
                m, u, st = units[t]
                bank = t % NR
                col0 = (120 - 8 * m) * 128 + 512 * u
                zone = u < 2
                P.op("pe", lambda e: e.matmul(PF[bank][:], lhsT=QT[0:64, m * 128:(m + 1) * 128], rhs=KT[0:64, col0:col0 + 512],
                                              start=True, stop=not zone),
                     reads=[qtb, ktb[col0 // 4096]], writes=[PFb[bank]], sig=not zone)
                if zone:
                    P.op("pe", lambda e: e.matmul(PF[bank][:], lhsT=identb[:], rhs=MSB[:, u * 512:(u + 1) * 512], start=False, stop=True),
                         reads=[mskb, cbuf], writes=[PFb[bank]])

            def sg(t):
                bank = t % NR
                P.op("act", lambda e: e.activation(out=SS[t % NR], in_=PF[bank][:], func=AF.Sigmoid, scale=-1.0),
                     reads=[PFb[bank]], writes=[ssb[t % NR]])

            def scan(t):
                m, u, st = units[t]
                cur, prv = CP[t % NR], CP[(t - 2) % NR]
                if u == 0:
                    P.op("pool", lambda e: e.memset(cur[:, 0:1], 1.0), writes=[cpb[t % NR]])
                    init = 1.0
                    rd = [ssb[t % NR]]
                else:
                    P.op("pool", lambda e: e.tensor_copy(out=cur[:, 0:1], in_=prv[:, 512:513]),
                         reads=[cpb[(t - 2) % NR]], writes=[cpb[t % NR]])
                    init = prv[:, 512:513]
                    rd = [ssb[t % NR], cpb[(t - 2) % NR]]
                P.op("dve", lambda e: e.tensor_tensor_scan(out=cur[:, 1:513], data0=SS[t % NR], data1=ZB[:, 0:512], initial=init,
                                                           op0=ALU.mult, op1=ALU.add),
                     reads=rd + [zerob], writes=[cpb[t % NR]])
                P.op("pool", lambda e: e.tensor_tensor(out=PT[t % NR], in0=cur[:, 0:512], in1=cur[:, 1:513], op=ALU.subtract),
                     reads=[cpb[t % NR]], writes=[ptb[t % NR]])

            def tr(t):
                pbi = t % 2
                for j in range(4):
                    P.op("pe", lambda e, j=j: e.transpose(out=PB[pbi][:, j * 128:(j + 1) * 128], in_=PT[t % NR][:, j * 128:(j + 1) * 128],
                                                          identity=identb[:]),
                         reads=[ptb[t % NR], cbuf], writes=[PBb[pbi]], sig=(j == 3))
                P.op("act", lambda e: e.activation(out=WT[t % NR], in_=PB[pbi][:, 0:512], func=AF.Copy),
                     reads=[PBb[pbi]], writes=[wtb[t % NR]])

            def pv(t):
                m, u, st = units[t]
                ob = 4 + st
                last_u = 2 * (m + 1) - 1
                slot0 = (120 - 8 * m) + 4 * u
                for j in range(4):
                    P.op("pe", lambda e, j=j: e.matmul(PF[ob][:, 0:64], lhsT=WT[t % NR][:, j * 128:(j + 1) * 128],
                                                       rhs=VV[:, slot0 + j, 0:64], start=(u == 0 and j == 0),
                                                       stop=(u == last_u and j == 3)),
                         reads=[wtb[t % NR], vvb], writes=[PFb[ob]], sig=(j == 3))
                if u == last_u:
                    P.op("dve", lambda e: e.tensor_copy(out=MT[:, m, half * 64:(half + 1) * 64], in_=PF[ob][:, 0:64]),
                         reads=[PFb[ob]], writes=[mtb])

            pipeline(n, [(0, qk), (0, sg), (0, scan), (2, tr), (3, pv)])

        def flush_pair(pair):
            for q4 in range(4):
                pbi = q4 % 2
                for j in range(4):
                    m = q4 * 4 + j
                    P.op("pe", lambda e, m=m, j=j, pbi=pbi: e.transpose(out=PB[pbi][:, j * 128:(j + 1) * 128], in_=MT[:, m, :], identity=identb[:]),
                         reads=[mtb, cbuf], writes=[PBb[pbi]], sig=(j == 3))
                P.op("dve", lambda e, q4=q4, pbi=pbi: e.tensor_copy(out=MIX3[:, pair, q4 * 512:(q4 + 1) * 512], in_=PB[pbi][:, 0:512]),
                     reads=[PBb[pbi]], writes=[mixb[pair]])

        for hh, (typ, h) in enumerate(HEADS):
            if head_sel is not None and hh not in head_sel:
                continue
            if hh == 10:
                P.barrier()
            load_head(hh, typ)
            import os
            dbg = os.environ.get("DBG_MODE", "")
            if dbg.startswith("loadonly"):
                P.op("dve", lambda e: e.tensor_copy(out=MT[:, 0, :], in_=KT[:, 0:128]), reads=ktb + [vvb, qtb], writes=[mtb])
            elif typ == "sb":
                sb_head(hh, h)
            else:
                softmax_head(hh, typ, h)
            if dbg.endswith("noflush"):
                continue
            if hh % 2 == 1 or head_sel is not None:
                flush_pair(hh // 2)
        return MIX3, mixb

    ZB = sb("ZB", [128, 512], BF16)
    zerob = Buf("zero")
    P.op("pool", lambda e: e.memset(ZB[:], 0.0), writes=[zerob])

    def out_proj(layer, MIX3, mixb):
        wo = WD[("wout", layer)]
        wov = wo.rearrange("(k p) n -> p k n", p=128)
        WO3 = vb(0, 8192).rearrange("p (k n) -> p k n", k=8)
        wob = [Buf("wo0"), Buf("wo1")]
        for i in range(2):
            P.dma("pool", WO3[:, i * 4:(i + 1) * 4, :], wov[:, i * 4:(i + 1) * 4, :], writes=[wob[i]], sembuf=wob[i])
        load_gain(1, layer, 3)
        for m in range(NB):
            for hf in range(2):
                bank = 4 + hf
                for k in range(8):
                    P.op("pe", lambda e, k=k, m=m, hf=hf, bank=bank: e.matmul(
                        PF[bank][:], lhsT=MIX3[:, k, m * 128:(m + 1) * 128], rhs=WO3[:, k, hf * 512:(hf + 1) * 512],
                        start=(k == 0), stop=(k == 7)),
                        reads=[mixb[k], wob[k // 4]], writes=[PFb[bank]], sig=(k == 7))
            postnorm_residual(m, (4, 5), 1, 1.0)

    for (kind, layer) in stages:
        if kind == "A":
            ffn(layer, 0, 0, 1)
            P.barrier()
            proj(layer)
            P.barrier()
        else:
            MIX3, mixb = attention_body(layer)
            P.barrier()
            if debug_mix:
                dbg = dout("dbg_mix", [128, 8 * TOK], BF16)
                P.dma("sp", dbg.rearrange("p (k t) -> p k t", k=8), MIX3, reads=mixb, sembuf=Buf("dbg"), final=True)
            if attn_only:
                continue
            out_proj(layer, MIX3, mixb)
            P.barrier()
            ffn(layer, 1, 4, 5)
            P.barrier()
    xo = x_out.rearrange("(m p) d -> p m d", p=128)
    xst = Buf("xstore")
    for m in range(NB):
        P.dma("sp", xo[:, m, :], X[:, m * D:(m + 1) * D], reads=[xb[m]], sembuf=xst, final=True)
    P.run()
    es.close()
    return nc


_SHARED = None


def _tok_index(c):
    m = np.arange(NB)[:, None]
    j = np.arange(128)[None, :]
    return ((8 * m + c) * 128 + 127 - j).reshape(-1)


def _launch(stages, per_core_extra, weights, debug_mix=False, head_sel=None, attn_only=False):
    global _SHARED
    if _SHARED is None:
        _SHARED = shared_constants()
    nc = build_program(stages, debug_mix=debug_mix, head_sel=head_sel, attn_only=attn_only)
    in_maps = []
    for c in range(NCORES):
        mp = dict(_SHARED)
        mp.update(core_constants(c))
        mp.update(weights)
        mp.update(per_core_extra[c])
        in_maps.append(mp)
    res = run_bass_kernel_spmd(nc, in_maps, core_ids=list(range(NCORES)))
    return res.results


def _weights_for(stages, inp):
    w = {}
    for (kind, l) in stages:
        j = 0 if kind == "A" else 1
        w[f"gate{l}_{j}"] = np.ascontiguousarray(inp["w_ffn_gate"][l, j])
        w[f"up{l}_{j}"] = np.ascontiguousarray(inp["w_ffn_up"][l, j])
        w[f"down{l}_{j}"] = np.ascontiguousarray(inp["w_ffn_down"][l, j])
        w[f"gains{l}"] = np.ascontiguousarray(inp["norm_gains"][l])
        if kind == "A":
            w[f"win{l}"] = np.ascontiguousarray(inp["w_in"][l])
        else:
            w[f"wout{l}"] = np.ascontiguousarray(inp["w_out"][l])
            w[f"dlam{l}"] = np.ascontiguousarray(inp["diff_lambda"][l].reshape(1, 128))
            w[f"dgain{l}"] = np.ascontiguousarray(inp["diff_subln_gain"][l].reshape(1, 64))
    return w


def _gather_kv(results):
    kt_all = np.zeros((16, 128, S), NPBF)
    v_all = np.zeros((16, 128, NBLK, 65), NPBF)
    for c in range(NCORES):
        kt = results[c]["kt_out"].reshape(16, 128, NB, 128)
        vv = results[c]["v_out"].reshape(16, 128, NB, 65)
        for m in range(NB):
            slot = 127 - (8 * m + c)
            kt_all[:, :, slot * 128:(slot + 1) * 128] = kt[:, :, m, :]
            v_all[:, :, slot, :] = vv[:, :, m, :]
    return kt_all, v_all.reshape(16, 128, NBLK * 65)


def kernel(x, norm_gains, w_ffn_gate, w_ffn_up, w_ffn_down, w_in, w_out, diff_lambda, diff_subln_gain):
    inp = dict(x=np.asarray(x), norm_gains=np.asarray(norm_gains), w_ffn_gate=np.asarray(w_ffn_gate),
               w_ffn_up=np.asarray(w_ffn_up), w_ffn_down=np.asarray(w_ffn_down), w_in=np.asarray(w_in),
               w_out=np.asarray(w_out), diff_lambda=np.asarray(diff_lambda), diff_subln_gain=np.asarray(diff_subln_gain))
    x2 = inp["x"].reshape(S, D)
    idx = [_tok_index(c) for c in range(NCORES)]
    xs = [np.ascontiguousarray(x2[idx[c]]) for c in range(NCORES)]
    st = [("A", 0)]
    r = _launch(st, [{"x_in": xs[c]} for c in range(NCORES)], _weights_for(st, inp))
    for l in range(DEPTH):
        kt_all, v_all = _gather_kv(r)
        st = [("B", l)] + ([("A", l + 1)] if l + 1 < DEPTH else [])
        extra = [{"x_in": r[c]["x_out"], "kt_all": kt_all, "v_all": v_all, "qt_in": r[c]["qt_out"]} for c in range(NCORES)]
        r = _launch(st, extra, _weights_for(st, inp))
    out = np.zeros((S, D), np.float32)
    for c in range(NCORES):
        out[idx[c]] = r[c]["x_out"]
    return out.reshape(1, S, D)
```

```python
import math
import contextlib
import numpy as np
import ml_dtypes
import concourse.bass as bass
import concourse.mybir as mybir
from concourse.bass_utils import run_bass_kernel_spmd

F32 = mybir.dt.float32
BF16 = mybir.dt.bfloat16
AF = mybir.ActivationFunctionType
ALU = mybir.AluOpType
NPBF = ml_dtypes.bfloat16

NCORES = 8
D = 1024
DFF = 2816
NF = DFF // 128
S = 16384
TOK = S // NCORES
NB = TOK // 128
NBLK = S // 128
DEPTH = 2
EPS = 1e-6
NEG = -30000.0
THRESH = 120.0
HEADS = ([("dil", h) for h in range(6)] + [("diff", h) for h in range(4)] + [("sb", h) for h in range(6)])
QCOL = {"dil": 0, "diff": 1152, "sb": 1920}
KCOL = {"dil": 384, "diff": 1408, "sb": 2304}
VCOL = {"dil": 768, "diff": 1664, "sb": 2688}
KROWS = {"dil": 96, "diff": 112, "sb": 64}


def alibi(n):
    return [2.0 ** (-8.0 * (i + 1) / n) for i in range(n)]


SLOPE = {("dil", h): alibi(6)[h] for h in range(6)}
SLOPE.update({("diff", h): alibi(4)[h] for h in range(4)})
SLOPE_LIST = [("dil", h) for h in range(6)] + [("diff", h) for h in range(4)]


class Buf:
    __slots__ = ("name", "writer", "readers", "sem", "semcnt")

    def __init__(self, name):
        self.name = name
        self.writer = None
        self.readers = {}
        self.sem = None
        self.semcnt = 0


ENGS = ("pe", "act", "dve", "pool", "sp")


class Prog:
    def __init__(self, nc, es):
        self.nc = nc
        self.es = es
        self.q = {e: [] for e in ENGS}
        self.esem = {e: es.enter_context(nc.semaphore("S_" + e)) for e in ("pe", "act", "dve", "pool")}
        self.ecnt = {e: 0 for e in self.esem}
        self.waited = {e: {} for e in ENGS}
        self.pending = {e: [] for e in ENGS}
        self.finals = []
        self.nsem = 4
        self.dma_tks = {}

    def _wait(self, e, tk):
        if tk is None:
            return
        sem, val, src = tk
        if sem is None:
            assert src == e, "cross-engine wait on unsignaled op"
            return
        if src == "pe" and e == "pe":
            return
        key = id(sem)
        if self.waited[e].get(key, 0) >= val:
            return
        self.waited[e][key] = val
        self.q[e].append(("w", sem, val))

    def _deps(self, e, reads, writes):
        for b in reads:
            self._wait(e, b.writer)
        for b in writes:
            self._wait(e, b.writer)
            for r in b.readers.values():
                self._wait(e, r)

    def _commit(self, tk, reads, writes):
        for b in reads:
            if b not in writes:
                b.readers[id(tk[0])] = tk
        for b in writes:
            b.writer = tk
            b.readers = {}

    def op(self, e, fn, reads=(), writes=(), sig=True):
        self._deps(e, reads, writes)
        if sig:
            self.ecnt[e] += 1
            tk = (self.esem[e], self.ecnt[e], e)
            self.q[e].append(("o", fn, self.esem[e], 1))
            for (rs, ws) in self.pending[e]:
                self._commit(tk, rs, ws)
            self.pending[e] = []
            self._commit(tk, reads, writes)
        else:
            assert e == "pe"
            self.q[e].append(("o", fn, None, 0))
            self.pending[e].append((tuple(reads), tuple(writes)))
            for b in writes:
                b.writer = (None, 0, e)
                b.readers = {}

    def dma(self, e, out_ap, in_ap, reads=(), writes=(), sembuf=None, final=False):
        self._deps(e, reads, writes)
        sb = sembuf
        if sb.sem is None:
            sb.sem = self.es.enter_context(self.nc.semaphore("D%d_%s" % (self.nsem, sb.name)))
            self.nsem += 1
        sb.semcnt += 16
        tk = (sb.sem, sb.semcnt, "dma")
        self.q[e].append(("d", out_ap, in_ap, sb.sem))
        self.dma_tks[id(sb.sem)] = tk
        self._commit(tk, reads, writes)
        if final:
            self.finals.append(tk)
        return tk

    def barrier(self):
        for b_e in ENGS:
            assert not self.pending[b_e], "dangling unsignaled ops on " + b_e
        tks = [(self.esem[e], self.ecnt[e], e) for e in self.esem if self.ecnt[e] > 0]
        tks += list(self.dma_tks.values())
        for e in ENGS:
            for tk in tks:
                self._wait(e, tk)

    def run(self):
        nc = self.nc
        for b_e in ENGS:
            assert not self.pending[b_e], "dangling unsignaled ops on " + b_e
        for tk in self.finals:
            self._wait("sp", tk)

        def replay(eng, items):
            for it in items:
                if it[0] == "w":
                    eng.wait_ge(it[1], it[2])
                elif it[0] == "o":
                    ins = it[1](eng)
                    if it[2] is not None:
                        ins.then_inc(it[2], it[3])
                else:
                    eng.dma_start(out=it[1], in_=it[2]).then_inc(it[3], 16)

        with nc.Block() as block:
            @block.tensor
            def _(eng):
                replay(eng, self.q["pe"])

            @block.scalar
            def _(eng):
                replay(eng, self.q["act"])

            @block.vector
            def _(eng):
                replay(eng, self.q["dve"])

            @block.gpsimd
            def _(eng):
                replay(eng, self.q["pool"])

            @block.sync
            def _(eng):
                replay(eng, self.q["sp"])


def pipeline(n, stages):
    maxlag = max(l for l, _ in stages)
    for t in range(n + maxlag):
        for lag, fn in stages:
            u = t - lag
            if 0 <= u < n:
                fn(u)


def _bf(x):
    return np.asarray(x, dtype=np.float32).astype(NPBF)


def core_constants(c):
    p = np.arange(128)[:, None].astype(np.int64)
    col = np.arange(128)[None, :].astype(np.int64)
    out = {}
    msb = np.zeros((128, 8, 128), np.float32)
    for a in range(8):
        r = 7 - a
        if r > c:
            msb[:, a, :] = NEG
        elif r == c:
            msb[:, a, :] = np.where(col > p, 0.0, NEG)
    out["c_msb"] = _bf(msb.reshape(128, 1024))
    mc = np.zeros((128, 8, 128), np.float32)
    for r in range(8):
        if r > c:
            mc[:, r, :] = NEG
        elif r == c:
            mc[:, r, :] = np.where(p >= col, 0.0, NEG)
    out["c_mc"] = _bf(mc.reshape(128, 1024))
    td = np.full((128, 24, 128), NEG, np.float32)
    for idx in range(24):
        Dd = idx - 7 + c
        if 0 <= Dd <= 16:
            delta = 128 * Dd - col + p
            mult = ((delta >= 0) & (delta <= 128)).astype(np.int64)
            mult = mult + ((delta >= 0) & (delta <= 512) & (delta % 4 == 0))
            mult = mult + ((delta >= 0) & (delta <= 2048) & (delta % 16 == 0))
            td[:, idx, :] = np.where(mult > 0, np.log(np.maximum(mult, 1)), NEG)
    out["c_td"] = _bf(td.reshape(128, 24 * 128))
    ds = np.arange(159)[None, :]
    out["c_base"] = (128.0 * (ds - 31 + c) + p).astype(np.float32)
    return out


def shared_constants():
    out = {}
    out["c_identb"] = _bf(np.eye(128))
    out["c_identf"] = np.eye(128, dtype=np.float32)
    e_dil = np.zeros((8, 128), np.float32)
    for r in range(4):
        e_dil[r, 64 + r] = 1.0
    e_diff = np.zeros((8, 128), np.float32)
    for r in range(4):
        e_diff[r, 32 + r] = 1.0
        e_diff[4 + r, 96 + r] = 1.0
    out["c_edil"] = _bf(e_dil)
    out["c_ediff"] = _bf(e_diff)
    colq = np.arange(512)
    jq = (colq % 128).astype(np.float32)
    ii = (colq // 128).astype(np.float32)
    qa = np.stack([jq, jq, -1024.0 * ii, -1024.0 * ii] * 2, 0)
    out["c_qaug"] = _bf(qa)
    ks = np.zeros((8, 10, 128), np.float32)
    for si, key in enumerate(SLOPE_LIST):
        s = np.float32(SLOPE[key])
        hi = np.float32(s.astype(NPBF))
        lo = np.float32(np.float32(s - hi).astype(NPBF))
        sel = e_dil if key[0] == "dil" else e_diff
        for r, v in enumerate([hi, lo, hi, lo] * 2):
            ks[r, si, :] = sel[r] * v
    out["c_ksel"] = _bf(ks.reshape(8, 1280))
    out["c_ones8"] = _bf(np.ones((8, 512)))
    scl = np.ones((128, 2), np.float32)
    scl[0:64, 0] = 64.0 ** -0.5
    scl[0:32, 1] = 32.0 ** -0.5
    scl[64:96, 1] = 32.0 ** -0.5
    out["c_scl"] = scl
    return out


CONST_SPECS = {
    "c_msb": ([128, 1024], BF16), "c_mc": ([128, 1024], BF16), "c_td": ([128, 3072], BF16),
    "c_base": ([128, 159], F32), "c_identb": ([128, 128], BF16), "c_identf": ([128, 128], F32),
    "c_edil": ([8, 128], BF16), "c_ediff": ([8, 128], BF16), "c_qaug": ([8, 512], BF16),
    "c_ksel": ([8, 1280], BF16), "c_ones8": ([8, 512], BF16), "c_scl": ([128, 2], F32),
}


def build_program(stages, debug_mix=False, head_sel=None, attn_only=False):
    nc = bass.Bass("TRN2", target_bir_lowering=False)
    es = contextlib.ExitStack()
    P = Prog(nc, es)
    has_b = [l for (k, l) in stages if k == "B"]
    has_a = [l for (k, l) in stages if k == "A"]
    last_is_a = stages[-1][0] == "A"

    def din(name, shape, dt=F32):
        return nc.dram_tensor(name, shape, dt, kind="ExternalInput").ap()

    def dout(name, shape, dt=F32):
        return nc.dram_tensor(name, shape, dt, kind="ExternalOutput").ap()

    x_in = din("x_in", [TOK, D])
    x_out = dout("x_out", [TOK, D])
    consts = {k: din(k, shp, dt) for k, (shp, dt) in CONST_SPECS.items()}
    WD = {}
    for l in has_a:
        WD[("gate", l, 0)] = din(f"gate{l}_0", [D, DFF])
        WD[("up", l, 0)] = din(f"up{l}_0", [D, DFF])
        WD[("down", l, 0)] = din(f"down{l}_0", [DFF, D])
        WD[("win", l)] = din(f"win{l}", [D, 3072])
    for l in has_b:
        WD[("gate", l, 1)] = din(f"gate{l}_1", [D, DFF])
        WD[("up", l, 1)] = din(f"up{l}_1", [D, DFF])
        WD[("down", l, 1)] = din(f"down{l}_1", [DFF, D])
        WD[("wout", l)] = din(f"wout{l}", [D, D])
        WD[("dlam", l)] = din(f"dlam{l}", [1, 128])
        WD[("dgain", l)] = din(f"dgain{l}", [1, 64])
    gains = {l: din(f"gains{l}", [6, D]) for l in sorted(set(has_a + has_b))}
    if has_b:
        kt_all = din("kt_all", [16, 128, S], BF16)
        v_all = din("v_all", [16, 128, NBLK * 65], BF16)
        qt_in = din("qt_in", [16, 128, TOK], BF16)
    if last_is_a:
        qt_out = dout("qt_out", [16, 128, TOK], BF16)
        kt_out = dout("kt_out", [16, 128, TOK], BF16)
        v_out = dout("v_out", [16, 128, NB * 65], BF16)

    def sb(name, shape, dt):
        return es.enter_context(nc.sbuf_tensor(name, shape, dt))

    def ps(name, shape, dt):
        return es.enter_context(nc.psum_tensor(name, shape, dt))

    X = sb("X", [128, NB * D], F32)
    xb = [Buf(f"x{m}") for m in range(NB)]
    ARB = sb("ARB", [128, 59392], BF16)
    ARF = sb("ARF", [128, 4112], F32)
    identb = sb("identb", [128, 128], BF16)
    identf = sb("identf", [128, 128], F32)
    edil = sb("edil", [8, 128], BF16)
    ediff = sb("ediff", [8, 128], BF16)
    qaug = sb("qaug", [8, 512], BF16)
    ksel = sb("ksel", [8, 1280], BF16)
    ones8 = sb("ones8", [8, 512], BF16)
    scl = sb("scl", [128, 2], F32)
    base = sb("base", [128, 159], F32)
    epsT = sb("epsT", [128, 1], F32)
    small = sb("small", [128, 64], F32)
    cbuf = Buf("consts")
    PF = [ps(f"pf{i}", [128, 512], F32) for i in range(6)]
    PFb = [Buf(f"pf{i}") for i in range(6)]
    PB = [ps(f"pb{i}", [128, 1024], BF16) for i in range(2)]
    PBb = [Buf(f"pb{i}") for i in range(2)]

    def vb(off, n):
        return ARB[:, off:off + n]

    def vf(off, n):
        return ARF[:, off:off + n]

    xsem = Buf("xload")
    xv = x_in.rearrange("(m p) d -> p m d", p=128)
    for m in range(NB):
        P.dma("sp", X[:, m * D:(m + 1) * D], xv[:, m, :], writes=[xb[m]], sembuf=xsem)
    tkx = (xsem.sem, xsem.semcnt, "dma")
    for m in range(NB):
        xb[m].writer = tkx
    for (tile, key) in ((identb, "c_identb"), (identf, "c_identf"), (edil, "c_edil"), (ediff, "c_ediff"),
                        (qaug, "c_qaug"), (ksel, "c_ksel"), (ones8, "c_ones8"), (scl, "c_scl"), (base, "c_base")):
        P.dma("sp", tile[:], consts[key], writes=[], sembuf=cbuf)
    cbuf.writer = (cbuf.sem, cbuf.semcnt, "dma")
    P.op("pool", lambda e: e.memset(epsT[:], EPS), writes=[cbuf])

    GREP = [vf(0, 1024), vf(1024, 1024)]
    GREPb = [Buf("grep0"), Buf("grep1")]
    TT = vf(2048, 1024)
    TTb = Buf("tt")

    def load_gain(slot, layer, idx):
        P.dma("sp", GREP[slot], bcast_rows(gains[layer], idx, D), writes=[GREPb[slot]], sembuf=GREPb[slot])

    def bcast_rows(ap2d, row, n):
        r = ap2d[row:row + 1, 0:n]
        return bass.AP(tensor=r.tensor, offset=r.offset, ap=[[0, 128], [1, n]])

    sc_ctr = [0]

    def sc_col():
        i = sc_ctr[0] % 48
        sc_ctr[0] += 1
        return small[:, i:i + 1], SCb[i]

    SCb = [Buf(f"sc{i}") for i in range(64)]

    def prenorm_block(m, gslot, hn_ap, hnb):
        xm = X[:, m * D:(m + 1) * D]
        ssq, ssqb = sc_col()
        std, stdb = sc_col()
        rstd, rstdb = sc_col()
        P.op("act", lambda e: e.activation(out=hn_ap, in_=xm, func=AF.Square, accum_out=ssq),
             reads=[xb[m]], writes=[ssqb, hnb])
        P.op("act", lambda e: e.activation(out=std, in_=ssq, func=AF.Sqrt, bias=epsT[:, 0:1], scale=1.0 / D),
             reads=[ssqb, cbuf], writes=[stdb])
        P.op("dve", lambda e: e.reciprocal(out=rstd, in_=std), reads=[stdb], writes=[rstdb])
        P.op("dve", lambda e: e.scalar_tensor_tensor(out=hn_ap, in0=xm, scalar=rstd, in1=GREP[gslot],
                                                     op0=ALU.mult, op1=ALU.mult),
             reads=[xb[m], rstdb, GREPb[gslot]], writes=[hnb])

    def transpose_block(hn_ap, hnb, hT3, hTb, tokcol, pbi):
        pb = PB[pbi]
        for k in range(8):
            P.op("pe", lambda e, k=k: e.transpose(out=pb[:, k * 128:(k + 1) * 128], in_=hn_ap[:, k * 128:(k + 1) * 128],
                                                  identity=identb[:]),
                 reads=[hnb, cbuf], writes=[PBb[pbi]], sig=(k == 7))
        P.op("act", lambda e: e.activation(out=hT3[:, :, tokcol:tokcol + 128],
                                           in_=pb[:, 0:1024].rearrange("p (k t) -> p k t", k=8), func=AF.Copy),
             reads=[PBb[pbi]], writes=[hTb])

    def postnorm_residual(m, ybanks, gslot, coef):
        xm = X[:, m * D:(m + 1) * D]
        s0, s0b = sc_col()
        s1, s1b = sc_col()
        ssq, ssqb = sc_col()
        std, stdb = sc_col()
        rstd, rstdb = sc_col()
        for (sc, scb, bi) in ((s0, s0b, ybanks[0]), (s1, s1b, ybanks[1])):
            P.op("act", lambda e, sc=sc, bi=bi: e.activation(out=JUNK2[0], in_=PF[bi][:], func=AF.Square, accum_out=sc),
                 reads=[PFb[bi]], writes=[scb, JUNK2b[0]])
        P.op("dve", lambda e: e.tensor_tensor(out=ssq, in0=s0, in1=s1, op=ALU.add), reads=[s0b, s1b], writes=[ssqb])
        P.op("act", lambda e: e.activation(out=std, in_=ssq, func=AF.Sqrt, bias=epsT[:, 0:1], scale=1.0 / D),
             reads=[ssqb, cbuf], writes=[stdb])
        P.op("dve", lambda e: e.reciprocal(out=rstd, in_=std), reads=[stdb], writes=[rstdb])
        for hf in range(2):
            bi = ybanks[hf]
            P.op("dve", lambda e, bi=bi, hf=hf: e.scalar_tensor_tensor(
                out=TT[:, hf * 512:(hf + 1) * 512], in0=PF[bi][:], scalar=float(coef),
                in1=GREP[gslot][:, hf * 512:(hf + 1) * 512], op0=ALU.mult, op1=ALU.mult),
                reads=[PFb[bi], GREPb[gslot]], writes=[TTb])
        P.op("dve", lambda e: e.scalar_tensor_tensor(out=xm, in0=TT[:], scalar=rstd, in1=xm, op0=ALU.mult, op1=ALU.add),
             reads=[TTb, rstdb, xb[m]], writes=[xb[m]])

    def ffn(layer, which, g_pre, g_post):
        wg, wu, wdn = WD[("gate", layer, which)], WD[("up", layer, which)], WD[("down", layer, which)]
        ACT3 = vb(0, 22528).rearrange("p (f t) -> p f t", f=NF)
        WDN = vb(22528, 22528).rearrange("p (h f c) -> p h f c", h=2, f=NF)
        HT3 = vb(45056, 8192).rearrange("p (k t) -> p k t", k=8)
        WGU = vb(53248, 4096).rearrange("p (s w k n) -> p s w k n", s=2, w=2, k=8)
        HN = [vb(57344, 1024), vb(58368, 1024)]
        actb = [Buf(f"act{f}") for f in range(NF)]
        wdb = [[Buf(f"wd{h}_{i}") for i in range(2)] for h in range(2)]
        htb = Buf("ht")
        wgub = [[Buf(f"wgu{s}_{w}") for w in range(2)] for s in range(2)]
        hnb = [Buf("hn0"), Buf("hn1")]
        wgv = wg.rearrange("(k p) n -> p k n", p=128)
        wuv = wu.rearrange("(k p) n -> p k n", p=128)
        wdv = wdn.rearrange("(f p) c -> p f c", p=128)
        load_gain(0, layer, g_pre)
        load_gain(1, layer, g_post)

        def load_gu(f):
            s = f % 2
            P.dma("pool", WGU[:, s, 0], wgv[:, :, f * 128:(f + 1) * 128], writes=[wgub[s][0]], sembuf=wgub[s][0])
            P.dma("pool", WGU[:, s, 1], wuv[:, :, f * 128:(f + 1) * 128], writes=[wgub[s][1]], sembuf=wgub[s][1])

        def load_wd(h):
            for i in range(2):
                P.dma("pool", WDN[:, h, i * 11:(i + 1) * 11, :], wdv[:, i * 11:(i + 1) * 11, h * 512:(h + 1) * 512],
                      writes=[wdb[h][i]], sembuf=wdb[h][i])

        for tg in range(2):
            for j in range(8):
                m = tg * 8 + j
                prenorm_block(m, 0, HN[j % 2], hnb[j % 2])
                transpose_block(HN[j % 2], hnb[j % 2], HT3, htb, j * 128, j % 2)
            load_gu(0)
            load_gu(1)
            load_wd(0)
            load_wd(1)
            for f in range(NF):
                s = f % 2
                for hf in range(2):
                    gb, ub = (0, 1) if hf == 0 else (2, 3)
                    for (w, bank) in ((0, gb), (1, ub)):
                        for k in range(8):
                            P.op("pe", lambda e, w=w, bank=bank, k=k, s=s, hf=hf: e.matmul(
                                PF[bank][:], lhsT=WGU[:, s, w, k, :], rhs=HT3[:, k, hf * 512:(hf + 1) * 512],
                                start=(k == 0), stop=(k == 7)),
                                reads=[wgub[s][w], htb], writes=[PFb[bank]], sig=(k == 7))
                    SG = JUNK2[hf]
                    P.op("act", lambda e, gb=gb, SG=SG: e.activation(out=SG, in_=PF[gb][:], func=AF.Silu),
                         reads=[PFb[gb]], writes=[JUNK2b[hf]])
                    P.op("dve", lambda e, ub=ub, SG=SG, f=f, hf=hf: e.tensor_tensor(
                        out=ACT3[:, f, hf * 512:(hf + 1) * 512], in0=SG, in1=PF[ub][:], op=ALU.mult),
                        reads=[JUNK2b[hf], PFb[ub]], writes=[actb[f]])
                if f + 2 < NF:
                    load_gu(f + 2)
            for j in range(8):
                m = tg * 8 + j
                for h in range(2):
                    bank = 4 + h
                    for f in range(NF):
                        P.op("pe", lambda e, f=f, h=h, bank=bank, j=j: e.matmul(
                            PF[bank][:], lhsT=ACT3[:, f, j * 128:(j + 1) * 128], rhs=WDN[:, h, f, :],
                            start=(f == 0), stop=(f == NF - 1)),
                            reads=[actb[f], wdb[h][f // 11]], writes=[PFb[bank]], sig=(f == NF - 1))
                postnorm_residual(m, (4, 5), 1, 0.5)

    JUNK2 = [sb("SG0", [128, 512], BF16)[:], sb("SG1", [128, 512], BF16)[:]]
    JUNK2b = [Buf("sg0"), Buf("sg1")]

    def proj(layer):
        win = WD[("win", layer)]
        winv = win.rearrange("(k p) n -> p k n", p=128)
        HT3 = vb(0, 16384).rearrange("p (k t) -> p k t", k=8)
        WV = vb(16384, 8192).rearrange("p (k n) -> p k n", k=8)
        WST = vb(24576, 4096).rearrange("p (s k n) -> p s k n", s=4, k=8)
        HN = [vb(28672, 1024), vb(29696, 1024)]
        QST = [vb(30720, 512), vb(31232, 512)]
        VST = [vb(31744, 520).rearrange("p (h e) -> p h e", h=8), vb(32264, 520).rearrange("p (h e) -> p h e", h=8)]
        htb = Buf("pht")
        wvb = [Buf(f"wv{i}") for i in range(3)]
        wstb = [Buf(f"wst{i}") for i in range(4)]
        hnb = [Buf("phn0"), Buf("phn1")]
        qstb = [Buf("qst0"), Buf("qst1")]
        vstb = [Buf("vst0"), Buf("vst1")]
        load_gain(0, layer, 2)
        for s4 in range(4):
            P.op("pool", lambda e, s4=s4: e.memset(WST[:, s4], 0.0), writes=[wstb[s4]])
        for i in range(2):
            P.op("pool", lambda e, i=i: e.memset(VST[i], 1.0), writes=[vstb[i]])
        for i, (c0, n, d0) in enumerate(((768, 384, 0), (1664, 256, 384), (2688, 384, 640))):
            P.dma("pool", WV[:, :, d0:d0 + n], winv[:, :, c0:c0 + n], writes=[wvb[i]], sembuf=wvb[i])
        for m in range(NB):
            prenorm_block(m, 0, HN[m % 2], hnb[m % 2])
            transpose_block(HN[m % 2], hnb[m % 2], HT3, htb, m * 128, m % 2)
        cnt = [0, 0]
        qi = 0
        ring = 0
        for hh, (typ, h) in enumerate(HEADS):
            for which in ("q", "k"):
                colb = (QCOL if which == "q" else KCOL)[typ] + 64 * h
                if typ == "diff":
                    s4 = 2 + cnt[1] % 2
                    cnt[1] += 1
                    P.dma("pool", WST[:, s4, :, 0:32], winv[:, :, colb:colb + 32], writes=[wstb[s4]], sembuf=wstb[s4])
                    P.dma("pool", WST[:, s4, :, 64:96], winv[:, :, colb + 32:colb + 64], writes=[wstb[s4]], sembuf=wstb[s4])
                else:
                    s4 = cnt[0] % 2
                    cnt[0] += 1
                    P.dma("pool", WST[:, s4, :, 0:64], winv[:, :, colb:colb + 64], writes=[wstb[s4]], sembuf=wstb[s4])
                for g in range(4):
                    bank = ring % 4
                    ring += 1
                    aug = typ != "sb"
                    for k in range(8):
                        P.op("pe", lambda e, k=k, s4=s4, g=g, bank=bank, aug=aug: e.matmul(
                            PF[bank][:], lhsT=WST[:, s4, k, :], rhs=HT3[:, k, g * 512:(g + 1) * 512],
                            start=(k == 0), stop=(k == 7 and not aug)),
                            reads=[wstb[s4], htb], writes=[PFb[bank]], sig=(k == 7 and not aug))
                    if aug:
                        if which == "q":
                            sel = (edil if typ == "dil" else ediff)[:, :]
                            src = qaug[:, :]
                        else:
                            si = SLOPE_LIST.index((typ, h))
                            sel = ksel[:, si * 128:(si + 1) * 128]
                            src = ones8[:, :]
                        P.op("pe", lambda e, sel=sel, src=src, bank=bank: e.matmul(
                            PF[bank][:], lhsT=sel, rhs=src, start=False, stop=True),
                            reads=[cbuf], writes=[PFb[bank]])
                    st = qi % 2
                    qi += 1
                    if which == "q":
                        sc_i = 1 if typ == "diff" else 0
                        P.op("dve", lambda e, bank=bank, st=st, sc_i=sc_i: e.tensor_scalar(
                            out=QST[st], in0=PF[bank][:], scalar1=scl[:, sc_i:sc_i + 1], scalar2=None, op0=ALU.mult),
                            reads=[PFb[bank], cbuf], writes=[qstb[st]])
                        dst = qt_out[hh, :, g * 512:(g + 1) * 512]
                    else:
                        P.op("act", lambda e, bank=bank, st=st: e.activation(out=QST[st], in_=PF[bank][:], func=AF.Copy),
                             reads=[PFb[bank]], writes=[qstb[st]])
                        dst = kt_out[hh, :, g * 512:(g + 1) * 512]
                    P.dma("sp", dst, QST[st], reads=[qstb[st]], sembuf=qstb[st], final=True)
        for hf in range(2):
            for m in range(NB):
                bank = ring % 4
                ring += 1
                for k in range(8):
                    P.op("pe", lambda e, k=k, m=m, hf=hf, bank=bank: e.matmul(
                        PF[bank][:], lhsT=HT3[:, k, m * 128:(m + 1) * 128], rhs=WV[:, k, hf * 512:(hf + 1) * 512],
                        start=(k == 0), stop=(k == 7)),
                        reads=[htb, wvb[0], wvb[1], wvb[2]], writes=[PFb[bank]], sig=(k == 7))
                st = m % 2
                P.op("act", lambda e, bank=bank, st=st: e.activation(
                    out=VST[st][:, :, 0:64], in_=PF[bank][:].rearrange("p (h e) -> p h e", h=8), func=AF.Copy),
                    reads=[PFb[bank]], writes=[vstb[st]])
                dst = v_out[hf * 8:(hf + 1) * 8, :, m * 65:(m + 1) * 65].rearrange("h p e -> p h e")
                P.dma("sp", dst, VST[st], reads=[vstb[st]], sembuf=vstb[st], final=True)

    def attention_body(layer):
        lam_init = 0.8 - 0.6 * math.exp(-0.3 * layer)
        KT = vb(0, 16384)
        VV = vb(16384, 8320).rearrange("p (s e) -> p s e", e=65)
        QT = vb(24704, 2048)
        MIX3 = vb(26752, 16384).rearrange("p (k t) -> p k t", k=8)
        MSB = vb(43136, 1024)
        MC = vb(44160, 1024)
        TD = vb(45184, 3072)
        PT = [vb(48256 + i * 512, 512) for i in range(4)]
        WT = [vb(50304 + i * 512, 512) for i in range(4)]
        MT = vb(52352, 2048).rearrange("p (m f) -> p m f", m=NB)
        ktb = [Buf(f"kt{i}") for i in range(4)]
        vvb = Buf("vv")
        qtb = Buf("qt")
        mixb = [Buf(f"mix{k}") for k in range(8)]
        mskb = Buf("masks")
        ptb = [Buf(f"pt{i}") for i in range(4)]
        wtb = [Buf(f"wt{i}") for i in range(4)]
        mtb = Buf("mt")
        SS = [vf(i * 512, 512) for i in range(4)]
        CP = [vf(2048 + i * 516, 516) for i in range(4)]
        BT = vf(0, 159)
        OAf = vf(160, 512)
        TMP = vf(672, 64)
        OTMP = vf(736, 64)
        GSUB = vf(800, 64)
        LAMT = vf(864, 128)
        ssb = [Buf(f"ss{i}") for i in range(4)]
        cpb = [Buf(f"cp{i}") for i in range(4)]
        btb = Buf("bt")
        oab = Buf("oa")
        tmpb = Buf("tmp")
        otmpb = Buf("otmp")
        gsubb = Buf("gsub")
        lamb = Buf("lam")
        lamneg, lamnegb = small[:, 60:61], Buf("lamneg")

        P.op("pool", lambda e: e.memset(vb(52352, 2048), 0.0), writes=[mtb])
        P.dma("sp", MSB, consts["c_msb"], writes=[mskb], sembuf=mskb)
        P.dma("sp", MC, consts["c_mc"], writes=[mskb], sembuf=mskb)
        P.dma("sp", TD, consts["c_td"], writes=[mskb], sembuf=mskb)
        P.dma("sp", LAMT, bcast_rows(WD[("dlam", layer)], 0, 128), writes=[lamb], sembuf=lamb)
        P.dma("sp", GSUB, bcast_rows(WD[("dgain", layer)], 0, 64), writes=[gsubb], sembuf=gsubb)
        s1, s1b = sc_col()
        s2, s2b = sc_col()
        P.op("dve", lambda e: e.scalar_tensor_tensor(out=TMP[:, 0:32], in0=LAMT[:, 0:32], scalar=1.0, in1=LAMT[:, 32:64],
                                                     op0=ALU.mult, op1=ALU.mult, accum_out=s1),
             reads=[lamb], writes=[tmpb, s1b])
        P.op("dve", lambda e: e.scalar_tensor_tensor(out=TMP[:, 0:32], in0=LAMT[:, 64:96], scalar=1.0, in1=LAMT[:, 96:128],
                                                     op0=ALU.mult, op1=ALU.mult, accum_out=s2),
             reads=[lamb, tmpb], writes=[tmpb, s2b])
        P.op("act", lambda e: e.activation(out=s1, in_=s1, func=AF.Exp), reads=[s1b], writes=[s1b])
        P.op("act", lambda e: e.activation(out=s2, in_=s2, func=AF.Exp), reads=[s2b], writes=[s2b])
        P.op("dve", lambda e: e.tensor_tensor(out=lamneg, in0=s2, in1=s1, op=ALU.subtract), reads=[s1b, s2b], writes=[lamnegb])
        P.op("dve", lambda e: e.tensor_scalar(out=lamneg, in0=lamneg, scalar1=-lam_init, scalar2=None, op0=ALU.add),
             reads=[lamnegb], writes=[lamnegb])
        P.op("dve", lambda e: e.tensor_scalar(out=GSUB, in0=GSUB, scalar1=float(1.0 - lam_init), scalar2=None, op0=ALU.mult),
             reads=[gsubb], writes=[gsubb])

        def load_head(hh, typ):
            R = KROWS[typ]
            for i in range(4):
                P.dma("sp", KT[0:R, i * 4096:(i + 1) * 4096], kt_all[hh, 0:R, i * 4096:(i + 1) * 4096],
                      writes=[ktb[i]], sembuf=ktb[i])
            P.dma("sp", VV.rearrange("p s e -> p (s e)"), v_all[hh], writes=[vvb], sembuf=vvb)
            P.dma("sp", QT[0:R, :], qt_in[hh, 0:R, :], writes=[qtb], sembuf=qtb)

        def kblk_buf(slot):
            return ktb[slot // 32]

        def softmax_head(hh, typ, h):
            slope = SLOPE[(typ, h)]
            half = hh % 2
            P.op("dve", lambda e: e.tensor_scalar(out=BT, in0=base[:, :], scalar1=float(-slope), scalar2=None, op0=ALU.mult),
                 reads=[cbuf], writes=[btb])
            maps = [(0, 68)] if typ == "dil" else [(0, 36), (64, 36)]
            nmap = len(maps)
            sbanks = [0, 1, 2] if typ == "dil" else [0, 1]
            for g in range(4):
                units = []
                for kb in range(0, 32 * g + 32):
                    e_ = kb - 32 * g
                    ilo = 0 if e_ < 0 else e_ // 8
                    ihi = -1
                    for i in range(ilo, 4):
                        dmin = 128 * (32 * g + 8 * i - kb) - 127
                        ok = slope * dmin <= THRESH
                        if typ == "dil":
                            dsv = 32 * g + 8 * i - kb
                            ok = ok and (-7 <= dsv <= 16)
                        if ok:
                            ihi = i
                    if typ == "dil":
                        while ilo <= ihi and not (-7 <= 32 * g + 8 * ilo - kb <= 16):
                            ilo += 1
                    if ihi >= ilo:
                        units.append((kb, ilo, ihi))
                n = len(units)
                seq = [(u, mi) for u in range(n) for mi in range(nmap)]
                ns = len(seq)
                for mi in range(nmap):
                    P.op("pe", lambda e, mi=mi: e.matmul(PF[4 + mi][0:65, :], lhsT=VV[:, 0, :], rhs=ZB[:, :], start=True, stop=False),
                         reads=[vvb, zerob], writes=[PFb[4 + mi]], sig=False)

                def qk(t, g=g, units=units, seq=seq):
                    u, mi = seq[t]
                    kb, ilo, ihi = units[u]
                    r0, nr = maps[mi]
                    bank = sbanks[t % len(sbanks)]
                    slot = 127 - kb
                    c0, c1 = ilo * 128, (ihi + 1) * 128
                    e_ = kb - 32 * g
                    need_mask = (typ == "dil") or (e_ >= 0)
                    P.op("pe", lambda e: e.matmul(PF[bank][:, c0:c1], lhsT=KT[r0:r0 + nr, slot * 128:(slot + 1) * 128],
                                                  rhs=QT[r0:r0 + nr, g * 512 + c0:g * 512 + c1], start=True, stop=not need_mask),
                         reads=[kblk_buf(slot), qtb], writes=[PFb[bank]], sig=not need_mask)
                    if need_mask:
                        if typ == "dil":
                            i0 = 32 * g + 8 * ilo - kb + 7
                            cnt = ihi - ilo + 1
                            t0 = TD[:, i0 * 128:(i0 + 1) * 128]
                            rhs = bass.AP(tensor=t0.tensor, offset=t0.offset, ap=[list(t0.ap[0]), [8 * 128, cnt], [1, 128]])
                            P.op("pe", lambda e: e.matmul(PF[bank][:, c0:c1], lhsT=identb[:], rhs=rhs, start=False, stop=True),
                                 reads=[mskb, cbuf], writes=[PFb[bank]])
                        else:
                            r = e_ % 8
                            P.op("pe", lambda e: e.matmul(PF[bank][:, c0:c0 + 128], lhsT=identb[:], rhs=MC[:, r * 128:(r + 1) * 128],
                                                          start=False, stop=True),
                                 reads=[mskb, cbuf], writes=[PFb[bank]])

                def ex(t, g=g, units=units, seq=seq):
                    u, mi = seq[t]
                    kb, ilo, ihi = units[u]
                    bank = sbanks[t % len(sbanks)]
                    c0, c1 = ilo * 128, (ihi + 1) * 128
                    ds_ = 32 * g - kb + 31
                    P.op("act", lambda e: e.activation(out=PT[t % 3][:, c0:c1], in_=PF[bank][:, c0:c1], func=AF.Exp,
                                                       bias=BT[:, ds_:ds_ + 1], scale=1.0),
                         reads=[PFb[bank], btb], writes=[ptb[t % 3]])

                def pv(t, g=g, units=units, seq=seq, ns=ns):
                    u, mi = seq[t]
                    kb, ilo, ihi = units[u]
                    slot = 127 - kb
                    c0, c1 = ilo * 128, (ihi + 1) * 128
                    ob = 4 + mi
                    last = (u == len(units) - 1)
                    P.op("pe", lambda e: e.matmul(PF[ob][0:65, c0:c1], lhsT=VV[:, slot, :], rhs=PT[t % 3][:, c0:c1],
                                                  start=False, stop=last),
                         reads=[vvb, ptb[t % 3]], writes=[PFb[ob]], sig=True)

                pipeline(ns, [(0, qk), (0, ex), (1, pv)])
                tb = [3, 2]
                for mi in range(nmap):
                    ob = 4 + mi
                    P.op("dve", lambda e, ob=ob: e.tensor_copy(out=OAf[0:65, :], in_=PF[ob][0:65, :]),
                         reads=[PFb[ob]], writes=[oab])
                    for i in range(4):
                        P.op("pe", lambda e, i=i, mi=mi: e.transpose(out=PF[tb[mi]][:, i * 65:(i + 1) * 65],
                                                                   in_=OAf[0:65, i * 128:(i + 1) * 128],
                                                                   identity=identf[0:65, 0:65]),
                             reads=[oab, cbuf], writes=[PFb[tb[mi]]], sig=(i == 3))
                for i in range(4):
                    m = 4 * g + i
                    T1 = PF[3][:, i * 65:(i + 1) * 65]
                    r1, r1b = sc_col()
                    P.op("dve", lambda e, T1=T1, r1=r1: e.reciprocal(out=r1, in_=T1[:, 64:65]), reads=[PFb[3]], writes=[r1b])
                    if typ == "dil":
                        P.op("dve", lambda e, T1=T1, r1=r1, m=m: e.tensor_scalar(
                            out=MT[:, m, half * 64:(half + 1) * 64], in0=T1[:, 0:64], scalar1=r1, scalar2=None, op0=ALU.mult),
                            reads=[PFb[3], r1b], writes=[mtb])
                    else:
                        T2 = PF[2][:, i * 65:(i + 1) * 65]
                        r2, r2b = sc_col()
                        ssq, ssqb = sc_col()
                        std, stdb = sc_col()
                        rstd, rstdb = sc_col()
                        P.op("dve", lambda e, T2=T2, r2=r2: e.reciprocal(out=r2, in_=T2[:, 64:65]), reads=[PFb[2]], writes=[r2b])
                        P.op("dve", lambda e, r2=r2: e.tensor_tensor(out=r2, in0=r2, in1=lamneg, op=ALU.mult),
                             reads=[r2b, lamnegb], writes=[r2b])
                        P.op("dve", lambda e, T1=T1, r1=r1: e.tensor_scalar(out=TMP, in0=T1[:, 0:64], scalar1=r1, scalar2=None, op0=ALU.mult),
                             reads=[PFb[3], r1b], writes=[tmpb])
                        P.op("dve", lambda e, T2=T2, r2=r2: e.scalar_tensor_tensor(out=OTMP, in0=T2[:, 0:64], scalar=r2, in1=TMP,
                                                                                  op0=ALU.mult, op1=ALU.add),
                             reads=[PFb[2], r2b, tmpb], writes=[otmpb])
                        P.op("dve", lambda e, ssq=ssq: e.scalar_tensor_tensor(out=TMP, in0=OTMP, scalar=1.0, in1=OTMP,
                                                                              op0=ALU.mult, op1=ALU.mult, accum_out=ssq),
                             reads=[otmpb, tmpb], writes=[tmpb, ssqb])
                        P.op("act", lambda e, ssq=ssq, std=std: e.activation(out=std, in_=ssq, func=AF.Sqrt, bias=epsT[:, 0:1], scale=1.0 / 64),
                             reads=[ssqb, cbuf], writes=[stdb])
                        P.op("dve", lambda e, std=std, rstd=rstd: e.reciprocal(out=rstd, in_=std), reads=[stdb], writes=[rstdb])
                        P.op("dve", lambda e, rstd=rstd, m=m: e.scalar_tensor_tensor(
                            out=MT[:, m, half * 64:(half + 1) * 64], in0=OTMP, scalar=rstd, in1=GSUB, op0=ALU.mult, op1=ALU.mult),
                            reads=[otmpb, rstdb, gsubb], writes=[mtb])

        def sb_head(hh, h):
            half = hh % 2
            sa = [0, 3, 4, 7, 8, 11, 12, 15]
            sbm = [1, 2, 5, 6, 9, 10, 13, 14]
            ua = [(m, u, 0) for m in sa for u in range(2 * (m + 1))]
            ub = [(m, u, 1) for m in sbm for u in range(2 * (m + 1))]
            assert len(ua) == len(ub)
            units = [x for pair_ in zip(ua, ub) for x in pair_]
            n = len(units)
            NR = 4
            ptcb = [Buf(f"ptc{i}") for i in range(NR)]

            def qk(t):
                m, u, st = units[t]
                bank = t % NR
                col0 = (120 - 8 * m) * 128 + 512 * u
                zone = u < 2
                P.op("pe", lambda e: e.matmul(PF[bank][:], lhsT=QT[0:64, m * 128:(m + 1) * 128], rhs=KT[0:64, col0:col0 + 512],
                                              start=True, stop=not zone),
                     reads=[qtb, ktb[col0 // 4096]], writes=[PFb[bank]], sig=not zone)
                if zone:
                    P.op("pe", lambda e: e.matmul(PF[bank][:], lhsT=identb[:], rhs=MSB[:, u * 512:(u + 1) * 512], start=False, stop=True),
                         reads=[mskb, cbuf], writes=[PFb[bank]])

            def sg(t):
                bank = t % NR
                P.op("act", lambda e: e.activation(out=SS[t % NR], in_=PF[bank][:], func=AF.Sigmoid, scale=-1.0),
                     reads=[PFb[bank]], writes=[ssb[t % NR]])

            def scan(t):
                m, u, st = units[t]
                cur, prv = CP[t % NR], CP[(t - 2) % NR]
                if u == 0:
                    init = 1.0
                    rd = [ssb[t % NR]]
                else:
                    init = prv[:, 512:513]
                    rd = [ssb[t % NR], cpb[(t - 2) % NR]]
                P.op("dve", lambda e: e.tensor_tensor_scan(out=cur[:, 1:513], data0=SS[t % NR], data1=ZB[:, 0:512], initial=init,
                                                           op0=ALU.mult, op1=ALU.add),
                     reads=rd + [zerob], writes=[cpb[t % NR]])
                P.op("pool", lambda e: e.tensor_tensor(out=PT[t % NR][:, 1:512], in0=cur[:, 1:512], in1=cur[:, 2:513], op=ALU.subtract),
                     reads=[cpb[t % NR]], writes=[ptb[t % NR]])
                if u == 0:
                    P.op("pool", lambda e: e.tensor_scalar(out=PT[t % NR][:, 0:1], in0=cur[:, 1:2], scalar1=-1.0, scalar2=1.0,
                                                           op0=ALU.mult, op1=ALU.add),
                         reads=[cpb[t % NR]], writes=[ptcb[t % NR]])
                else:
                    P.op("pool", lambda e: e.tensor_tensor(out=PT[t % NR][:, 0:1], in0=prv[:, 512:513], in1=cur[:, 1:2], op=ALU.subtract),
                         reads=[cpb[t % NR], cpb[(t - 2) % NR]], writes=[ptcb[t % NR]])

            def tr(t):
                pbi = t % 2
                for j in range(4):
                    P.op("pe", lambda e, j=j: e.transpose(out=PB[pbi][:, j * 128:(j + 1) * 128], in_=PT[t % NR][:, j * 128:(j + 1) * 128],
                                                          identity=identb[:]),
                         reads=[ptb[t % NR], ptcb[t % NR], cbuf], writes=[PBb[pbi]], sig=(j == 3))
                P.op("act", lambda e: e.activation(out=WT[t % NR], in_=PB[pbi][:, 0:512], func=AF.Copy),
                     reads=[PBb[pbi]], writes=[wtb[t % NR]])

            def pv(t):
                m, u, st = units[t]
                ob = 4 + st
                last_u = 2 * (m + 1) - 1
                slot0 = (120 - 8 * m) + 4 * u
                for j in range(4):
                    P.op("pe", lambda e, j=j: e.matmul(PF[ob][:, 0:64], lhsT=WT[t % NR][:, j * 128:(j + 1) * 128],
                                                       rhs=VV[:, slot0 + j, 0:64], start=(u == 0 and j == 0),
                                                       stop=(u == last_u and j == 3)),
                         reads=[wtb[t % NR], vvb], writes=[PFb[ob]], sig=(j == 3))
                if u == last_u:
                    P.op("dve", lambda e: e.tensor_copy(out=MT[:, m, half * 64:(half + 1) * 64], in_=PF[ob][:, 0:64]),
                         reads=[PFb[ob]], writes=[mtb])

            pipeline(n, [(0, qk), (0, sg), (0, scan), (2, tr), (3, pv)])

        def flush_pair(pair):
            for q4 in range(4):
                pbi = q4 % 2
                for j in range(4):
                    m = q4 * 4 + j
                    P.op("pe", lambda e, m=m, j=j, pbi=pbi: e.transpose(out=PB[pbi][:, j * 128:(j + 1) * 128], in_=MT[:, m, :], identity=identb[:]),
                         reads=[mtb, cbuf], writes=[PBb[pbi]], sig=(j == 3))
                P.op("dve", lambda e, q4=q4, pbi=pbi: e.tensor_copy(out=MIX3[:, pair, q4 * 512:(q4 + 1) * 512], in_=PB[pbi][:, 0:512]),
                     reads=[PBb[pbi]], writes=[mixb[pair]])

        for hh, (typ, h) in enumerate(HEADS):
            if head_sel is not None and hh not in head_sel:
                continue
            if hh == 10:
                P.barrier()
            load_head(hh, typ)
            import os
            dbg = os.environ.get("DBG_MODE", "")
            if dbg.startswith("loadonly"):
                P.op("dve", lambda e: e.tensor_copy(out=MT[:, 0, :], in_=KT[:, 0:128]), reads=ktb + [vvb, qtb], writes=[mtb])
            elif typ == "sb":
                sb_head(hh, h)
            else:
                softmax_head(hh, typ, h)
            if dbg.endswith("noflush"):
                continue
            if hh % 2 == 1 or head_sel is not None:
                flush_pair(hh // 2)
        return MIX3, mixb

    ZB = sb("ZB", [128, 512], BF16)
    zerob = Buf("zero")
    P.op("pool", lambda e: e.memset(ZB[:], 0.0), writes=[zerob])

    def out_proj(layer, MIX3, mixb):
        wo = WD[("wout", layer)]
        wov = wo.rearrange("(k p) n -> p k n", p=128)
        WO3 = vb(0, 8192).rearrange("p (k n) -> p k n", k=8)
        wob = [Buf("wo0"), Buf("wo1")]
        for i in range(2):
            P.dma("pool", WO3[:, i * 4:(i + 1) * 4, :], wov[:, i * 4:(i + 1) * 4, :], writes=[wob[i]], sembuf=wob[i])
        load_gain(1, layer, 3)
        for m in range(NB):
            for hf in range(2):
                bank = 4 + hf
                for k in range(8):
                    P.op("pe", lambda e, k=k, m=m, hf=hf, bank=bank: e.matmul(
                        PF[bank][:], lhsT=MIX3[:, k, m * 128:(m + 1) * 128], rhs=WO3[:, k, hf * 512:(hf + 1) * 512],
                        start=(k == 0), stop=(k == 7)),
                        reads=[mixb[k], wob[k // 4]], writes=[PFb[bank]], sig=(k == 7))
            postnorm_residual(m, (4, 5), 1, 1.0)

    for (kind, layer) in stages:
        if kind == "A":
            ffn(layer, 0, 0, 1)
            P.barrier()
            proj(layer)
            P.barrier()
        else:
            MIX3, mixb = attention_body(layer)
            P.barrier()
            if debug_mix:
                dbg = dout("dbg_mix", [128, 8 * TOK], BF16)
                P.dma("sp", dbg.rearrange("p (k t) -> p k t", k=8), MIX3, reads=mixb, sembuf=Buf("dbg"), final=True)
            if attn_only:
                continue
            out_proj(layer, MIX3, mixb)
            P.barrier()
            ffn(layer, 1, 4, 5)
            P.barrier()
    xo = x_out.rearrange("(m p) d -> p m d", p=128)
    xst = Buf("xstore")
    for m in range(NB):
        P.dma("sp", xo[:, m, :], X[:, m * D:(m + 1) * D], reads=[xb[m]], sembuf=xst, final=True)
    P.run()
    es.close()
    return nc


_SHARED = None


def _tok_index(c):
    m = np.arange(NB)[:, None]
    j = np.arange(128)[None, :]
    return ((8 * m + c) * 128 + 127 - j).reshape(-1)


def _launch(stages, per_core_extra, weights, debug_mix=False, head_sel=None, attn_only=False):
    global _SHARED
    if _SHARED is None:
        _SHARED = shared_constants()
    nc = build_program(stages, debug_mix=debug_mix, head_sel=head_sel, attn_only=attn_only)
    in_maps = []
    for c in range(NCORES):
        mp = dict(_SHARED)
        mp.update(core_constants(c))
        mp.update(weights)
        mp.update(per_core_extra[c])
        in_maps.append(mp)
    res = run_bass_kernel_spmd(nc, in_maps, core_ids=list(range(NCORES)))
    return res.results


def _weights_for(stages, inp):
    w = {}
    for (kind, l) in stages:
        j = 0 if kind == "A" else 1
        w[f"gate{l}_{j}"] = np.ascontiguousarray(inp["w_ffn_gate"][l, j])
        w[f"up{l}_{j}"] = np.ascontiguousarray(inp["w_ffn_up"][l, j])
        w[f"down{l}_{j}"] = np.ascontiguousarray(inp["w_ffn_down"][l, j])
        w[f"gains{l}"] = np.ascontiguousarray(inp["norm_gains"][l])
        if kind == "A":
            w[f"win{l}"] = np.ascontiguousarray(inp["w_in"][l])
        else:
            w[f"wout{l}"] = np.ascontiguousarray(inp["w_out"][l])
            w[f"dlam{l}"] = np.ascontiguousarray(inp["diff_lambda"][l].reshape(1, 128))
            w[f"dgain{l}"] = np.ascontiguousarray(inp["diff_subln_gain"][l].reshape(1, 64))
    return w


def _gather_kv(results):
    kt_all = np.zeros((16, 128, S), NPBF)
    v_all = np.zeros((16, 128, NBLK, 65), NPBF)
    for c in range(NCORES):
        kt = results[c]["kt_out"].reshape(16, 128, NB, 128)
        vv = results[c]["v_out"].reshape(16, 128, NB, 65)
        for m in range(NB):
            slot = 127 - (8 * m + c)
            kt_all[:, :, slot * 128:(slot + 1) * 128] = kt[:, :, m, :]
            v_all[:, :, slot, :] = vv[:, :, m, :]
    return kt_all, v_all.reshape(16, 128, NBLK * 65)


def kernel(x, norm_gains, w_ffn_gate, w_ffn_up, w_ffn_down, w_in, w_out, diff_lambda, diff_subln_gain):
    inp = dict(x=np.asarray(x), norm_gains=np.asarray(norm_gains), w_ffn_gate=np.asarray(w_ffn_gate),
               w_ffn_up=np.asarray(w_ffn_up), w_ffn_down=np.asarray(w_ffn_down), w_in=np.asarray(w_in),
               w_out=np.asarray(w_out), diff_lambda=np.asarray(diff_lambda), diff_subln_gain=np.asarray(diff_subln_gain))
    x2 = inp["x"].reshape(S, D)
    idx = [_tok_index(c) for c in range(NCORES)]
    xs = [np.ascontiguousarray(x2[idx[c]]) for c in range(NCORES)]
    st = [("A", 0)]
    r = _launch(st, [{"x_in": xs[c]} for c in range(NCORES)], _weights_for(st, inp))
    for l in range(DEPTH):
        kt_all, v_all = _gather_kv(r)
        st = [("B", l)] + ([("A", l + 1)] if l + 1 < DEPTH else [])
        extra = [{"x_in": r[c]["x_out"], "kt_all": kt_all, "v_all": v_all, "qt_in": r[c]["qt_out"]} for c in range(NCORES)]
        r = _launch(st, extra, _weights_for(st, inp))
    out = np.zeros((S, D), np.float32)
    for c in range(NCORES):
        out[idx[c]] = r[c]["x_out"]
    return out.reshape(1, S, D)
```

```python
import math
import contextlib
import numpy as np
import ml_dtypes
import concourse.bass as bass
import concourse.mybir as mybir
from concourse.bass_utils import run_bass_kernel_spmd

F32 = mybir.dt.float32
BF16 = mybir.dt.bfloat16
AF = mybir.ActivationFunctionType
ALU = mybir.AluOpType
NPBF = ml_dtypes.bfloat16

NCORES = 8
D = 1024
DFF = 2816
NF = DFF // 128
S = 16384
TOK = S // NCORES
NB = TOK // 128
NBLK = S // 128
DEPTH = 2
EPS = 1e-6
NEG = -30000.0
THRESH = 120.0
HEADS = ([("dil", h) for h in range(6)] + [("diff", h) for h in range(4)] + [("sb", h) for h in range(6)])
QCOL = {"dil": 0, "diff": 1152, "sb": 1920}
KCOL = {"dil": 384, "diff": 1408, "sb": 2304}
VCOL = {"dil": 768, "diff": 1664, "sb": 2688}
KROWS = {"dil": 96, "diff": 112, "sb": 64}


def alibi(n):
    return [2.0 ** (-8.0 * (i + 1) / n) for i in range(n)]


SLOPE = {("dil", h): alibi(6)[h] for h in range(6)}
SLOPE.update({("diff", h): alibi(4)[h] for h in range(4)})
SLOPE_LIST = [("dil", h) for h in range(6)] + [("diff", h) for h in range(4)]


class Buf:
    __slots__ = ("name", "writer", "readers", "sem", "semcnt")

    def __init__(self, name):
        self.name = name
        self.writer = None
        self.readers = {}
        self.sem = None
        self.semcnt = 0


ENGS = ("pe", "act", "dve", "pool", "sp")


class Prog:
    def __init__(self, nc, es):
        self.nc = nc
        self.es = es
        self.q = {e: [] for e in ENGS}
        self.esem = {e: es.enter_context(nc.semaphore("S_" + e)) for e in ("pe", "act", "dve", "pool")}
        self.ecnt = {e: 0 for e in self.esem}
        self.waited = {e: {} for e in ENGS}
        self.pending = {e: [] for e in ENGS}
        self.finals = []
        self.nsem = 4
        self.dma_tks = {}

    def _wait(self, e, tk):
        if tk is None:
            return
        sem, val, src = tk
        if sem is None:
            assert src == e, "cross-engine wait on unsignaled op"
            return
        if src == "pe" and e == "pe":
            return
        key = id(sem)
        if self.waited[e].get(key, 0) >= val:
            return
        self.waited[e][key] = val
        self.q[e].append(("w", sem, val))

    def _deps(self, e, reads, writes):
        for b in reads:
            self._wait(e, b.writer)
        for b in writes:
            self._wait(e, b.writer)
            for r in b.readers.values():
                self._wait(e, r)

    def _commit(self, tk, reads, writes):
        for b in reads:
            if b not in writes:
                b.readers[id(tk[0])] = tk
        for b in writes:
            b.writer = tk
            b.readers = {}

    def op(self, e, fn, reads=(), writes=(), sig=True):
        self._deps(e, reads, writes)
        if sig:
            self.ecnt[e] += 1
            tk = (self.esem[e], self.ecnt[e], e)
            self.q[e].append(("o", fn, self.esem[e], 1))
            for (rs, ws) in self.pending[e]:
                self._commit(tk, rs, ws)
            self.pending[e] = []
            self._commit(tk, reads, writes)
        else:
            assert e == "pe"
            self.q[e].append(("o", fn, None, 0))
            self.pending[e].append((tuple(reads), tuple(writes)))
            for b in writes:
                b.writer = (None, 0, e)
                b.readers = {}

    def dma(self, e, out_ap, in_ap, reads=(), writes=(), sembuf=None, final=False):
        self._deps(e, reads, writes)
        sb = sembuf
        if sb.sem is None:
            sb.sem = self.es.enter_context(self.nc.semaphore("D%d_%s" % (self.nsem, sb.name)))
            self.nsem += 1
        sb.semcnt += 16
        tk = (sb.sem, sb.semcnt, "dma")
        self.q[e].append(("d", out_ap, in_ap, sb.sem))
        self.dma_tks[id(sb.sem)] = tk
        self._commit(tk, reads, writes)
        if final:
            self.finals.append(tk)
        return tk

    def barrier(self):
        for b_e in ENGS:
            assert not self.pending[b_e], "dangling unsignaled ops on " + b_e
        tks = [(self.esem[e], self.ecnt[e], e) for e in self.esem if self.ecnt[e] > 0]
        tks += list(self.dma_tks.values())
        for e in ENGS:
            for tk in tks:
                self._wait(e, tk)

    def run(self):
        nc = self.nc
        for b_e in ENGS:
            assert not self.pending[b_e], "dangling unsignaled ops on " + b_e
        for tk in self.finals:
            self._wait("sp", tk)

        def replay(eng, items):
            for it in items:
                if it[0] == "w":
                    eng.wait_ge(it[1], it[2])
                elif it[0] == "o":
                    ins = it[1](eng)
                    if it[2] is not None:
                        ins.then_inc(it[2], it[3])
                else:
                    eng.dma_start(out=it[1], in_=it[2]).then_inc(it[3], 16)

        with nc.Block() as block:
            @block.tensor
            def _(eng):
                replay(eng, self.q["pe"])

            @block.scalar
            def _(eng):
                replay(eng, self.q["act"])

            @block.vector
            def _(eng):
                replay(eng, self.q["dve"])

            @block.gpsimd
            def _(eng):
                replay(eng, self.q["pool"])

            @block.sync
            def _(eng):
                replay(eng, self.q["sp"])


def pipeline(n, stages):
    maxlag = max(l for l, _ in stages)
    for t in range(n + maxlag):
        for lag, fn in stages:
            u = t - lag
            if 0 <= u < n:
                fn(u)


def _bf(x):
    return np.asarray(x, dtype=np.float32).astype(NPBF)


def core_constants(c):
    p = np.arange(128)[:, None].astype(np.int64)
    col = np.arange(128)[None, :].astype(np.int64)
    out = {}
    msb = np.zeros((128, 8, 128), np.float32)
    for a in range(8):
        r = 7 - a
        if r > c:
            msb[:, a, :] = NEG
        elif r == c:
            msb[:, a, :] = np.where(col > p, 0.0, NEG)
    out["c_msb"] = _bf(msb.reshape(128, 1024))
    mc = np.zeros((128, 8, 128), np.float32)
    for r in range(8):
        if r > c:
            mc[:, r, :] = NEG
        elif r == c:
            mc[:, r, :] = np.where(p >= col, 0.0, NEG)
    out["c_mc"] = _bf(mc.reshape(128, 1024))
    td = np.full((128, 24, 128), NEG, np.float32)
    for idx in range(24):
        Dd = idx - 7 + c
        if 0 <= Dd <= 16:
            delta = 128 * Dd - col + p
            mult = ((delta >= 0) & (delta <= 128)).astype(np.int64)
            mult = mult + ((delta >= 0) & (delta <= 512) & (delta % 4 == 0))
            mult = mult + ((delta >= 0) & (delta <= 2048) & (delta % 16 == 0))
            td[:, idx, :] = np.where(mult > 0, np.log(np.maximum(mult, 1)), NEG)
    out["c_td"] = _bf(td.reshape(128, 24 * 128))
    ds = np.arange(159)[None, :]
    out["c_base"] = (128.0 * (ds - 31 + c) + p).astype(np.float32)
    return out


def shared_constants():
    out = {}
    out["c_identb"] = _bf(np.eye(128))
    out["c_identf"] = np.eye(128, dtype=np.float32)
    e_dil = np.zeros((8, 128), np.float32)
    for r in range(4):
        e_dil[r, 64 + r] = 1.0
    e_diff = np.zeros((8, 128), np.float32)
    for r in range(4):
        e_diff[r, 32 + r] = 1.0
        e_diff[4 + r, 96 + r] = 1.0
    out["c_edil"] = _bf(e_dil)
    out["c_ediff"] = _bf(e_diff)
    colq = np.arange(512)
    jq = (colq % 128).astype(np.float32)
    ii = (colq // 128).astype(np.float32)
    qa = np.stack([jq, jq, -1024.0 * ii, -1024.0 * ii] * 2, 0)
    out["c_qaug"] = _bf(qa)
    ks = np.zeros((8, 10, 128), np.float32)
    for si, key in enumerate(SLOPE_LIST):
        s = np.float32(SLOPE[key])
        hi = np.float32(s.astype(NPBF))
        lo = np.float32(np.float32(s - hi).astype(NPBF))
        sel = e_dil if key[0] == "dil" else e_diff
        for r, v in enumerate([hi, lo, hi, lo] * 2):
            ks[r, si, :] = sel[r] * v
    out["c_ksel"] = _bf(ks.reshape(8, 1280))
    out["c_ones8"] = _bf(np.ones((8, 512)))
    scl = np.ones((128, 2), np.float32)
    scl[0:64, 0] = 64.0 ** -0.5
    scl[0:32, 1] = 32.0 ** -0.5
    scl[64:96, 1] = 32.0 ** -0.5
    out["c_scl"] = scl
    return out


CONST_SPECS = {
    "c_msb": ([128, 1024], BF16), "c_mc": ([128, 1024], BF16), "c_td": ([128, 3072], BF16),
    "c_base": ([128, 159], F32), "c_identb": ([128, 128], BF16), "c_identf": ([128, 128], F32),
    "c_edil": ([8, 128], BF16), "c_ediff": ([8, 128], BF16), "c_qaug": ([8, 512], BF16),
    "c_ksel": ([8, 1280], BF16), "c_ones8": ([8, 512], BF16), "c_scl": ([128, 2], F32),
}


def build_program(stages, debug_mix=False, head_sel=None, attn_only=False):
    nc = bass.Bass("TRN2", target_bir_lowering=False)
    es = contextlib.ExitStack()
    P = Prog(nc, es)
    has_b = [l for (k, l) in stages if k == "B"]
    has_a = [l for (k, l) in stages if k == "A"]
    last_is_a = stages[-1][0] == "A"

    def din(name, shape, dt=F32):
        return nc.dram_tensor(name, shape, dt, kind="ExternalInput").ap()

    def dout(name, shape, dt=F32):
        return nc.dram_tensor(name, shape, dt, kind="ExternalOutput").ap()

    x_in = din("x_in", [TOK, D])
    x_out = dout("x_out", [TOK, D])
    consts = {k: din(k, shp, dt) for k, (shp, dt) in CONST_SPECS.items()}
    WD = {}
    for l in has_a:
        WD[("gate", l, 0)] = din(f"gate{l}_0", [D, DFF])
        WD[("up", l, 0)] = din(f"up{l}_0", [D, DFF])
        WD[("down", l, 0)] = din(f"down{l}_0", [DFF, D])
        WD[("win", l)] = din(f"win{l}", [D, 3072])
    for l in has_b:
        WD[("gate", l, 1)] = din(f"gate{l}_1", [D, DFF])
        WD[("up", l, 1)] = din(f"up{l}_1", [D, DFF])
        WD[("down", l, 1)] = din(f"down{l}_1", [DFF, D])
        WD[("wout", l)] = din(f"wout{l}", [D, D])
        WD[("dlam", l)] = din(f"dlam{l}", [1, 128])
        WD[("dgain", l)] = din(f"dgain{l}", [1, 64])
    gains = {l: din(f"gains{l}", [6, D]) for l in sorted(set(has_a + has_b))}
    if has_b:
        kt_all = din("kt_all", [16, 128, S], BF16)
        v_all = din("v_all", [16, 128, NBLK * 65], BF16)
        qt_in = din("qt_in", [16, 128, TOK], BF16)
    if last_is_a:
        qt_out = dout("qt_out", [16, 128, TOK], BF16)
        kt_out = dout("kt_out", [16, 128, TOK], BF16)
        v_out = dout("v_out", [16, 128, NB * 65], BF16)

    def sb(name, shape, dt):
        return es.enter_context(nc.sbuf_tensor(name, shape, dt))

    def ps(name, shape, dt):
        return es.enter_context(nc.psum_tensor(name, shape, dt))

    X = sb("X", [128, NB * D], F32)
    xb = [Buf(f"x{m}") for m in range(NB)]
    ARB = sb("ARB", [128, 59392], BF16)
    ARF = sb("ARF", [128, 4112], F32)
    identb = sb("identb", [128, 128], BF16)
    identf = sb("identf", [128, 128], F32)
    edil = sb("edil", [8, 128], BF16)
    ediff = sb("ediff", [8, 128], BF16)
    qaug = sb("qaug", [8, 512], BF16)
    ksel = sb("ksel", [8, 1280], BF16)
    ones8 = sb("ones8", [8, 512], BF16)
    scl = sb("scl", [128, 2], F32)
    base = sb("base", [128, 159], F32)
    epsT = sb("epsT", [128, 1], F32)
    small = sb("small", [128, 64], F32)
    cbuf = Buf("consts")
    PF = [ps(f"pf{i}", [128, 512], F32) for i in range(6)]
    PFb = [Buf(f"pf{i}") for i in range(6)]
    PB = [ps(f"pb{i}", [128, 1024], BF16) for i in range(2)]
    PBb = [Buf(f"pb{i}") for i in range(2)]

    def vb(off, n):
        return ARB[:, off:off + n]

    def vf(off, n):
        return ARF[:, off:off + n]

    xsem = Buf("xload")
    xv = x_in.rearrange("(m p) d -> p m d", p=128)
    for m in range(NB):
        P.dma("sp", X[:, m * D:(m + 1) * D], xv[:, m, :], writes=[xb[m]], sembuf=xsem)
    tkx = (xsem.sem, xsem.semcnt, "dma")
    for m in range(NB):
        xb[m].writer = tkx
    for (tile, key) in ((identb, "c_identb"), (identf, "c_identf"), (edil, "c_edil"), (ediff, "c_ediff"),
                        (qaug, "c_qaug"), (ksel, "c_ksel"), (ones8, "c_ones8"), (scl, "c_scl"), (base, "c_base")):
        P.dma("sp", tile[:], consts[key], writes=[], sembuf=cbuf)
    cbuf.writer = (cbuf.sem, cbuf.semcnt, "dma")
    P.op("pool", lambda e: e.memset(epsT[:], EPS), writes=[cbuf])

    GREP = [vf(0, 1024), vf(1024, 1024)]
    GREPb = [Buf("grep0"), Buf("grep1")]
    TT = vf(2048, 1024)
    TTb = Buf("tt")

    def load_gain(slot, layer, idx):
        P.dma("sp", GREP[slot], bcast_rows(gains[layer], idx, D), writes=[GREPb[slot]], sembuf=GREPb[slot])

    def bcast_rows(ap2d, row, n):
        r = ap2d[row:row + 1, 0:n]
        return bass.AP(tensor=r.tensor, offset=r.offset, ap=[[0, 128], [1, n]])

    sc_ctr = [0]

    def sc_col():
        i = sc_ctr[0] % 48
        sc_ctr[0] += 1
        return small[:, i:i + 1], SCb[i]

    SCb = [Buf(f"sc{i}") for i in range(64)]

    def prenorm_block(m, gslot, hn_ap, hnb):
        xm = X[:, m * D:(m + 1) * D]
        ssq, ssqb = sc_col()
        std, stdb = sc_col()
        rstd, rstdb = sc_col()
        P.op("act", lambda e: e.activation(out=hn_ap, in_=xm, func=AF.Square, accum_out=ssq),
             reads=[xb[m]], writes=[ssqb, hnb])
        P.op("act", lambda e: e.activation(out=std, in_=ssq, func=AF.Sqrt, bias=epsT[:, 0:1], scale=1.0 / D),
             reads=[ssqb, cbuf], writes=[stdb])
        P.op("dve", lambda e: e.reciprocal(out=rstd, in_=std), reads=[stdb], writes=[rstdb])
        P.op("dve", lambda e: e.scalar_tensor_tensor(out=hn_ap, in0=xm, scalar=rstd, in1=GREP[gslot],
                                                     op0=ALU.mult, op1=ALU.mult),
             reads=[xb[m], rstdb, GREPb[gslot]], writes=[hnb])

    def transpose_block(hn_ap, hnb, hT3, hTb, tokcol, pbi):
        pb = PB[pbi]
        for k in range(8):
            P.op("pe", lambda e, k=k: e.transpose(out=pb[:, k * 128:(k + 1) * 128], in_=hn_ap[:, k * 128:(k + 1) * 128],
                                                  identity=identb[:]),
                 reads=[hnb, cbuf], writes=[PBb[pbi]], sig=(k == 7))
        P.op("act", lambda e: e.activation(out=hT3[:, :, tokcol:tokcol + 128],
                                           in_=pb[:, 0:1024].rearrange("p (k t) -> p k t", k=8), func=AF.Copy),
             reads=[PBb[pbi]], writes=[hTb])

    def postnorm_residual(m, ybanks, gslot, coef):
        xm = X[:, m * D:(m + 1) * D]
        s0, s0b = sc_col()
        s1, s1b = sc_col()
        ssq, ssqb = sc_col()
        std, stdb = sc_col()
        rstd, rstdb = sc_col()
        for (sc, scb, bi) in ((s0, s0b, ybanks[0]), (s1, s1b, ybanks[1])):
            P.op("act", lambda e, sc=sc, bi=bi: e.activation(out=JUNK2[0], in_=PF[bi][:], func=AF.Square, accum_out=sc),
                 reads=[PFb[bi]], writes=[scb, JUNK2b[0]])
        P.op("dve", lambda e: e.tensor_tensor(out=ssq, in0=s0, in1=s1, op=ALU.add), reads=[s0b, s1b], writes=[ssqb])
        P.op("act", lambda e: e.activation(out=std, in_=ssq, func=AF.Sqrt, bias=epsT[:, 0:1], scale=1.0 / D),
             reads=[ssqb, cbuf], writes=[stdb])
        P.op("dve", lambda e: e.reciprocal(out=rstd, in_=std), reads=[stdb], writes=[rstdb])
        for hf in range(2):
            bi = ybanks[hf]
            P.op("dve", lambda e, bi=bi, hf=hf: e.scalar_tensor_tensor(
                out=TT[:, hf * 512:(hf + 1) * 512], in0=PF[bi][:], scalar=float(coef),
                in1=GREP[gslot][:, hf * 512:(hf + 1) * 512], op0=ALU.mult, op1=ALU.mult),
                reads=[PFb[bi], GREPb[gslot]], writes=[TTb])
        P.op("dve", lambda e: e.scalar_tensor_tensor(out=xm, in0=TT[:], scalar=rstd, in1=xm, op0=ALU.mult, op1=ALU.add),
             reads=[TTb, rstdb, xb[m]], writes=[xb[m]])

    def ffn(layer, which, g_pre, g_post):
        wg, wu, wdn = WD[("gate", layer, which)], WD[("up", layer, which)], WD[("down", layer, which)]
        ACT3 = vb(0, 22528).rearrange("p (f t) -> p f t", f=NF)
        WDN = vb(22528, 22528).rearrange("p (h f c) -> p h f c", h=2, f=NF)
        HT3 = vb(45056, 8192).rearrange("p (k t) -> p k t", k=8)
        WGU = vb(53248, 4096).rearrange("p (s w k n) -> p s w k n", s=2, w=2, k=8)
        HN = [vb(57344, 1024), vb(58368, 1024)]
        actb = [Buf(f"act{f}") for f in range(NF)]
        wdb = [[Buf(f"wd{h}_{i}") for i in range(2)] for h in range(2)]
        htb = Buf("ht")
        wgub = [[Buf(f"wgu{s}_{w}") for w in range(2)] for s in range(2)]
        hnb = [Buf("hn0"), Buf("hn1")]
        wgv = wg.rearrange("(k p) n -> p k n", p=128)
        wuv = wu.rearrange("(k p) n -> p k n", p=128)
        wdv = wdn.rearrange("(f p) c -> p f c", p=128)
        load_gain(0, layer, g_pre)
        load_gain(1, layer, g_post)

        def load_gu(f):
            s = f % 2
            P.dma("pool", WGU[:, s, 0], wgv[:, :, f * 128:(f + 1) * 128], writes=[wgub[s][0]], sembuf=wgub[s][0])
            P.dma("pool", WGU[:, s, 1], wuv[:, :, f * 128:(f + 1) * 128], writes=[wgub[s][1]], sembuf=wgub[s][1])

        def load_wd(h):
            for i in range(2):
                P.dma("pool", WDN[:, h, i * 11:(i + 1) * 11, :], wdv[:, i * 11:(i + 1) * 11, h * 512:(h + 1) * 512],
                      writes=[wdb[h][i]], sembuf=wdb[h][i])

        for tg in range(2):
            for j in range(8):
                m = tg * 8 + j
                prenorm_block(m, 0, HN[j % 2], hnb[j % 2])
                transpose_block(HN[j % 2], hnb[j % 2], HT3, htb, j * 128, j % 2)
            load_gu(0)
            load_gu(1)
            load_wd(0)
            load_wd(1)
            for f in range(NF):
                s = f % 2
                for hf in range(2):
                    gb, ub = (0, 1) if hf == 0 else (2, 3)
                    for (w, bank) in ((0, gb), (1, ub)):
                        for k in range(8):
                            P.op("pe", lambda e, w=w, bank=bank, k=k, s=s, hf=hf: e.matmul(
                                PF[bank][:], lhsT=WGU[:, s, w, k, :], rhs=HT3[:, k, hf * 512:(hf + 1) * 512],
                                start=(k == 0), stop=(k == 7)),
                                reads=[wgub[s][w], htb], writes=[PFb[bank]], sig=(k == 7))
                    SG = JUNK2[hf]
                    P.op("act", lambda e, gb=gb, SG=SG: e.activation(out=SG, in_=PF[gb][:], func=AF.Silu),
                         reads=[PFb[gb]], writes=[JUNK2b[hf]])
                    P.op("dve", lambda e, ub=ub, SG=SG, f=f, hf=hf: e.tensor_tensor(
                        out=ACT3[:, f, hf * 512:(hf + 1) * 512], in0=SG, in1=PF[ub][:], op=ALU.mult),
                        reads=[JUNK2b[hf], PFb[ub]], writes=[actb[f]])
                if f + 2 < NF:
                    load_gu(f + 2)
            for j in range(8):
                m = tg * 8 + j
                for h in range(2):
                    bank = 4 + h
                    for f in range(NF):
                        P.op("pe", lambda e, f=f, h=h, bank=bank, j=j: e.matmul(
                            PF[bank][:], lhsT=ACT3[:, f, j * 128:(j + 1) * 128], rhs=WDN[:, h, f, :],
                            start=(f == 0), stop=(f == NF - 1)),
                            reads=[actb[f], wdb[h][f // 11]], writes=[PFb[bank]], sig=(f == NF - 1))
                postnorm_residual(m, (4, 5), 1, 0.5)

    JUNK2 = [sb("SG0", [128, 512], BF16)[:], sb("SG1", [128, 512], BF16)[:]]
    JUNK2b = [Buf("sg0"), Buf("sg1")]

    def proj(layer):
        win = WD[("win", layer)]
        winv = win.rearrange("(k p) n -> p k n", p=128)
        HT3 = vb(0, 16384).rearrange("p (k t) -> p k t", k=8)
        WV = vb(16384, 8192).rearrange("p (k n) -> p k n", k=8)
        WST = vb(24576, 4096).rearrange("p (s k n) -> p s k n", s=4, k=8)
        HN = [vb(28672, 1024), vb(29696, 1024)]
        QST = [vb(30720, 512), vb(31232, 512)]
        VST = [vb(31744, 520).rearrange("p (h e) -> p h e", h=8), vb(32264, 520).rearrange("p (h e) -> p h e", h=8)]
        htb = Buf("pht")
        wvb = [Buf(f"wv{i}") for i in range(3)]
        wstb = [Buf(f"wst{i}") for i in range(4)]
        hnb = [Buf("phn0"), Buf("phn1")]
        qstb = [Buf("qst0"), Buf("qst1")]
        vstb = [Buf("vst0"), Buf("vst1")]
        load_gain(0, layer, 2)
        for s4 in range(4):
            P.op("pool", lambda e, s4=s4: e.memset(WST[:, s4], 0.0), writes=[wstb[s4]])
        for i in range(2):
            P.op("pool", lambda e, i=i: e.memset(VST[i], 1.0), writes=[vstb[i]])
        for i, (c0, n, d0) in enumerate(((768, 384, 0), (1664, 256, 384), (2688, 384, 640))):
            P.dma("pool", WV[:, :, d0:d0 + n], winv[:, :, c0:c0 + n], writes=[wvb[i]], sembuf=wvb[i])
        for m in range(NB):
            prenorm_block(m, 0, HN[m % 2], hnb[m % 2])
            transpose_block(HN[m % 2], hnb[m % 2], HT3, htb, m * 128, m % 2)
        cnt = [0, 0]
        qi = 0
        ring = 0
        for hh, (typ, h) in enumerate(HEADS):
            for which in ("q", "k"):
                colb = (QCOL if which == "q" else KCOL)[typ] + 64 * h
                if typ == "diff":
                    s4 = 2 + cnt[1] % 2
                    cnt[1] += 1
                    P.dma("pool", WST[:, s4, :, 0:32], winv[:, :, colb:colb + 32], writes=[wstb[s4]], sembuf=wstb[s4])
                    P.dma("pool", WST[:, s4, :, 64:96], winv[:, :, colb + 32:colb + 64], writes=[wstb[s4]], sembuf=wstb[s4])
                else:
                    s4 = cnt[0] % 2
                    cnt[0] += 1
                    P.dma("pool", WST[:, s4, :, 0:64], winv[:, :, colb:colb + 64], writes=[wstb[s4]], sembuf=wstb[s4])
                for g in range(4):
                    bank = ring % 4
                    ring += 1
                    aug = typ != "sb"
                    for k in range(8):
                        P.op("pe", lambda e, k=k, s4=s4, g=g, bank=bank, aug=aug: e.matmul(
                            PF[bank][:], lhsT=WST[:, s4, k, :], rhs=HT3[:, k, g * 512:(g + 1) * 512],
                            start=(k == 0), stop=(k == 7 and not aug)),
                            reads=[wstb[s4], htb], writes=[PFb[bank]], sig=(k == 7 and not aug))
                    if aug:
                        if which == "q":
                            sel = (edil if typ == "dil" else ediff)[:, :]
                            src = qaug[:, :]
                        else:
                            si = SLOPE_LIST.index((typ, h))
                            sel = ksel[:, si * 128:(si + 1) * 128]
                            src = ones8[:, :]
                        P.op("pe", lambda e, sel=sel, src=src, bank=bank: e.matmul(
                            PF[bank][:], lhsT=sel, rhs=src, start=False, stop=True),
                            reads=[cbuf], writes=[PFb[bank]])
                    st = qi % 2
                    qi += 1
                    if which == "q":
                        sc_i = 1 if typ == "diff" else 0
                        P.op("dve", lambda e, bank=bank, st=st, sc_i=sc_i: e.tensor_scalar(
                            out=QST[st], in0=PF[bank][:], scalar1=scl[:, sc_i:sc_i + 1], scalar2=None, op0=ALU.mult),
                            reads=[PFb[bank], cbuf], writes=[qstb[st]])
                        dst = qt_out[hh, :, g * 512:(g + 1) * 512]
                    else:
                        P.op("act", lambda e, bank=bank, st=st: e.activation(out=QST[st], in_=PF[bank][:], func=AF.Copy),
                             reads=[PFb[bank]], writes=[qstb[st]])
                        dst = kt_out[hh, :, g * 512:(g + 1) * 512]
                    P.dma("sp", dst, QST[st], reads=[qstb[st]], sembuf=qstb[st], final=True)
        for hf in range(2):
            for m in range(NB):
                bank = ring % 4
                ring += 1
                for k in range(8):
                    P.op("pe", lambda e, k=k, m=m, hf=hf, bank=bank: e.matmul(
                        PF[bank][:], lhsT=HT3[:, k, m * 128:(m + 1) * 128], rhs=WV[:, k, hf * 512:(hf + 1) * 512],
                        start=(k == 0), stop=(k == 7)),
                        reads=[htb, wvb[0], wvb[1], wvb[2]], writes=[PFb[bank]], sig=(k == 7))
                st = m % 2
                P.op("act", lambda e, bank=bank, st=st: e.activation(
                    out=VST[st][:, :, 0:64], in_=PF[bank][:].rearrange("p (h e) -> p h e", h=8), func=AF.Copy),
                    reads=[PFb[bank]], writes=[vstb[st]])
                dst = v_out[hf * 8:(hf + 1) * 8, :, m * 65:(m + 1) * 65].rearrange("h p e -> p h e")
                P.dma("sp", dst, VST[st], reads=[vstb[st]], sembuf=vstb[st], final=True)

    def attention_body(layer):
        lam_init = 0.8 - 0.6 * math.exp(-0.3 * layer)
        KT = vb(0, 16384)
        VV = vb(16384, 8320).rearrange("p (s e) -> p s e", e=65)
        QT = vb(24704, 2048)
        MIX3 = vb(26752, 16384).rearrange("p (k t) -> p k t", k=8)
        MSB = vb(43136, 1024)
        MC = vb(44160, 1024)
        TD = vb(45184, 3072)
        PT = [vb(48256 + i * 512, 512) for i in range(4)]
        WT = [vb(50304 + i * 512, 512) for i in range(4)]
        MT = vb(52352, 2048).rearrange("p (m f) -> p m f", m=NB)
        ktb = [Buf(f"kt{i}") for i in range(4)]
        vvb = Buf("vv")
        qtb = Buf("qt")
        mixb = [Buf(f"mix{k}") for k in range(8)]
        mskb = Buf("masks")
        ptb = [Buf(f"pt{i}") for i in range(4)]
        wtb = [Buf(f"wt{i}") for i in range(4)]
        mtb = Buf("mt")
        SS = [vf(i * 512, 512) for i in range(4)]
        CP = [vf(2048 + i * 516, 516) for i in range(4)]
        BT = vf(0, 159)
        OAf = vf(160, 512)
        TMP = vf(672, 64)
        OTMP = vf(736, 64)
        GSUB = vf(800, 64)
        LAMT = vf(864, 128)
        ssb = [Buf(f"ss{i}") for i in range(4)]
        cpb = [Buf(f"cp{i}") for i in range(4)]
        btb = Buf("bt")
        oab = Buf("oa")
        tmpb = Buf("tmp")
        otmpb = Buf("otmp")
        gsubb = Buf("gsub")
        lamb = Buf("lam")
        lamneg, lamnegb = small[:, 60:61], Buf("lamneg")

        P.op("pool", lambda e: e.memset(vb(52352, 2048), 0.0), writes=[mtb])
        P.dma("sp", MSB, consts["c_msb"], writes=[mskb], sembuf=mskb)
        P.dma("sp", MC, consts["c_mc"], writes=[mskb], sembuf=mskb)
        P.dma("sp", TD, consts["c_td"], writes=[mskb], sembuf=mskb)
        P.dma("sp", LAMT, bcast_rows(WD[("dlam", layer)], 0, 128), writes=[lamb], sembuf=lamb)
        P.dma("sp", GSUB, bcast_rows(WD[("dgain", layer)], 0, 64), writes=[gsubb], sembuf=gsubb)
        s1, s1b = sc_col()
        s2, s2b = sc_col()
        P.op("dve", lambda e: e.scalar_tensor_tensor(out=TMP[:, 0:32], in0=LAMT[:, 0:32], scalar=1.0, in1=LAMT[:, 32:64],
                                                     op0=ALU.mult, op1=ALU.mult, accum_out=s1),
             reads=[lamb], writes=[tmpb, s1b])
        P.op("dve", lambda e: e.scalar_tensor_tensor(out=TMP[:, 0:32], in0=LAMT[:, 64:96], scalar=1.0, in1=LAMT[:, 96:128],
                                                     op0=ALU.mult, op1=ALU.mult, accum_out=s2),
             reads=[lamb, tmpb], writes=[tmpb, s2b])
        P.op("act", lambda e: e.activation(out=s1, in_=s1, func=AF.Exp), reads=[s1b], writes=[s1b])
        P.op("act", lambda e: e.activation(out=s2, in_=s2, func=AF.Exp), reads=[s2b], writes=[s2b])
        P.op("dve", lambda e: e.tensor_tensor(out=lamneg, in0=s2, in1=s1, op=ALU.subtract), reads=[s1b, s2b], writes=[lamnegb])
        P.op("dve", lambda e: e.tensor_scalar(out=lamneg, in0=lamneg, scalar1=-lam_init, scalar2=None, op0=ALU.add),
             reads=[lamnegb], writes=[lamnegb])
        P.op("dve", lambda e: e.tensor_scalar(out=GSUB, in0=GSUB, scalar1=float(1.0 - lam_init), scalar2=None, op0=ALU.mult),
             reads=[gsubb], writes=[gsubb])

        def load_head(hh, typ):
            R = KROWS[typ]
            for i in range(4):
                P.dma("sp", KT[0:R, i * 4096:(i + 1) * 4096], kt_all[hh, 0:R, i * 4096:(i + 1) * 4096],
                      writes=[ktb[i]], sembuf=ktb[i])
            P.dma("sp", VV.rearrange("p s e -> p (s e)"), v_all[hh], writes=[vvb], sembuf=vvb)
            P.dma("sp", QT[0:R, :], qt_in[hh, 0:R, :], writes=[qtb], sembuf=qtb)

        def kblk_buf(slot):
            return ktb[slot // 32]

        def softmax_head(hh, typ, h):
            slope = SLOPE[(typ, h)]
            half = hh % 2
            P.op("dve", lambda e: e.tensor_scalar(out=BT, in0=base[:, :], scalar1=float(-slope), scalar2=None, op0=ALU.mult),
                 reads=[cbuf], writes=[btb])
            maps = [(0, 68)] if typ == "dil" else [(0, 36), (64, 36)]
            nmap = len(maps)
            sbanks = [0, 1, 2] if typ == "dil" else [0, 1]
            for g in range(4):
                units = []
                for kb in range(0, 32 * g + 32):
                    e_ = kb - 32 * g
                    ilo = 0 if e_ < 0 else e_ // 8
                    ihi = -1
                    for i in range(ilo, 4):
                        dmin = 128 * (32 * g + 8 * i - kb) - 127
                        ok = slope * dmin <= THRESH
                        if typ == "dil":
                            dsv = 32 * g + 8 * i - kb
                            ok = ok and (-7 <= dsv <= 16)
                        if ok:
                            ihi = i
                    if typ == "dil":
                        while ilo <= ihi and not (-7 <= 32 * g + 8 * ilo - kb <= 16):
                            ilo += 1
                    if ihi >= ilo:
                        units.append((kb, ilo, ihi))
                n = len(units)
                seq = [(u, mi) for u in range(n) for mi in range(nmap)]
                ns = len(seq)
                for mi in range(nmap):
                    P.op("pe", lambda e, mi=mi: e.matmul(PF[4 + mi][0:65, :], lhsT=VV[:, 0, :], rhs=ZB[:, :], start=True, stop=False),
                         reads=[vvb, zerob], writes=[PFb[4 + mi]], sig=False)

                def qk(t, g=g, units=units, seq=seq):
                    u, mi = seq[t]
                    kb, ilo, ihi = units[u]
                    r0, nr = maps[mi]
                    bank = sbanks[t % len(sbanks)]
                    slot = 127 - kb
                    c0, c1 = ilo * 128, (ihi + 1) * 128
                    e_ = kb - 32 * g
                    need_mask = (typ == "dil") or (e_ >= 0)
                    P.op("pe", lambda e: e.matmul(PF[bank][:, c0:c1], lhsT=KT[r0:r0 + nr, slot * 128:(slot + 1) * 128],
                                                  rhs=QT[r0:r0 + nr, g * 512 + c0:g * 512 + c1], start=True, stop=not need_mask),
                         reads=[kblk_buf(slot), qtb], writes=[PFb[bank]], sig=not need_mask)
                    if need_mask:
                        if typ == "dil":
                            i0 = 32 * g + 8 * ilo - kb + 7
                            cnt = ihi - ilo + 1
                            t0 = TD[:, i0 * 128:(i0 + 1) * 128]
                            rhs = bass.AP(tensor=t0.tensor, offset=t0.offset, ap=[list(t0.ap[0]), [8 * 128, cnt], [1, 128]])
                            P.op("pe", lambda e: e.matmul(PF[bank][:, c0:c1], lhsT=identb[:], rhs=rhs, start=False, stop=True),
                                 reads=[mskb, cbuf], writes=[PFb[bank]])
                        else:
                            r = e_ % 8
                            P.op("pe", lambda e: e.matmul(PF[bank][:, c0:c0 + 128], lhsT=identb[:], rhs=MC[:, r * 128:(r + 1) * 128],
                                                          start=False, stop=True),
                                 reads=[mskb, cbuf], writes=[PFb[bank]])

                def ex(t, g=g, units=units, seq=seq):
                    u, mi = seq[t]
                    kb, ilo, ihi = units[u]
                    bank = sbanks[t % len(sbanks)]
                    c0, c1 = ilo * 128, (ihi + 1) * 128
                    ds_ = 32 * g - kb + 31
                    P.op("act", lambda e: e.activation(out=PT[t % 3][:, c0:c1], in_=PF[bank][:, c0:c1], func=AF.Exp,
                                                       bias=BT[:, ds_:ds_ + 1], scale=1.0),
                         reads=[PFb[bank], btb], writes=[ptb[t % 3]])

                def pv(t, g=g, units=units, seq=seq, ns=ns):
                    u, mi = seq[t]
                    kb, ilo, ihi = units[u]
                    slot = 127 - kb
                    c0, c1 = ilo * 128, (ihi + 1) * 128
                    ob = 4 + mi
                    last = (u == len(units) - 1)
                    P.op("pe", lambda e: e.matmul(PF[ob][0:65, c0:c1], lhsT=VV[:, slot, :], rhs=PT[t % 3][:, c0:c1],
                                                  start=False, stop=last),
                         reads=[vvb, ptb[t % 3]], writes=[PFb[ob]], sig=True)

                pipeline(ns, [(0, qk), (0, ex), (1, pv)])
                tb = [3, 2]
                for mi in range(nmap):
                    ob = 4 + mi
                    P.op("dve", lambda e, ob=ob: e.tensor_copy(out=OAf[0:65, :], in_=PF[ob][0:65, :]),
                         reads=[PFb[ob]], writes=[oab])
                    for i in range(4):
                        P.op("pe", lambda e, i=i, mi=mi: e.transpose(out=PF[tb[mi]][:, i * 65:(i + 1) * 65],
                                                                   in_=OAf[0:65, i * 128:(i + 1) * 128],
                                                                   identity=identf[0:65, 0:65]),
                             reads=[oab, cbuf], writes=[PFb[tb[mi]]], sig=(i == 3))
                for i in range(4):
                    m = 4 * g + i
                    T1 = PF[3][:, i * 65:(i + 1) * 65]
                    r1, r1b = sc_col()
                    P.op("dve", lambda e, T1=T1, r1=r1: e.reciprocal(out=r1, in_=T1[:, 64:65]), reads=[PFb[3]], writes=[r1b])
                    if typ == "dil":
                        P.op("dve", lambda e, T1=T1, r1=r1, m=m: e.tensor_scalar(
                            out=MT[:, m, half * 64:(half + 1) * 64], in0=T1[:, 0:64], scalar1=r1, scalar2=None, op0=ALU.mult),
                            reads=[PFb[3], r1b], writes=[mtb])
                    else:
                        T2 = PF[2][:, i * 65:(i + 1) * 65]
                        r2, r2b = sc_col()
                        ssq, ssqb = sc_col()
                        std, stdb = sc_col()
                        rstd, rstdb = sc_col()
                        P.op("dve", lambda e, T2=T2, r2=r2: e.reciprocal(out=r2, in_=T2[:, 64:65]), reads=[PFb[2]], writes=[r2b])
                        P.op("dve", lambda e, r2=r2: e.tensor_tensor(out=r2, in0=r2, in1=lamneg, op=ALU.mult),
                             reads=[r2b, lamnegb], writes=[r2b])
                        P.op("dve", lambda e, T1=T1, r1=r1: e.tensor_scalar(out=TMP, in0=T1[:, 0:64], scalar1=r1, scalar2=None, op0=ALU.mult),
                             reads=[PFb[3], r1b], writes=[tmpb])
                        P.op("dve", lambda e, T2=T2, r2=r2: e.scalar_tensor_tensor(out=OTMP, in0=T2[:, 0:64], scalar=r2, in1=TMP,
                                                                                  op0=ALU.mult, op1=ALU.add),
                             reads=[PFb[2], r2b, tmpb], writes=[otmpb])
                        P.op("dve", lambda e, ssq=ssq: e.scalar_tensor_tensor(out=TMP, in0=OTMP, scalar=1.0, in1=OTMP,
                                                                              op0=ALU.mult, op1=ALU.mult, accum_out=ssq),
                             reads=[otmpb, tmpb], writes=[tmpb, ssqb])
                        P.op("act", lambda e, ssq=ssq, std=std: e.activation(out=std, in_=ssq, func=AF.Sqrt, bias=epsT[:, 0:1], scale=1.0 / 64),
                             reads=[ssqb, cbuf], writes=[stdb])
                        P.op("dve", lambda e, std=std, rstd=rstd: e.reciprocal(out=rstd, in_=std), reads=[stdb], writes=[rstdb])
                        P.op("dve", lambda e, rstd=rstd, m=m: e.scalar_tensor_tensor(
                            out=MT[:, m, half * 64:(half + 1) * 64], in0=OTMP, scalar=rstd, in1=GSUB, op0=ALU.mult, op1=ALU.mult),
                            reads=[otmpb, rstdb, gsubb], writes=[mtb])

        def sb_head(hh, h):
            half = hh % 2
            sa = [0, 3, 4, 7, 8, 11, 12, 15]
            sbm = [1, 2, 5, 6, 9, 10, 13, 14]
            ua = [(m, u, 0) for m in sa for u in range(2 * (m + 1))]
            ub = [(m, u, 1) for m in sbm for u in range(2 * (m + 1))]
            assert len(ua) == len(ub)
            units = [x for pair_ in zip(ua, ub) for x in pair_]
            n = len(units)
            NR = 4
            ptcb = [Buf(f"ptc{i}") for i in range(NR)]

            def qk(t):
                m, u, st = units[t]
                bank = t % NR
                col0 = (120 - 8 * m) * 128 + 512 * u
                zone = u < 2
                P.op("pe", lambda e: e.matmul(PF[bank][:], lhsT=QT[0:64, m * 128:(m + 1) * 128], rhs=KT[0:64, col0:col0 + 512],
                                              start=True, stop=not zone),
                     reads=[qtb, ktb[col0 // 4096]], writes=[PFb[bank]], sig=not zone)
                if zone:
                    P.op("pe", lambda e: e.matmul(PF[bank][:], lhsT=identb[:], rhs=MSB[:, u * 512:(u + 1) * 512], start=False, stop=True),
                         reads=[mskb, cbuf], writes=[PFb[bank]])

            def sg(t):
                bank = t % NR
                P.op("act", lambda e: e.activation(out=SS[t % NR], in_=PF[bank][:], func=AF.Sigmoid, scale=-1.0),
                     reads=[PFb[bank]], writes=[ssb[t % NR]])

            def scan(t):
                m, u, st = units[t]
                cur, prv = CP[t % NR], CP[(t - 2) % NR]
                if u == 0:
                    init = 1.0
                    rd = [ssb[t % NR]]
                else:
                    init = prv[:, 512:513]
                    rd = [ssb[t % NR], cpb[(t - 2) % NR]]
                P.op("dve", lambda e: e.tensor_tensor_scan(out=cur[:, 1:513], data0=SS[t % NR], data1=ZB[:, 0:512], initial=init,
                                                           op0=ALU.mult, op1=ALU.add),
                     reads=rd + [zerob], writes=[cpb[t % NR]])
                P.op("dve", lambda e: e.tensor_tensor(out=PT[t % NR][:, 1:512], in0=cur[:, 1:512], in1=cur[:, 2:513], op=ALU.subtract),
                     reads=[cpb[t % NR]], writes=[ptb[t % NR]])
                if u == 0:
                    P.op("pool", lambda e: e.tensor_scalar(out=PT[t % NR][:, 0:1], in0=cur[:, 1:2], scalar1=-1.0, scalar2=1.0,
                                                           op0=ALU.mult, op1=ALU.add),
                         reads=[cpb[t % NR]], writes=[ptcb[t % NR]])
                else:
                    P.op("pool", lambda e: e.tensor_tensor(out=PT[t % NR][:, 0:1], in0=prv[:, 512:513], in1=cur[:, 1:2], op=ALU.subtract),
                         reads=[cpb[t % NR], cpb[(t - 2) % NR]], writes=[ptcb[t % NR]])

            def tr(t):
                pbi = t % 2
                for j in range(4):
                    P.op("pe", lambda e, j=j: e.transpose(out=PB[pbi][:, j * 128:(j + 1) * 128], in_=PT[t % NR][:, j * 128:(j + 1) * 128],
                                                          identity=identb[:]),
                         reads=[ptb[t % NR], ptcb[t % NR], cbuf], writes=[PBb[pbi]], sig=(j == 3))
                P.op("act", lambda e: e.activation(out=WT[t % NR], in_=PB[pbi][:, 0:512], func=AF.Copy),
                     reads=[PBb[pbi]], writes=[wtb[t % NR]])

            def pv(t):
                m, u, st = units[t]
                ob = 4 + st
                last_u = 2 * (m + 1) - 1
                slot0 = (120 - 8 * m) + 4 * u
                for j in range(4):
                    P.op("pe", lambda e, j=j: e.matmul(PF[ob][:, 0:64], lhsT=WT[t % NR][:, j * 128:(j + 1) * 128],
                                                       rhs=VV[:, slot0 + j, 0:64], start=(u == 0 and j == 0),
                                                       stop=(u == last_u and j == 3)),
                         reads=[wtb[t % NR], vvb], writes=[PFb[ob]], sig=(j == 3))
                if u == last_u:
                    P.op("dve", lambda e: e.tensor_copy(out=MT[:, m, half * 64:(half + 1) * 64], in_=PF[ob][:, 0:64]),
                         reads=[PFb[ob]], writes=[mtb])

            pipeline(n, [(0, qk), (0, sg), (0, scan), (2, tr), (3, pv)])

        def flush_pair(pair):
            for q4 in range(4):
                pbi = q4 % 2
                for j in range(4):
                    m = q4 * 4 + j
                    P.op("pe", lambda e, m=m, j=j, pbi=pbi: e.transpose(out=PB[pbi][:, j * 128:(j + 1) * 128], in_=MT[:, m, :], identity=identb[:]),
                         reads=[mtb, cbuf], writes=[PBb[pbi]], sig=(j == 3))
                P.op("dve", lambda e, q4=q4, pbi=pbi: e.tensor_copy(out=MIX3[:, pair, q4 * 512:(q4 + 1) * 512], in_=PB[pbi][:, 0:512]),
                     reads=[PBb[pbi]], writes=[mixb[pair]])

        for hh, (typ, h) in enumerate(HEADS):
            if head_sel is not None and hh not in head_sel:
                continue
            if hh == 10:
                P.barrier()
            load_head(hh, typ)
            import os
            dbg = os.environ.get("DBG_MODE", "")
            if dbg.startswith("loadonly"):
                P.op("dve", lambda e: e.tensor_copy(out=MT[:, 0, :], in_=KT[:, 0:128]), reads=ktb + [vvb, qtb], writes=[mtb])
            elif typ == "sb":
                sb_head(hh, h)
            else:
                softmax_head(hh, typ, h)
            if dbg.endswith("noflush"):
                continue
            if hh % 2 == 1 or head_sel is not None:
                flush_pair(hh // 2)
        return MIX3, mixb

    ZB = sb("ZB", [128, 512], BF16)
    zerob = Buf("zero")
    P.op("pool", lambda e: e.memset(ZB[:], 0.0), writes=[zerob])

    def out_proj(layer, MIX3, mixb):
        wo = WD[("wout", layer)]
        wov = wo.rearrange("(k p) n -> p k n", p=128)
        WO3 = vb(0, 8192).rearrange("p (k n) -> p k n", k=8)
        wob = [Buf("wo0"), Buf("wo1")]
        for i in range(2):
            P.dma("pool", WO3[:, i * 4:(i + 1) * 4, :], wov[:, i * 4:(i + 1) * 4, :], writes=[wob[i]], sembuf=wob[i])
        load_gain(1, layer, 3)
        for m in range(NB):
            for hf in range(2):
                bank = 4 + hf
                for k in range(8):
                    P.op("pe", lambda e, k=k, m=m, hf=hf, bank=bank: e.matmul(
                        PF[bank][:], lhsT=MIX3[:, k, m * 128:(m + 1) * 128], rhs=WO3[:, k, hf * 512:(hf + 1) * 512],
                        start=(k == 0), stop=(k == 7)),
                        reads=[mixb[k], wob[k // 4]], writes=[PFb[bank]], sig=(k == 7))
            postnorm_residual(m, (4, 5), 1, 1.0)

    for (kind, layer) in stages:
        if kind == "A":
            ffn(layer, 0, 0, 1)
            P.barrier()
            proj(layer)
            P.barrier()
        else:
            MIX3, mixb = attention_body(layer)
            P.barrier()
            if debug_mix:
                dbg = dout("dbg_mix", [128, 8 * TOK], BF16)
                P.dma("sp", dbg.rearrange("p (k t) -> p k t", k=8), MIX3, reads=mixb, sembuf=Buf("dbg"), final=True)
            if attn_only:
                continue
            out_proj(layer, MIX3, mixb)
            P.barrier()
            ffn(layer, 1, 4, 5)
            P.barrier()
    xo = x_out.rearrange("(m p) d -> p m d", p=128)
    xst = Buf("xstore")
    for m in range(NB):
        P.dma("sp", xo[:, m, :], X[:, m * D:(m + 1) * D], reads=[xb[m]], sembuf=xst, final=True)
    P.run()
    es.close()
    return nc


_SHARED = None


def _tok_index(c):
    m = np.arange(NB)[:, None]
    j = np.arange(128)[None, :]
    return ((8 * m + c) * 128 + 127 - j).reshape(-1)


def _launch(stages, per_core_extra, weights, debug_mix=False, head_sel=None, attn_only=False):
    global _SHARED
    if _SHARED is None:
        _SHARED = shared_constants()
    nc = build_program(stages, debug_mix=debug_mix, head_sel=head_sel, attn_only=attn_only)
    in_maps = []
    for c in range(NCORES):
        mp = dict(_SHARED)
        mp.update(core_constants(c))
        mp.update(weights)
        mp.update(per_core_extra[c])
        in_maps.append(mp)
    res = run_bass_kernel_spmd(nc, in_maps, core_ids=list(range(NCORES)))
    return res.results


def _weights_for(stages, inp):
    w = {}
    for (kind, l) in stages:
        j = 0 if kind == "A" else 1
        w[f"gate{l}_{j}"] = np.ascontiguousarray(inp["w_ffn_gate"][l, j])
        w[f"up{l}_{j}"] = np.ascontiguousarray(inp["w_ffn_up"][l, j])
        w[f"down{l}_{j}"] = np.ascontiguousarray(inp["w_ffn_down"][l, j])
        w[f"gains{l}"] = np.ascontiguousarray(inp["norm_gains"][l])
        if kind == "A":
            w[f"win{l}"] = np.ascontiguousarray(inp["w_in"][l])
        else:
            w[f"wout{l}"] = np.ascontiguousarray(inp["w_out"][l])
            w[f"dlam{l}"] = np.ascontiguousarray(inp["diff_lambda"][l].reshape(1, 128))
            w[f"dgain{l}"] = np.ascontiguousarray(inp["diff_subln_gain"][l].reshape(1, 64))
    return w


def _gather_kv(results):
    kt_all = np.zeros((16, 128, S), NPBF)
    v_all = np.zeros((16, 128, NBLK, 65), NPBF)
    for c in range(NCORES):
        kt = results[c]["kt_out"].reshape(16, 128, NB, 128)
        vv = results[c]["v_out"].reshape(16, 128, NB, 65)
        for m in range(NB):
            slot = 127 - (8 * m + c)
            kt_all[:, :, slot * 128:(slot + 1) * 128] = kt[:, :, m, :]
            v_all[:, :, slot, :] = vv[:, :, m, :]
    return kt_all, v_all.reshape(16, 128, NBLK * 65)


def kernel(x, norm_gains, w_ffn_gate, w_ffn_up, w_ffn_down, w_in, w_out, diff_lambda, diff_subln_gain):
    inp = dict(x=np.asarray(x), norm_gains=np.asarray(norm_gains), w_ffn_gate=np.asarray(w_ffn_gate),
               w_ffn_up=np.asarray(w_ffn_up), w_ffn_down=np.asarray(w_ffn_down), w_in=np.asarray(w_in),
               w_out=np.asarray(w_out), diff_lambda=np.asarray(diff_lambda), diff_subln_gain=np.asarray(diff_subln_gain))
    x2 = inp["x"].reshape(S, D)
    idx = [_tok_index(c) for c in range(NCORES)]
    xs = [np.ascontiguousarray(x2[idx[c]]) for c in range(NCORES)]
    st = [("A", 0)]
    r = _launch(st, [{"x_in": xs[c]} for c in range(NCORES)], _weights_for(st, inp))
    for l in range(DEPTH):
        kt_all, v_all = _gather_kv(r)
        st = [("B", l)] + ([("A", l + 1)] if l + 1 < DEPTH else [])
        extra = [{"x_in": r[c]["x_out"], "kt_all": kt_all, "v_all": v_all, "qt_in": r[c]["qt_out"]} for c in range(NCORES)]
        r = _launch(st, extra, _weights_for(st, inp))
    out = np.zeros((S, D), np.float32)
    for c in range(NCORES):
        out[idx[c]] = r[c]["x_out"]
    return out.reshape(1, S, D)
```

```python
import math
import contextlib
import numpy as np
import ml_dtypes
import concourse.bass as bass
import concourse.mybir as mybir
from concourse.bass_utils import run_bass_kernel_spmd

F32 = mybir.dt.float32
BF16 = mybir.dt.bfloat16
AF = mybir.ActivationFunctionType
ALU = mybir.AluOpType
NPBF = ml_dtypes.bfloat16

NCORES = 8
D = 1024
DFF = 2816
NF = DFF // 128
S = 16384
TOK = S // NCORES
NB = TOK // 128
NBLK = S // 128
DEPTH = 2
EPS = 1e-6
NEG = -30000.0
THRESH = 120.0
HEADS = ([("dil", h) for h in range(6)] + [("diff", h) for h in range(4)] + [("sb", h) for h in range(6)])
QCOL = {"dil": 0, "diff": 1152, "sb": 1920}
KCOL = {"dil": 384, "diff": 1408, "sb": 2304}
VCOL = {"dil": 768, "diff": 1664, "sb": 2688}
KROWS = {"dil": 96, "diff": 112, "sb": 64}


def alibi(n):
    return [2.0 ** (-8.0 * (i + 1) / n) for i in range(n)]


SLOPE = {("dil", h): alibi(6)[h] for h in range(6)}
SLOPE.update({("diff", h): alibi(4)[h] for h in range(4)})
SLOPE_LIST = [("dil", h) for h in range(6)] + [("diff", h) for h in range(4)]


class Buf:
    __slots__ = ("name", "writer", "readers", "sem", "semcnt")

    def __init__(self, name):
        self.name = name
        self.writer = None
        self.readers = {}
        self.sem = None
        self.semcnt = 0


ENGS = ("pe", "act", "dve", "pool", "sp")


class Prog:
    def __init__(self, nc, es):
        self.nc = nc
        self.es = es
        self.q = {e: [] for e in ENGS}
        self.esem = {e: es.enter_context(nc.semaphore("S_" + e)) for e in ("pe", "act", "dve", "pool")}
        self.ecnt = {e: 0 for e in self.esem}
        self.waited = {e: {} for e in ENGS}
        self.pending = {e: [] for e in ENGS}
        self.finals = []
        self.nsem = 4
        self.dma_tks = {}

    def _wait(self, e, tk):
        if tk is None:
            return
        sem, val, src = tk
        if sem is None:
            assert src == e, "cross-engine wait on unsignaled op"
            return
        if src == "pe" and e == "pe":
            return
        key = id(sem)
        if self.waited[e].get(key, 0) >= val:
            return
        self.waited[e][key] = val
        self.q[e].append(("w", sem, val))

    def _deps(self, e, reads, writes):
        for b in reads:
            self._wait(e, b.writer)
        for b in writes:
            self._wait(e, b.writer)
            for r in b.readers.values():
                self._wait(e, r)

    def _commit(self, tk, reads, writes):
        for b in reads:
            if b not in writes:
                b.readers[id(tk[0])] = tk
        for b in writes:
            b.writer = tk
            b.readers = {}

    def op(self, e, fn, reads=(), writes=(), sig=True):
        self._deps(e, reads, writes)
        if sig:
            self.ecnt[e] += 1
            tk = (self.esem[e], self.ecnt[e], e)
            self.q[e].append(("o", fn, self.esem[e], 1))
            for (rs, ws) in self.pending[e]:
                self._commit(tk, rs, ws)
            self.pending[e] = []
            self._commit(tk, reads, writes)
        else:
            assert e == "pe"
            self.q[e].append(("o", fn, None, 0))
            self.pending[e].append((tuple(reads), tuple(writes)))
            for b in writes:
                b.writer = (None, 0, e)
                b.readers = {}

    def dma(self, e, out_ap, in_ap, reads=(), writes=(), sembuf=None, final=False):
        self._deps(e, reads, writes)
        sb = sembuf
        if sb.sem is None:
            sb.sem = self.es.enter_context(self.nc.semaphore("D%d_%s" % (self.nsem, sb.name)))
            self.nsem += 1
        sb.semcnt += 16
        tk = (sb.sem, sb.semcnt, "dma")
        self.q[e].append(("d", out_ap, in_ap, sb.sem))
        self.dma_tks[id(sb.sem)] = tk
        self._commit(tk, reads, writes)
        if final:
            self.finals.append(tk)
        return tk

    def barrier(self):
        for b_e in ENGS:
            assert not self.pending[b_e], "dangling unsignaled ops on " + b_e
        tks = [(self.esem[e], self.ecnt[e], e) for e in self.esem if self.ecnt[e] > 0]
        tks += list(self.dma_tks.values())
        for e in ENGS:
            for tk in tks:
                self._wait(e, tk)

    def run(self):
        nc = self.nc
        for b_e in ENGS:
            assert not self.pending[b_e], "dangling unsignaled ops on " + b_e
        for tk in self.finals:
            self._wait("sp", tk)

        def replay(eng, items):
            for it in items:
                if it[0] == "w":
                    eng.wait_ge(it[1], it[2])
                elif it[0] == "o":
                    ins = it[1](eng)
                    if it[2] is not None:
                        ins.then_inc(it[2], it[3])
                else:
                    eng.dma_start(out=it[1], in_=it[2]).then_inc(it[3], 16)

        with nc.Block() as block:
            @block.tensor
            def _(eng):
                replay(eng, self.q["pe"])

            @block.scalar
            def _(eng):
                replay(eng, self.q["act"])

            @block.vector
            def _(eng):
                replay(eng, self.q["dve"])

            @block.gpsimd
            def _(eng):
                replay(eng, self.q["pool"])

            @block.sync
            def _(eng):
                replay(eng, self.q["sp"])


def pipeline(n, stages):
    maxlag = max(l for l, _ in stages)
    for t in range(n + maxlag):
        for lag, fn in stages:
            u = t - lag
            if 0 <= u < n:
                fn(u)


def _bf(x):
    return np.asarray(x, dtype=np.float32).astype(NPBF)


def core_constants(c):
    p = np.arange(128)[:, None].astype(np.int64)
    col = np.arange(128)[None, :].astype(np.int64)
    out = {}
    msb = np.zeros((128, 8, 128), np.float32)
    for a in range(8):
        r = 7 - a
        if r > c:
            msb[:, a, :] = NEG
        elif r == c:
            msb[:, a, :] = np.where(col > p, 0.0, NEG)
    out["c_msb"] = _bf(msb.reshape(128, 1024))
    mc = np.zeros((128, 8, 128), np.float32)
    for r in range(8):
        if r > c:
            mc[:, r, :] = NEG
        elif r == c:
            mc[:, r, :] = np.where(p >= col, 0.0, NEG)
    out["c_mc"] = _bf(mc.reshape(128, 1024))
    td = np.full((128, 24, 128), NEG, np.float32)
    for idx in range(24):
        Dd = idx - 7 + c
        if 0 <= Dd <= 16:
            delta = 128 * Dd - col + p
            mult = ((delta >= 0) & (delta <= 128)).astype(np.int64)
            mult = mult + ((delta >= 0) & (delta <= 512) & (delta % 4 == 0))
            mult = mult + ((delta >= 0) & (delta <= 2048) & (delta % 16 == 0))
            td[:, idx, :] = np.where(mult > 0, np.log(np.maximum(mult, 1)), NEG)
    out["c_td"] = _bf(td.reshape(128, 24 * 128))
    ds = np.arange(159)[None, :]
    out["c_base"] = (128.0 * (ds - 31 + c) + p).astype(np.float32)
    return out


def shared_constants():
    out = {}
    out["c_identb"] = _bf(np.eye(128))
    out["c_identf"] = np.eye(128, dtype=np.float32)
    e_dil = np.zeros((8, 128), np.float32)
    for r in range(4):
        e_dil[r, 64 + r] = 1.0
    e_diff = np.zeros((8, 128), np.float32)
    for r in range(4):
        e_diff[r, 32 + r] = 1.0
        e_diff[4 + r, 96 + r] = 1.0
    out["c_edil"] = _bf(e_dil)
    out["c_ediff"] = _bf(e_diff)
    colq = np.arange(512)
    jq = (colq % 128).astype(np.float32)
    ii = (colq // 128).astype(np.float32)
    qa = np.stack([jq, jq, -1024.0 * ii, -1024.0 * ii] * 2, 0)
    out["c_qaug"] = _bf(qa)
    ks = np.zeros((8, 10, 128), np.float32)
    for si, key in enumerate(SLOPE_LIST):
        s = np.float32(SLOPE[key])
        hi = np.float32(s.astype(NPBF))
        lo = np.float32(np.float32(s - hi).astype(NPBF))
        sel = e_dil if key[0] == "dil" else e_diff
        for r, v in enumerate([hi, lo, hi, lo] * 2):
            ks[r, si, :] = sel[r] * v
    out["c_ksel"] = _bf(ks.reshape(8, 1280))
    out["c_ones8"] = _bf(np.ones((8, 512)))
    scl = np.ones((128, 2), np.float32)
    scl[0:64, 0] = 64.0 ** -0.5
    scl[0:32, 1] = 32.0 ** -0.5
    scl[64:96, 1] = 32.0 ** -0.5
    out["c_scl"] = scl
    return out


CONST_SPECS = {
    "c_msb": ([128, 1024], BF16), "c_mc": ([128, 1024], BF16), "c_td": ([128, 3072], BF16),
    "c_base": ([128, 159], F32), "c_identb": ([128, 128], BF16), "c_identf": ([128, 128], F32),
    "c_edil": ([8, 128], BF16), "c_ediff": ([8, 128], BF16), "c_qaug": ([8, 512], BF16),
    "c_ksel": ([8, 1280], BF16), "c_ones8": ([8, 512], BF16), "c_scl": ([128, 2], F32),
}


def build_program(stages, debug_mix=False, head_sel=None, attn_only=False, fused=False):
    nc = bass.Bass("TRN2", target_bir_lowering=False)
    es = contextlib.ExitStack()
    P = Prog(nc, es)
    has_b = [l for (k, l) in stages if k == "B"]
    has_a = [l for (k, l) in stages if k == "A"]
    last_is_a = stages[-1][0] == "A"

    def din(name, shape, dt=F32):
        return nc.dram_tensor(name, shape, dt, kind="ExternalInput").ap()

    def dout(name, shape, dt=F32):
        return nc.dram_tensor(name, shape, dt, kind="ExternalOutput").ap()

    x_in = din("x_in", [TOK, D])
    x_out = dout("x_out", [TOK, D])
    consts = {k: din(k, shp, dt) for k, (shp, dt) in CONST_SPECS.items()}
    WD = {}
    for l in has_a:
        WD[("gate", l, 0)] = din(f"gate{l}_0", [D, DFF])
        WD[("up", l, 0)] = din(f"up{l}_0", [D, DFF])
        WD[("down", l, 0)] = din(f"down{l}_0", [DFF, D])
        WD[("win", l)] = din(f"win{l}", [D, 3072])
    for l in has_b:
        WD[("gate", l, 1)] = din(f"gate{l}_1", [D, DFF])
        WD[("up", l, 1)] = din(f"up{l}_1", [D, DFF])
        WD[("down", l, 1)] = din(f"down{l}_1", [DFF, D])
        WD[("wout", l)] = din(f"wout{l}", [D, D])
        WD[("dlam", l)] = din(f"dlam{l}", [1, 128])
        WD[("dgain", l)] = din(f"dgain{l}", [1, 64])
    gains = {l: din(f"gains{l}", [6, D]) for l in sorted(set(has_a + has_b))}
    if fused:
        kt_c2 = nc.dram_tensor("kt_c", [16 * 128, TOK], BF16)
        v_c2 = nc.dram_tensor("v_c", [16 * 128, NB * 65], BF16)
        kt_g2 = nc.dram_tensor("kt_g", [NCORES * 16 * 128, TOK], BF16)
        v_g2 = nc.dram_tensor("v_g", [NCORES * 16 * 128, NB * 65], BF16)
        qt_s = nc.dram_tensor("qt_s", [16, 128, TOK], BF16).ap()
        kt_all, v_all, qt_in, qt_out = kt_g2.ap(), v_g2.ap(), qt_s, qt_s
        kt_out = kt_c2.ap().rearrange("(h r) n -> h r n", h=16)
        v_out = v_c2.ap().rearrange("(h r) n -> h r n", h=16)
    else:
        if has_b:
            kt_all = din("kt_all", [NCORES * 16 * 128, TOK], BF16)
            v_all = din("v_all", [NCORES * 16 * 128, NB * 65], BF16)
            qt_in = din("qt_in", [16, 128, TOK], BF16)
        if last_is_a:
            qt_out = dout("qt_out", [16, 128, TOK], BF16)
            kt_out = dout("kt_out", [16, 128, TOK], BF16)
            v_out = dout("v_out", [16, 128, NB * 65], BF16)

    def sb(name, shape, dt):
        return es.enter_context(nc.sbuf_tensor(name, shape, dt))

    def ps(name, shape, dt):
        return es.enter_context(nc.psum_tensor(name, shape, dt))

    X = sb("X", [128, NB * D], F32)
    xb = [Buf(f"x{m}") for m in range(NB)]
    ARB = sb("ARB", [128, 59392], BF16)
    ARF = sb("ARF", [128, 4112], F32)
    identb = sb("identb", [128, 128], BF16)
    identf = sb("identf", [128, 128], F32)
    edil = sb("edil", [8, 128], BF16)
    ediff = sb("ediff", [8, 128], BF16)
    qaug = sb("qaug", [8, 512], BF16)
    ksel = sb("ksel", [8, 1280], BF16)
    ones8 = sb("ones8", [8, 512], BF16)
    scl = sb("scl", [128, 2], F32)
    base = sb("base", [128, 159], F32)
    epsT = sb("epsT", [128, 1], F32)
    small = sb("small", [128, 64], F32)
    cbuf = Buf("consts")
    PF = [ps(f"pf{i}", [128, 512], F32) for i in range(6)]
    PFb = [Buf(f"pf{i}") for i in range(6)]
    PB = [ps(f"pb{i}", [128, 1024], BF16) for i in range(2)]
    PBb = [Buf(f"pb{i}") for i in range(2)]

    def vb(off, n):
        return ARB[:, off:off + n]

    def vf(off, n):
        return ARF[:, off:off + n]

    xsem = Buf("xload")
    xv = x_in.rearrange("(m p) d -> p m d", p=128)
    for m in range(NB):
        P.dma("sp", X[:, m * D:(m + 1) * D], xv[:, m, :], writes=[xb[m]], sembuf=xsem)
    tkx = (xsem.sem, xsem.semcnt, "dma")
    for m in range(NB):
        xb[m].writer = tkx
    for (tile, key) in ((identb, "c_identb"), (identf, "c_identf"), (edil, "c_edil"), (ediff, "c_ediff"),
                        (qaug, "c_qaug"), (ksel, "c_ksel"), (ones8, "c_ones8"), (scl, "c_scl"), (base, "c_base")):
        P.dma("sp", tile[:], consts[key], writes=[], sembuf=cbuf)
    cbuf.writer = (cbuf.sem, cbuf.semcnt, "dma")
    P.op("pool", lambda e: e.memset(epsT[:], EPS), writes=[cbuf])

    GREP = [vf(0, 1024), vf(1024, 1024)]
    GREPb = [Buf("grep0"), Buf("grep1")]
    TT = vf(2048, 1024)
    TTb = Buf("tt")

    def load_gain(slot, layer, idx):
        P.dma("sp", GREP[slot], bcast_rows(gains[layer], idx, D), writes=[GREPb[slot]], sembuf=GREPb[slot])

    def bcast_rows(ap2d, row, n):
        r = ap2d[row:row + 1, 0:n]
        return bass.AP(tensor=r.tensor, offset=r.offset, ap=[[0, 128], [1, n]])

    sc_ctr = [0]

    def sc_col():
        i = sc_ctr[0] % 48
        sc_ctr[0] += 1
        return small[:, i:i + 1], SCb[i]

    SCb = [Buf(f"sc{i}") for i in range(64)]

    def prenorm_block(m, gslot, hn_ap, hnb):
        xm = X[:, m * D:(m + 1) * D]
        ssq, ssqb = sc_col()
        std, stdb = sc_col()
        rstd, rstdb = sc_col()
        P.op("act", lambda e: e.activation(out=hn_ap, in_=xm, func=AF.Square, accum_out=ssq),
             reads=[xb[m]], writes=[ssqb, hnb])
        P.op("act", lambda e: e.activation(out=std, in_=ssq, func=AF.Sqrt, bias=epsT[:, 0:1], scale=1.0 / D),
             reads=[ssqb, cbuf], writes=[stdb])
        P.op("dve", lambda e: e.reciprocal(out=rstd, in_=std), reads=[stdb], writes=[rstdb])
        P.op("dve", lambda e: e.scalar_tensor_tensor(out=hn_ap, in0=xm, scalar=rstd, in1=GREP[gslot],
                                                     op0=ALU.mult, op1=ALU.mult),
             reads=[xb[m], rstdb, GREPb[gslot]], writes=[hnb])

    def transpose_block(hn_ap, hnb, hT3, hTb, tokcol, pbi):
        pb = PB[pbi]
        for k in range(8):
            P.op("pe", lambda e, k=k: e.transpose(out=pb[:, k * 128:(k + 1) * 128], in_=hn_ap[:, k * 128:(k + 1) * 128],
                                                  identity=identb[:]),
                 reads=[hnb, cbuf], writes=[PBb[pbi]], sig=(k == 7))
        P.op("act", lambda e: e.activation(out=hT3[:, :, tokcol:tokcol + 128],
                                           in_=pb[:, 0:1024].rearrange("p (k t) -> p k t", k=8), func=AF.Copy),
             reads=[PBb[pbi]], writes=[hTb])

    def postnorm_residual(m, ybanks, gslot, coef):
        xm = X[:, m * D:(m + 1) * D]
        s0, s0b = sc_col()
        s1, s1b = sc_col()
        ssq, ssqb = sc_col()
        std, stdb = sc_col()
        rstd, rstdb = sc_col()
        for (sc, scb, bi) in ((s0, s0b, ybanks[0]), (s1, s1b, ybanks[1])):
            P.op("act", lambda e, sc=sc, bi=bi: e.activation(out=JUNK2[0], in_=PF[bi][:], func=AF.Square, accum_out=sc),
                 reads=[PFb[bi]], writes=[scb, JUNK2b[0]])
        P.op("dve", lambda e: e.tensor_tensor(out=ssq, in0=s0, in1=s1, op=ALU.add), reads=[s0b, s1b], writes=[ssqb])
        P.op("act", lambda e: e.activation(out=std, in_=ssq, func=AF.Sqrt, bias=epsT[:, 0:1], scale=1.0 / D),
             reads=[ssqb, cbuf], writes=[stdb])
        P.op("dve", lambda e: e.reciprocal(out=rstd, in_=std), reads=[stdb], writes=[rstdb])
        for hf in range(2):
            bi = ybanks[hf]
            P.op("dve", lambda e, bi=bi, hf=hf: e.scalar_tensor_tensor(
                out=TT[:, hf * 512:(hf + 1) * 512], in0=PF[bi][:], scalar=float(coef),
                in1=GREP[gslot][:, hf * 512:(hf + 1) * 512], op0=ALU.mult, op1=ALU.mult),
                reads=[PFb[bi], GREPb[gslot]], writes=[TTb])
        P.op("dve", lambda e: e.scalar_tensor_tensor(out=xm, in0=TT[:], scalar=rstd, in1=xm, op0=ALU.mult, op1=ALU.add),
             reads=[TTb, rstdb, xb[m]], writes=[xb[m]])

    def ffn(layer, which, g_pre, g_post):
        wg, wu, wdn = WD[("gate", layer, which)], WD[("up", layer, which)], WD[("down", layer, which)]
        ACT3 = vb(0, 22528).rearrange("p (f t) -> p f t", f=NF)
        WDN = vb(22528, 22528).rearrange("p (h f c) -> p h f c", h=2, f=NF)
        HT3 = vb(45056, 8192).rearrange("p (k t) -> p k t", k=8)
        WGU = vb(53248, 4096).rearrange("p (s w k n) -> p s w k n", s=2, w=2, k=8)
        HN = [vb(57344, 1024), vb(58368, 1024)]
        actb = [Buf(f"act{f}") for f in range(NF)]
        wdb = [[Buf(f"wd{h}_{i}") for i in range(2)] for h in range(2)]
        htb = Buf("ht")
        wgub = [[Buf(f"wgu{s}_{w}") for w in range(2)] for s in range(2)]
        hnb = [Buf("hn0"), Buf("hn1")]
        wgv = wg.rearrange("(k p) n -> p k n", p=128)
        wuv = wu.rearrange("(k p) n -> p k n", p=128)
        wdv = wdn.rearrange("(f p) c -> p f c", p=128)
        load_gain(0, layer, g_pre)
        load_gain(1, layer, g_post)

        def load_gu(f):
            s = f % 2
            P.dma("pool", WGU[:, s, 0], wgv[:, :, f * 128:(f + 1) * 128], writes=[wgub[s][0]], sembuf=wgub[s][0])
            P.dma("pool", WGU[:, s, 1], wuv[:, :, f * 128:(f + 1) * 128], writes=[wgub[s][1]], sembuf=wgub[s][1])

        def load_wd(h):
            for i in range(2):
                P.dma("pool", WDN[:, h, i * 11:(i + 1) * 11, :], wdv[:, i * 11:(i + 1) * 11, h * 512:(h + 1) * 512],
                      writes=[wdb[h][i]], sembuf=wdb[h][i])

        for tg in range(2):
            for j in range(8):
                m = tg * 8 + j
                prenorm_block(m, 0, HN[j % 2], hnb[j % 2])
                transpose_block(HN[j % 2], hnb[j % 2], HT3, htb, j * 128, j % 2)
            load_gu(0)
            load_gu(1)
            load_wd(0)
            load_wd(1)
            for f in range(NF):
                s = f % 2
                for hf in range(2):
                    gb, ub = (0, 1) if hf == 0 else (2, 3)
                    for (w, bank) in ((0, gb), (1, ub)):
                        for k in range(8):
                            P.op("pe", lambda e, w=w, bank=bank, k=k, s=s, hf=hf: e.matmul(
                                PF[bank][:], lhsT=WGU[:, s, w, k, :], rhs=HT3[:, k, hf * 512:(hf + 1) * 512],
                                start=(k == 0), stop=(k == 7)),
                                reads=[wgub[s][w], htb], writes=[PFb[bank]], sig=(k == 7))
                    SG = JUNK2[hf]
                    P.op("act", lambda e, gb=gb, SG=SG: e.activation(out=SG, in_=PF[gb][:], func=AF.Silu),
                         reads=[PFb[gb]], writes=[JUNK2b[hf]])
                    P.op("dve", lambda e, ub=ub, SG=SG, f=f, hf=hf: e.tensor_tensor(
                        out=ACT3[:, f, hf * 512:(hf + 1) * 512], in0=SG, in1=PF[ub][:], op=ALU.mult),
                        reads=[JUNK2b[hf], PFb[ub]], writes=[actb[f]])
                if f + 2 < NF:
                    load_gu(f + 2)
            for j in range(8):
                m = tg * 8 + j
                for h in range(2):
                    bank = 4 + h
                    for f in range(NF):
                        P.op("pe", lambda e, f=f, h=h, bank=bank, j=j: e.matmul(
                            PF[bank][:], lhsT=ACT3[:, f, j * 128:(j + 1) * 128], rhs=WDN[:, h, f, :],
                            start=(f == 0), stop=(f == NF - 1)),
                            reads=[actb[f], wdb[h][f // 11]], writes=[PFb[bank]], sig=(f == NF - 1))
                postnorm_residual(m, (4, 5), 1, 0.5)

    JUNK2 = [sb("SG0", [128, 512], BF16)[:], sb("SG1", [128, 512], BF16)[:]]
    JUNK2b = [Buf("sg0"), Buf("sg1")]

    def proj(layer):
        win = WD[("win", layer)]
        winv = win.rearrange("(k p) n -> p k n", p=128)
        HT3 = vb(0, 16384).rearrange("p (k t) -> p k t", k=8)
        WV = vb(16384, 8192).rearrange("p (k n) -> p k n", k=8)
        WST = vb(24576, 4096).rearrange("p (s k n) -> p s k n", s=4, k=8)
        HN = [vb(28672, 1024), vb(29696, 1024)]
        QST = [vb(30720, 512), vb(31232, 512)]
        VST = [vb(31744, 520).rearrange("p (h e) -> p h e", h=8), vb(32264, 520).rearrange("p (h e) -> p h e", h=8)]
        htb = Buf("pht")
        wvb = [Buf(f"wv{i}") for i in range(3)]
        wstb = [Buf(f"wst{i}") for i in range(4)]
        hnb = [Buf("phn0"), Buf("phn1")]
        qstb = [Buf("qst0"), Buf("qst1")]
        vstb = [Buf("vst0"), Buf("vst1")]
        load_gain(0, layer, 2)
        for s4 in range(4):
            P.op("pool", lambda e, s4=s4: e.memset(WST[:, s4], 0.0), writes=[wstb[s4]])
        for i in range(2):
            P.op("pool", lambda e, i=i: e.memset(VST[i], 1.0), writes=[vstb[i]])
        for i, (c0, n, d0) in enumerate(((768, 384, 0), (1664, 256, 384), (2688, 384, 640))):
            P.dma("pool", WV[:, :, d0:d0 + n], winv[:, :, c0:c0 + n], writes=[wvb[i]], sembuf=wvb[i])
        for m in range(NB):
            prenorm_block(m, 0, HN[m % 2], hnb[m % 2])
            transpose_block(HN[m % 2], hnb[m % 2], HT3, htb, m * 128, m % 2)
        cnt = [0, 0]
        qi = 0
        ring = 0
        for hh, (typ, h) in enumerate(HEADS):
            for which in ("q", "k"):
                colb = (QCOL if which == "q" else KCOL)[typ] + 64 * h
                if typ == "diff":
                    s4 = 2 + cnt[1] % 2
                    cnt[1] += 1
                    P.dma("pool", WST[:, s4, :, 0:32], winv[:, :, colb:colb + 32], writes=[wstb[s4]], sembuf=wstb[s4])
                    P.dma("pool", WST[:, s4, :, 64:96], winv[:, :, colb + 32:colb + 64], writes=[wstb[s4]], sembuf=wstb[s4])
                else:
                    s4 = cnt[0] % 2
                    cnt[0] += 1
                    P.dma("pool", WST[:, s4, :, 0:64], winv[:, :, colb:colb + 64], writes=[wstb[s4]], sembuf=wstb[s4])
                for g in range(4):
                    bank = ring % 4
                    ring += 1
                    aug = typ != "sb"
                    for k in range(8):
                        P.op("pe", lambda e, k=k, s4=s4, g=g, bank=bank, aug=aug: e.matmul(
                            PF[bank][:], lhsT=WST[:, s4, k, :], rhs=HT3[:, k, g * 512:(g + 1) * 512],
                            start=(k == 0), stop=(k == 7 and not aug)),
                            reads=[wstb[s4], htb], writes=[PFb[bank]], sig=(k == 7 and not aug))
                    if aug:
                        if which == "q":
                            sel = (edil if typ == "dil" else ediff)[:, :]
                            src = qaug[:, :]
                        else:
                            si = SLOPE_LIST.index((typ, h))
                            sel = ksel[:, si * 128:(si + 1) * 128]
                            src = ones8[:, :]
                        P.op("pe", lambda e, sel=sel, src=src, bank=bank: e.matmul(
                            PF[bank][:], lhsT=sel, rhs=src, start=False, stop=True),
                            reads=[cbuf], writes=[PFb[bank]])
                    st = qi % 2
                    qi += 1
                    if which == "q":
                        sc_i = 1 if typ == "diff" else 0
                        P.op("dve", lambda e, bank=bank, st=st, sc_i=sc_i: e.tensor_scalar(
                            out=QST[st], in0=PF[bank][:], scalar1=scl[:, sc_i:sc_i + 1], scalar2=None, op0=ALU.mult),
                            reads=[PFb[bank], cbuf], writes=[qstb[st]])
                        dst = qt_out[hh, :, g * 512:(g + 1) * 512]
                    else:
                        P.op("act", lambda e, bank=bank, st=st: e.activation(out=QST[st], in_=PF[bank][:], func=AF.Copy),
                             reads=[PFb[bank]], writes=[qstb[st]])
                        dst = kt_out[hh, :, g * 512:(g + 1) * 512]
                    P.dma("sp", dst, QST[st], reads=[qstb[st]], sembuf=qstb[st], final=True)
        for hf in range(2):
            for m in range(NB):
                bank = ring % 4
                ring += 1
                for k in range(8):
                    P.op("pe", lambda e, k=k, m=m, hf=hf, bank=bank: e.matmul(
                        PF[bank][:], lhsT=HT3[:, k, m * 128:(m + 1) * 128], rhs=WV[:, k, hf * 512:(hf + 1) * 512],
                        start=(k == 0), stop=(k == 7)),
                        reads=[htb, wvb[0], wvb[1], wvb[2]], writes=[PFb[bank]], sig=(k == 7))
                st = m % 2
                P.op("act", lambda e, bank=bank, st=st: e.activation(
                    out=VST[st][:, :, 0:64], in_=PF[bank][:].rearrange("p (h e) -> p h e", h=8), func=AF.Copy),
                    reads=[PFb[bank]], writes=[vstb[st]])
                dst = v_out[hf * 8:(hf + 1) * 8, :, m * 65:(m + 1) * 65].rearrange("h p e -> p h e")
                P.dma("sp", dst, VST[st], reads=[vstb[st]], sembuf=vstb[st], final=True)

    def attention_body(layer):
        lam_init = 0.8 - 0.6 * math.exp(-0.3 * layer)
        KT = vb(0, 16384)
        VV = vb(16384, 8320).rearrange("p (s e) -> p s e", e=65)
        QT = vb(24704, 2048)
        MIX3 = vb(26752, 16384).rearrange("p (k t) -> p k t", k=8)
        MSB = vb(43136, 1024)
        MC = vb(44160, 1024)
        TD = vb(45184, 3072)
        PT = [vb(48256 + i * 512, 512) for i in range(4)]
        WT = [vb(50304 + i * 512, 512) for i in range(4)]
        MT = vb(52352, 2048).rearrange("p (m f) -> p m f", m=NB)
        ktb = [Buf(f"kt{i}") for i in range(8)]
        vvb = Buf("vv")
        qtb = Buf("qt")
        mixb = [Buf(f"mix{k}") for k in range(8)]
        mskb = Buf("masks")
        ptb = [Buf(f"pt{i}") for i in range(4)]
        wtb = [Buf(f"wt{i}") for i in range(4)]
        mtb = Buf("mt")
        SS = [vf(i * 512, 512) for i in range(4)]
        CP = [vf(2048 + i * 516, 516) for i in range(4)]
        BT = vf(0, 159)
        OAf = vf(160, 512)
        TMP = vf(672, 64)
        OTMP = vf(736, 64)
        GSUB = vf(800, 64)
        LAMT = vf(864, 128)
        ssb = [Buf(f"ss{i}") for i in range(4)]
        cpb = [Buf(f"cp{i}") for i in range(4)]
        btb = Buf("bt")
        oab = Buf("oa")
        tmpb = Buf("tmp")
        otmpb = Buf("otmp")
        gsubb = Buf("gsub")
        lamb = Buf("lam")
        lamneg, lamnegb = small[:, 60:61], Buf("lamneg")

        P.op("pool", lambda e: e.memset(vb(52352, 2048), 0.0), writes=[mtb])
        P.dma("sp", MSB, consts["c_msb"], writes=[mskb], sembuf=mskb)
        P.dma("sp", MC, consts["c_mc"], writes=[mskb], sembuf=mskb)
        P.dma("sp", TD, consts["c_td"], writes=[mskb], sembuf=mskb)
        P.dma("sp", LAMT, bcast_rows(WD[("dlam", layer)], 0, 128), writes=[lamb], sembuf=lamb)
        P.dma("sp", GSUB, bcast_rows(WD[("dgain", layer)], 0, 64), writes=[gsubb], sembuf=gsubb)
        s1, s1b = sc_col()
        s2, s2b = sc_col()
        P.op("dve", lambda e: e.scalar_tensor_tensor(out=TMP[:, 0:32], in0=LAMT[:, 0:32], scalar=1.0, in1=LAMT[:, 32:64],
                                                     op0=ALU.mult, op1=ALU.mult, accum_out=s1),
             reads=[lamb], writes=[tmpb, s1b])
        P.op("dve", lambda e: e.scalar_tensor_tensor(out=TMP[:, 0:32], in0=LAMT[:, 64:96], scalar=1.0, in1=LAMT[:, 96:128],
                                                     op0=ALU.mult, op1=ALU.mult, accum_out=s2),
             reads=[lamb, tmpb], writes=[tmpb, s2b])
        P.op("act", lambda e: e.activation(out=s1, in_=s1, func=AF.Exp), reads=[s1b], writes=[s1b])
        P.op("act", lambda e: e.activation(out=s2, in_=s2, func=AF.Exp), reads=[s2b], writes=[s2b])
        P.op("dve", lambda e: e.tensor_tensor(out=lamneg, in0=s2, in1=s1, op=ALU.subtract), reads=[s1b, s2b], writes=[lamnegb])
        P.op("dve", lambda e: e.tensor_scalar(out=lamneg, in0=lamneg, scalar1=-lam_init, scalar2=None, op0=ALU.add),
             reads=[lamnegb], writes=[lamnegb])
        P.op("dve", lambda e: e.tensor_scalar(out=GSUB, in0=GSUB, scalar1=float(1.0 - lam_init), scalar2=None, op0=ALU.mult),
             reads=[gsubb], writes=[gsubb])

        def load_head(hh, typ):
            R = KROWS[typ]
            for c_ in range(NCORES):
                r0 = c_ * 2048 + hh * 128
                d0 = (7 - c_) * 2048
                P.dma("sp", KT[0:R, d0:d0 + 2048], kt_all[r0:r0 + R, :], writes=[ktb[(7 - c_)]], sembuf=ktb[(7 - c_)])
            vsrc = v_all.rearrange("(c r) n -> r c n", c=NCORES)[hh * 128:(hh + 1) * 128, :, :]
            P.dma("sp", VV.rearrange("p (c s) e -> p c (s e)", c=NCORES), vsrc, writes=[vvb], sembuf=vvb)
            P.dma("sp", QT[0:R, :], qt_in[hh, 0:R, :], writes=[qtb], sembuf=qtb)

        def kcol(b):
            return (7 - b % 8) * 16 + b // 8

        def vslot(b):
            return (b % 8) * 16 + b // 8

        def kblk_buf(cb):
            return ktb[cb // 16]

        def softmax_head(hh, typ, h):
            slope = SLOPE[(typ, h)]
            half = hh % 2
            P.op("dve", lambda e: e.tensor_scalar(out=BT, in0=base[:, :], scalar1=float(-slope), scalar2=None, op0=ALU.mult),
                 reads=[cbuf], writes=[btb])
            maps = [(0, 68)] if typ == "dil" else [(0, 36), (64, 36)]
            nmap = len(maps)
            sbanks = [0, 1, 2] if typ == "dil" else [0, 1]
            for g in range(4):
                units = []
                for kb in range(0, 32 * g + 32):
                    e_ = kb - 32 * g
                    ilo = 0 if e_ < 0 else e_ // 8
                    ihi = -1
                    for i in range(ilo, 4):
                        dmin = 128 * (32 * g + 8 * i - kb) - 127
                        ok = slope * dmin <= THRESH
                        if typ == "dil":
                            dsv = 32 * g + 8 * i - kb
                            ok = ok and (-7 <= dsv <= 16)
                        if ok:
                            ihi = i
                    if typ == "dil":
                        while ilo <= ihi and not (-7 <= 32 * g + 8 * ilo - kb <= 16):
                            ilo += 1
                    if ihi >= ilo:
                        units.append((kb, ilo, ihi))
                n = len(units)
                seq = [(u, mi) for u in range(n) for mi in range(nmap)]
                ns = len(seq)
                for mi in range(nmap):
                    P.op("pe", lambda e, mi=mi: e.matmul(PF[4 + mi][0:65, :], lhsT=VV[:, 0, :], rhs=ZB[:, :], start=True, stop=False),
                         reads=[vvb, zerob], writes=[PFb[4 + mi]], sig=False)

                def qk(t, g=g, units=units, seq=seq):
                    u, mi = seq[t]
                    kb, ilo, ihi = units[u]
                    r0, nr = maps[mi]
                    bank = sbanks[t % len(sbanks)]
                    slot = kcol(kb)
                    c0, c1 = ilo * 128, (ihi + 1) * 128
                    e_ = kb - 32 * g
                    need_mask = (typ == "dil") or (e_ >= 0)
                    P.op("pe", lambda e: e.matmul(PF[bank][:, c0:c1], lhsT=KT[r0:r0 + nr, slot * 128:(slot + 1) * 128],
                                                  rhs=QT[r0:r0 + nr, g * 512 + c0:g * 512 + c1], start=True, stop=not need_mask),
                         reads=[kblk_buf(slot), qtb], writes=[PFb[bank]], sig=not need_mask)
                    if need_mask:
                        if typ == "dil":
                            i0 = 32 * g + 8 * ilo - kb + 7
                            cnt = ihi - ilo + 1
                            t0 = TD[:, i0 * 128:(i0 + 1) * 128]
                            rhs = bass.AP(tensor=t0.tensor, offset=t0.offset, ap=[list(t0.ap[0]), [8 * 128, cnt], [1, 128]])
                            P.op("pe", lambda e: e.matmul(PF[bank][:, c0:c1], lhsT=identb[:], rhs=rhs, start=False, stop=True),
                                 reads=[mskb, cbuf], writes=[PFb[bank]])
                        else:
                            r = e_ % 8
                            P.op("pe", lambda e: e.matmul(PF[bank][:, c0:c0 + 128], lhsT=identb[:], rhs=MC[:, r * 128:(r + 1) * 128],
                                                          start=False, stop=True),
                                 reads=[mskb, cbuf], writes=[PFb[bank]])

                def ex(t, g=g, units=units, seq=seq):
                    u, mi = seq[t]
                    kb, ilo, ihi = units[u]
                    bank = sbanks[t % len(sbanks)]
                    c0, c1 = ilo * 128, (ihi + 1) * 128
                    ds_ = 32 * g - kb + 31
                    P.op("act", lambda e: e.activation(out=PT[t % 3][:, c0:c1], in_=PF[bank][:, c0:c1], func=AF.Exp,
                                                       bias=BT[:, ds_:ds_ + 1], scale=1.0),
                         reads=[PFb[bank], btb], writes=[ptb[t % 3]])

                def pv(t, g=g, units=units, seq=seq, ns=ns):
                    u, mi = seq[t]
                    kb, ilo, ihi = units[u]
                    slot = vslot(kb)
                    c0, c1 = ilo * 128, (ihi + 1) * 128
                    ob = 4 + mi
                    last = (u == len(units) - 1)
                    P.op("pe", lambda e: e.matmul(PF[ob][0:65, c0:c1], lhsT=VV[:, slot, :], rhs=PT[t % 3][:, c0:c1],
                                                  start=False, stop=last),
                         reads=[vvb, ptb[t % 3]], writes=[PFb[ob]], sig=True)

                pipeline(ns, [(0, qk), (0, ex), (1, pv)])
                tb = [3, 2]
                for mi in range(nmap):
                    ob = 4 + mi
                    P.op("dve", lambda e, ob=ob: e.tensor_copy(out=OAf[0:65, :], in_=PF[ob][0:65, :]),
                         reads=[PFb[ob]], writes=[oab])
                    for i in range(4):
                        P.op("pe", lambda e, i=i, mi=mi: e.transpose(out=PF[tb[mi]][:, i * 65:(i + 1) * 65],
                                                                   in_=OAf[0:65, i * 128:(i + 1) * 128],
                                                                   identity=identf[0:65, 0:65]),
                             reads=[oab, cbuf], writes=[PFb[tb[mi]]], sig=(i == 3))
                for i in range(4):
                    m = 4 * g + i
                    T1 = PF[3][:, i * 65:(i + 1) * 65]
                    r1, r1b = sc_col()
                    P.op("dve", lambda e, T1=T1, r1=r1: e.reciprocal(out=r1, in_=T1[:, 64:65]), reads=[PFb[3]], writes=[r1b])
                    if typ == "dil":
                        P.op("dve", lambda e, T1=T1, r1=r1, m=m: e.tensor_scalar(
                            out=MT[:, m, half * 64:(half + 1) * 64], in0=T1[:, 0:64], scalar1=r1, scalar2=None, op0=ALU.mult),
                            reads=[PFb[3], r1b], writes=[mtb])
                    else:
                        T2 = PF[2][:, i * 65:(i + 1) * 65]
                        r2, r2b = sc_col()
                        ssq, ssqb = sc_col()
                        std, stdb = sc_col()
                        rstd, rstdb = sc_col()
                        P.op("dve", lambda e, T2=T2, r2=r2: e.reciprocal(out=r2, in_=T2[:, 64:65]), reads=[PFb[2]], writes=[r2b])
                        P.op("dve", lambda e, r2=r2: e.tensor_tensor(out=r2, in0=r2, in1=lamneg, op=ALU.mult),
                             reads=[r2b, lamnegb], writes=[r2b])
                        P.op("dve", lambda e, T1=T1, r1=r1: e.tensor_scalar(out=TMP, in0=T1[:, 0:64], scalar1=r1, scalar2=None, op0=ALU.mult),
                             reads=[PFb[3], r1b], writes=[tmpb])
                        P.op("dve", lambda e, T2=T2, r2=r2: e.scalar_tensor_tensor(out=OTMP, in0=T2[:, 0:64], scalar=r2, in1=TMP,
                                                                                  op0=ALU.mult, op1=ALU.add),
                             reads=[PFb[2], r2b, tmpb], writes=[otmpb])
                        P.op("dve", lambda e, ssq=ssq: e.scalar_tensor_tensor(out=TMP, in0=OTMP, scalar=1.0, in1=OTMP,
                                                                              op0=ALU.mult, op1=ALU.mult, accum_out=ssq),
                             reads=[otmpb, tmpb], writes=[tmpb, ssqb])
                        P.op("act", lambda e, ssq=ssq, std=std: e.activation(out=std, in_=ssq, func=AF.Sqrt, bias=epsT[:, 0:1], scale=1.0 / 64),
                             reads=[ssqb, cbuf], writes=[stdb])
                        P.op("dve", lambda e, std=std, rstd=rstd: e.reciprocal(out=rstd, in_=std), reads=[stdb], writes=[rstdb])
                        P.op("dve", lambda e, rstd=rstd, m=m: e.scalar_tensor_tensor(
                            out=MT[:, m, half * 64:(half + 1) * 64], in0=OTMP, scalar=rstd, in1=GSUB, op0=ALU.mult, op1=ALU.mult),
                            reads=[otmpb, rstdb, gsubb], writes=[mtb])

        def sb_head(hh, h):
            half = hh % 2
            sa = [0, 3, 4, 7, 8, 11, 12, 15]
            sbm = [1, 2, 5, 6, 9, 10, 13, 14]
            ua = [(m, u, 0) for m in sa for u in range(2 * (m + 1))]
            ub = [(m, u, 1) for m in sbm for u in range(2 * (m + 1))]
            assert len(ua) == len(ub)
            units = [x for pair_ in zip(ua, ub) for x in pair_]
            n = len(units)
            NR = 4
            ptcb = [Buf(f"ptc{i}") for i in range(NR)]

            def qk(t):
                m, u, st = units[t]
                bank = t % NR
                b0 = 8 * m + 7 - 4 * u
                cb0 = kcol(b0)
                k0 = KT[0:64, cb0 * 128:(cb0 + 1) * 128]
                krhs = bass.AP(tensor=k0.tensor, offset=k0.offset, ap=[list(k0.ap[0]), [2048, 4], [1, 128]])
                zone = u < 2
                P.op("pe", lambda e: e.matmul(PF[bank][:], lhsT=QT[0:64, m * 128:(m + 1) * 128], rhs=krhs,
                                              start=True, stop=not zone),
                     reads=[qtb] + [kblk_buf(kcol(b0 - j)) for j in range(4)], writes=[PFb[bank]], sig=not zone)
                if zone:
                    P.op("pe", lambda e: e.matmul(PF[bank][:], lhsT=identb[:], rhs=MSB[:, u * 512:(u + 1) * 512], start=False, stop=True),
                         reads=[mskb, cbuf], writes=[PFb[bank]])

            def sg(t):
                bank = t % NR
                P.op("act", lambda e: e.activation(out=SS[t % NR], in_=PF[bank][:], func=AF.Sigmoid, scale=-1.0),
                     reads=[PFb[bank]], writes=[ssb[t % NR]])

            def scan(t):
                m, u, st = units[t]
                cur, prv = CP[t % NR], CP[(t - 2) % NR]
                if u == 0:
                    init = 1.0
                    rd = [ssb[t % NR]]
                else:
                    init = prv[:, 512:513]
                    rd = [ssb[t % NR], cpb[(t - 2) % NR]]
                P.op("dve", lambda e: e.tensor_tensor_scan(out=cur[:, 1:513], data0=SS[t % NR], data1=ZB[:, 0:512], initial=init,
                                                           op0=ALU.mult, op1=ALU.add),
                     reads=rd + [zerob], writes=[cpb[t % NR]])
                P.op("dve", lambda e: e.tensor_tensor(out=PT[t % NR][:, 1:512], in0=cur[:, 1:512], in1=cur[:, 2:513], op=ALU.subtract),
                     reads=[cpb[t % NR]], writes=[ptb[t % NR]])
                if u == 0:
                    P.op("pool", lambda e: e.tensor_scalar(out=PT[t % NR][:, 0:1], in0=cur[:, 1:2], scalar1=-1.0, scalar2=1.0,
                                                           op0=ALU.mult, op1=ALU.add),
                         reads=[cpb[t % NR]], writes=[ptcb[t % NR]])
                else:
                    P.op("pool", lambda e: e.tensor_tensor(out=PT[t % NR][:, 0:1], in0=prv[:, 512:513], in1=cur[:, 1:2], op=ALU.subtract),
                         reads=[cpb[t % NR], cpb[(t - 2) % NR]], writes=[ptcb[t % NR]])

            def tr(t):
                pbi = t % 2
                for j in range(4):
                    P.op("pe", lambda e, j=j: e.transpose(out=PB[pbi][:, j * 128:(j + 1) * 128], in_=PT[t % NR][:, j * 128:(j + 1) * 128],
                                                          identity=identb[:]),
                         reads=[ptb[t % NR], ptcb[t % NR], cbuf], writes=[PBb[pbi]], sig=(j == 3))
                P.op("act", lambda e: e.activation(out=WT[t % NR], in_=PB[pbi][:, 0:512], func=AF.Copy),
                     reads=[PBb[pbi]], writes=[wtb[t % NR]])

            def pv(t):
                m, u, st = units[t]
                ob = 4 + st
                last_u = 2 * (m + 1) - 1
                b0 = 8 * m + 7 - 4 * u
                for j in range(4):
                    P.op("pe", lambda e, j=j: e.matmul(PF[ob][:, 0:64], lhsT=WT[t % NR][:, j * 128:(j + 1) * 128],
                                                       rhs=VV[:, vslot(b0 - j), 0:64], start=(u == 0 and j == 0),
                                                       stop=(u == last_u and j == 3)),
                         reads=[wtb[t % NR], vvb], writes=[PFb[ob]], sig=(j == 3))
                if u == last_u:
                    P.op("dve", lambda e: e.tensor_copy(out=MT[:, m, half * 64:(half + 1) * 64], in_=PF[ob][:, 0:64]),
                         reads=[PFb[ob]], writes=[mtb])

            pipeline(n, [(0, qk), (0, sg), (0, scan), (2, tr), (3, pv)])

        def flush_pair(pair):
            for q4 in range(4):
                pbi = q4 % 2
                for j in range(4):
                    m = q4 * 4 + j
                    P.op("pe", lambda e, m=m, j=j, pbi=pbi: e.transpose(out=PB[pbi][:, j * 128:(j + 1) * 128], in_=MT[:, m, :], identity=identb[:]),
                         reads=[mtb, cbuf], writes=[PBb[pbi]], sig=(j == 3))
                P.op("dve", lambda e, q4=q4, pbi=pbi: e.tensor_copy(out=MIX3[:, pair, q4 * 512:(q4 + 1) * 512], in_=PB[pbi][:, 0:512]),
                     reads=[PBb[pbi]], writes=[mixb[pair]])

        for hh, (typ, h) in enumerate(HEADS):
            if head_sel is not None and hh not in head_sel:
                continue
            if hh == 10:
                P.barrier()
            load_head(hh, typ)
            import os
            dbg = os.environ.get("DBG_MODE", "")
            if dbg.startswith("loadonly"):
                P.op("dve", lambda e: e.tensor_copy(out=MT[:, 0, :], in_=KT[:, 0:128]), reads=ktb + [vvb, qtb], writes=[mtb])
            elif typ == "sb":
                sb_head(hh, h)
            else:
                softmax_head(hh, typ, h)
            if dbg.endswith("noflush"):
                continue
            if hh % 2 == 1 or head_sel is not None:
                flush_pair(hh // 2)
        return MIX3, mixb

    ZB = sb("ZB", [128, 512], BF16)
    zerob = Buf("zero")
    P.op("pool", lambda e: e.memset(ZB[:], 0.0), writes=[zerob])

    def out_proj(layer, MIX3, mixb):
        wo = WD[("wout", layer)]
        wov = wo.rearrange("(k p) n -> p k n", p=128)
        WO3 = vb(0, 8192).rearrange("p (k n) -> p k n", k=8)
        wob = [Buf("wo0"), Buf("wo1")]
        for i in range(2):
            P.dma("pool", WO3[:, i * 4:(i + 1) * 4, :], wov[:, i * 4:(i + 1) * 4, :], writes=[wob[i]], sembuf=wob[i])
        load_gain(1, layer, 3)
        for m in range(NB):
            for hf in range(2):
                bank = 4 + hf
                for k in range(8):
                    P.op("pe", lambda e, k=k, m=m, hf=hf, bank=bank: e.matmul(
                        PF[bank][:], lhsT=MIX3[:, k, m * 128:(m + 1) * 128], rhs=WO3[:, k, hf * 512:(hf + 1) * 512],
                        start=(k == 0), stop=(k == 7)),
                        reads=[mixb[k], wob[k // 4]], writes=[PFb[bank]], sig=(k == 7))
            postnorm_residual(m, (4, 5), 1, 1.0)

    for (kind, layer) in stages:
        if kind == "A":
            ffn(layer, 0, 0, 1)
            P.barrier()
            proj(layer)
            P.barrier()
            if fused:
                for (gi, go) in ((kt_c2, kt_g2), (v_c2, v_g2)):
                    P.op("pool", lambda e, gi=gi, go=go: e.collective_compute(
                        "AllGather", ALU.bypass, replica_groups=[list(range(NCORES))],
                        ins=[gi.ap().opt()], outs=[go.ap().opt()]), writes=[Buf("ag")])
                P.barrier()
        else:
            MIX3, mixb = attention_body(layer)
            P.barrier()
            if debug_mix:
                dbg = dout("dbg_mix", [128, 8 * TOK], BF16)
                P.dma("sp", dbg.rearrange("p (k t) -> p k t", k=8), MIX3, reads=mixb, sembuf=Buf("dbg"), final=True)
            if attn_only:
                continue
            out_proj(layer, MIX3, mixb)
            P.barrier()
            ffn(layer, 1, 4, 5)
            P.barrier()
    xo = x_out.rearrange("(m p) d -> p m d", p=128)
    xst = Buf("xstore")
    for m in range(NB):
        P.dma("sp", xo[:, m, :], X[:, m * D:(m + 1) * D], reads=[xb[m]], sembuf=xst, final=True)
    P.run()
    es.close()
    return nc


_SHARED = None


def _tok_index(c):
    m = np.arange(NB)[:, None]
    j = np.arange(128)[None, :]
    return ((8 * m + c) * 128 + 127 - j).reshape(-1)


def _launch(stages, per_core_extra, weights, debug_mix=False, head_sel=None, attn_only=False, fused=False):
    global _SHARED
    if _SHARED is None:
        _SHARED = shared_constants()
    nc = build_program(stages, debug_mix=debug_mix, head_sel=head_sel, attn_only=attn_only, fused=fused)
    in_maps = []
    for c in range(NCORES):
        mp = dict(_SHARED)
        mp.update(core_constants(c))
        mp.update(weights)
        mp.update(per_core_extra[c])
        in_maps.append(mp)
    res = run_bass_kernel_spmd(nc, in_maps, core_ids=list(range(NCORES)))
    return res.results


def _weights_for(stages, inp):
    w = {}
    for (kind, l) in stages:
        j = 0 if kind == "A" else 1
        w[f"gate{l}_{j}"] = np.ascontiguousarray(inp["w_ffn_gate"][l, j])
        w[f"up{l}_{j}"] = np.ascontiguousarray(inp["w_ffn_up"][l, j])
        w[f"down{l}_{j}"] = np.ascontiguousarray(inp["w_ffn_down"][l, j])
        w[f"gains{l}"] = np.ascontiguousarray(inp["norm_gains"][l])
        if kind == "A":
            w[f"win{l}"] = np.ascontiguousarray(inp["w_in"][l])
        else:
            w[f"wout{l}"] = np.ascontiguousarray(inp["w_out"][l])
            w[f"dlam{l}"] = np.ascontiguousarray(inp["diff_lambda"][l].reshape(1, 128))
            w[f"dgain{l}"] = np.ascontiguousarray(inp["diff_subln_gain"][l].reshape(1, 64))
    return w


def _gather_kv(results):
    kt_all = np.concatenate([results[c]["kt_out"].reshape(16 * 128, TOK) for c in range(NCORES)], 0)
    v_all = np.concatenate([results[c]["v_out"].reshape(16 * 128, NB * 65) for c in range(NCORES)], 0)
    return kt_all, v_all


FUSED = False


def kernel(x, norm_gains, w_ffn_gate, w_ffn_up, w_ffn_down, w_in, w_out, diff_lambda, diff_subln_gain):
    inp = dict(x=np.asarray(x), norm_gains=np.asarray(norm_gains), w_ffn_gate=np.asarray(w_ffn_gate),
               w_ffn_up=np.asarray(w_ffn_up), w_ffn_down=np.asarray(w_ffn_down), w_in=np.asarray(w_in),
               w_out=np.asarray(w_out), diff_lambda=np.asarray(diff_lambda), diff_subln_gain=np.asarray(diff_subln_gain))
    x2 = inp["x"].reshape(S, D)
    idx = [_tok_index(c) for c in range(NCORES)]
    xs = [np.ascontiguousarray(x2[idx[c]]) for c in range(NCORES)]
    if FUSED:
        st = [(k, l) for l in range(DEPTH) for k in ("A", "B")]
        r = _launch(st, [{"x_in": xs[c]} for c in range(NCORES)], _weights_for(st, inp), fused=True)
    else:
        st = [("A", 0)]
        r = _launch(st, [{"x_in": xs[c]} for c in range(NCORES)], _weights_for(st, inp))
        for l in range(DEPTH):
            kt_all, v_all = _gather_kv(r)
            st = [("B", l)] + ([("A", l + 1)] if l + 1 < DEPTH else [])
            extra = [{"x_in": r[c]["x_out"], "kt_all": kt_all, "v_all": v_all, "qt_in": r[c]["qt_out"]} for c in range(NCORES)]
            r = _launch(st, extra, _weights_for(st, inp))
    out = np.zeros((S, D), np.float32)
    for c in range(NCORES):
        out[idx[c]] = r[c]["x_out"]
    return out.reshape(1, S, D)
```
